# Optimizing a Trainium2 kernel written in Bass

```python
import math
import jax, jax.numpy as jnp
from jax import lax
import numpy as np

D_MODEL = 1024
BATCH = 4
SEQ = 8192
DEPTH = 2

N_MIXERS = 2
N_HEADS = 16
HEAD_DIM = D_MODEL // N_HEADS
MOBA_BLOCK = 256
MOBA_TOPK = 3
Q_CHUNK = 64
REL_BUCKETS = 32
REL_MAX_DIST = 128
CONV_WIDTH = 31
PEER_HEADS = 8
PEER_NKEYS = 128
PEER_NEXPERTS = PEER_NKEYS * PEER_NKEYS
PEER_KEY_DIM = 256
PEER_HALF = PEER_KEY_DIM // 2
PEER_TOPK = 16
PEER_CHUNK = 128
EPS = 1e-6
NEG = -1e30

kernel_name = "moba_conformer_peer_hybrid"


def rmsnorm(x, g):
    xf = x.astype(jnp.float32)
    y = xf * lax.rsqrt(jnp.mean(xf * xf, axis=-1, keepdims=True) + EPS)
    return (y * g.astype(jnp.float32)).astype(x.dtype)


def layernorm(x, g, b):
    xf = x.astype(jnp.float32)
    mu = jnp.mean(xf, axis=-1, keepdims=True)
    var = jnp.mean(jnp.square(xf - mu), axis=-1, keepdims=True)
    y = (xf - mu) * lax.rsqrt(var + EPS)
    return (y * g.astype(jnp.float32) + b.astype(jnp.float32)).astype(x.dtype)


def t5_bucket(dist):
    max_exact = REL_BUCKETS // 2
    d = jnp.maximum(dist, 0)
    df = jnp.maximum(d, 1).astype(jnp.float32)
    large = max_exact + (jnp.log(df / max_exact) / math.log(REL_MAX_DIST / max_exact)
                         * (REL_BUCKETS - max_exact)).astype(jnp.int32)
    large = jnp.minimum(large, REL_BUCKETS - 1)
    return jnp.where(d < max_exact, d, large)


def moba_attention(h, w_qkv, w_o, rel_bias):
    B, S, D = h.shape
    qkv = h @ w_qkv
    q, k, v = jnp.split(qkv, 3, axis=-1)
    to_heads = lambda t: t.reshape(B, S, N_HEADS, HEAD_DIM).transpose(0, 2, 1, 3)
    q, k, v = to_heads(q), to_heads(k), to_heads(v)
    nb = -(-S // MOBA_BLOCK)
    pad = nb * MOBA_BLOCK - S
    k = jnp.pad(k, ((0, 0), (0, 0), (0, pad), (0, 0)))
    v = jnp.pad(v, ((0, 0), (0, 0), (0, pad), (0, 0)))
    k_blocks = k.reshape(B, N_HEADS, nb, MOBA_BLOCK, HEAD_DIM)
    v_blocks = v.reshape(B, N_HEADS, nb, MOBA_BLOCK, HEAD_DIM)
    k_mean = jnp.mean(k_blocks.astype(jnp.float32), axis=3).astype(k.dtype)
    n_sel = min(MOBA_TOPK, nb)
    scale = HEAD_DIM ** -0.5
    b_ix = jnp.arange(B)[:, None, None, None]
    h_ix = jnp.arange(N_HEADS)[:, None, None, None]
    blk_pos = jnp.arange(MOBA_BLOCK, dtype=jnp.int32)
    n_chunks = S // Q_CHUNK

    def chunk(c):
        q0 = c * Q_CHUNK
        own = q0 // MOBA_BLOCK
        qc = lax.dynamic_slice_in_dim(q, q0, Q_CHUNK, axis=2)
        q_pos = q0 + jnp.arange(Q_CHUNK, dtype=jnp.int32)
        gate = jnp.einsum('bhqd,bhnd->bhqn', qc, k_mean).astype(jnp.float32)
        past = jnp.arange(nb) < own
        gate = jnp.where(past, gate, NEG)
        _, sel = lax.top_k(gate, n_sel)
        sel_valid = sel < own
        k_sel = k_blocks[b_ix, jnp.arange(N_HEADS)[None, :, None, None], sel]
        v_sel = v_blocks[b_ix, jnp.arange(N_HEADS)[None, :, None, None], sel]
        s_sel = jnp.einsum('bhqd,bhqnkd->bhqnk', qc, k_sel).astype(jnp.float32) * scale
        k_pos_sel = sel[..., None] * MOBA_BLOCK + blk_pos
        bucket_sel = t5_bucket(q_pos[None, None, :, None, None] - k_pos_sel)
        bias_sel = rel_bias[h_ix[None], bucket_sel].astype(jnp.float32)
        s_sel = jnp.where(sel_valid[..., None], s_sel + bias_sel, NEG)
        k_own = lax.dynamic_slice_in_dim(k, own * MOBA_BLOCK, MOBA_BLOCK, axis=2)
        v_own = lax.dynamic_slice_in_dim(v, own * MOBA_BLOCK, MOBA_BLOCK, axis=2)
        s_own = jnp.einsum('bhqd,bhkd->bhqk', qc, k_own).astype(jnp.float32) * scale
        dist_own = q_pos[:, None] - (own * MOBA_BLOCK + blk_pos)[None, :]
        bias_own = rel_bias[:, t5_bucket(dist_own)].astype(jnp.float32)
        s_own = jnp.where(dist_own >= 0, s_own + bias_own[None], NEG)
        logits = jnp.concatenate([s_own, s_sel.reshape(B, N_HEADS, Q_CHUNK, n_sel * MOBA_BLOCK)], axis=-1)
        p = jax.nn.softmax(logits, axis=-1)
        p_own = p[..., :MOBA_BLOCK].astype(v.dtype)
        p_sel = p[..., MOBA_BLOCK:].reshape(B, N_HEADS, Q_CHUNK, n_sel, MOBA_BLOCK).astype(v.dtype)
        return (jnp.einsum('bhqk,bhkd->bhqd', p_own, v_own)
                + jnp.einsum('bhqnk,bhqnkd->bhqd', p_sel, v_sel))

    outs = lax.map(chunk, jnp.arange(n_chunks, dtype=jnp.int32))
    o = outs.transpose(1, 0, 3, 2, 4).reshape(B, S, D)
    return o @ w_o


def conformer_conv(h, w_pw1, b_pw1, w_dw, b_dw, ln_g, ln_b, w_pw2, b_pw2):
    a = h @ w_pw1 + b_pw1
    val, gate = jnp.split(a, 2, axis=-1)
    u = val * jax.nn.sigmoid(gate)
    u = lax.conv_general_dilated(u, w_dw[:, None, :], window_strides=(1,),
                                 padding=[(CONV_WIDTH - 1, 0)],
                                 dimension_numbers=('NWC', 'WIO', 'NWC'),
                                 feature_group_count=D_MODEL) + b_dw
    u = jax.nn.silu(layernorm(u, ln_g, ln_b))
    return u @ w_pw2 + b_pw2


def peer(h, w_pq, sub_keys, expert_u, expert_v):
    B, S, D = h.shape
    T = B * S
    xt = h.reshape(T, D)
    q = (xt @ w_pq).reshape(T, PEER_HEADS, 2, PEER_HALF)
    s = jnp.einsum('thcd,hcnd->thcn', q, sub_keys).astype(jnp.float32)
    top_s, top_i = lax.top_k(s, PEER_TOPK)
    cand_s = top_s[:, :, 0, :, None] + top_s[:, :, 1, None, :]
    cand_i = top_i[:, :, 0, :, None] * PEER_NKEYS + top_i[:, :, 1, None, :]
    best_s, best_pos = lax.top_k(cand_s.reshape(T, PEER_HEADS, PEER_TOPK * PEER_TOPK), PEER_TOPK)
    experts = jnp.take_along_axis(cand_i.reshape(T, PEER_HEADS, PEER_TOPK * PEER_TOPK), best_pos, axis=-1)
    gates = jax.nn.softmax(best_s, axis=-1)
    nc = T // PEER_CHUNK

    def chunk(args):
        xc, ec, gc = args
        u = expert_u[ec]
        act = jax.nn.gelu(jnp.einsum('cd,chkd->chk', xc, u))
        w = gc.astype(xc.dtype) * act
        return jnp.einsum('chk,chkd->cd', w, expert_v[ec])

    out = lax.map(chunk, (xt.reshape(nc, PEER_CHUNK, D),
                          experts.reshape(nc, PEER_CHUNK, PEER_HEADS, PEER_TOPK),
                          gates.reshape(nc, PEER_CHUNK, PEER_HEADS, PEER_TOPK)))
    return out.reshape(B, S, D)


def setup_inputs(seed: int = 0) -> dict:
    key = jax.random.key(seed)
    ks = jax.random.split(key, 24)
    n_attn = (DEPTH + 1) // 2
    n_conv = DEPTH // 2
    D = D_MODEL
    nrm = lambda k, shape, s: jax.random.normal(k, shape, jnp.float32) * s
    return {
        "x": nrm(ks[0], (BATCH, SEQ, D), 1.0),
        "rel_bias": nrm(ks[1], (N_HEADS, REL_BUCKETS), 0.2),
        "norm_mix": 1.0 + nrm(ks[2], (DEPTH, D), 0.05),
        "norm_ffn": 1.0 + nrm(ks[3], (DEPTH, D), 0.05),
        "attn_w_qkv": nrm(ks[4], (n_attn, D, 3 * D), D ** -0.5),
        "attn_w_o": nrm(ks[5], (n_attn, D, D), D ** -0.5),
        "conv_w_pw1": nrm(ks[6], (n_conv, D, 2 * D), D ** -0.5),
        "conv_b_pw1": nrm(ks[7], (n_conv, 2 * D), 0.01),
        "conv_w_dw": nrm(ks[8], (n_conv, CONV_WIDTH, D), CONV_WIDTH ** -0.5),
        "conv_b_dw": nrm(ks[9], (n_conv, D), 0.01),
        "conv_ln_g": 1.0 + nrm(ks[10], (n_conv, D), 0.05),
        "conv_ln_b": nrm(ks[11], (n_conv, D), 0.01),
        "conv_w_pw2": nrm(ks[12], (n_conv, D, D), D ** -0.5),
        "conv_b_pw2": nrm(ks[13], (n_conv, D), 0.01),
        "peer_w_q": nrm(ks[14], (DEPTH, D, PEER_HEADS * PEER_KEY_DIM), D ** -0.5),
        "peer_sub_keys": nrm(ks[15], (DEPTH, PEER_HEADS, 2, PEER_NKEYS, PEER_HALF), PEER_HALF ** -0.5),
        "peer_u": nrm(ks[16], (DEPTH, PEER_NEXPERTS, D), D ** -0.5),
        "peer_v": nrm(ks[17], (DEPTH, PEER_NEXPERTS, D), (PEER_HEADS * PEER_TOPK) ** -0.5),
        "norm_final": 1.0 + nrm(ks[18], (D,), 0.05),
    }


def reference(x, rel_bias, norm_mix, norm_ffn, attn_w_qkv, attn_w_o,
              conv_w_pw1, conv_b_pw1, conv_w_dw, conv_b_dw, conv_ln_g, conv_ln_b,
              conv_w_pw2, conv_b_pw2, peer_w_q, peer_sub_keys, peer_u, peer_v, norm_final):
    for i in range(DEPTH):
        h = rmsnorm(x, norm_mix[i])
        j = i // N_MIXERS
        if i % N_MIXERS == 0:
            mix = moba_attention(h, attn_w_qkv[j], attn_w_o[j], rel_bias)
        else:
            mix = conformer_conv(h, conv_w_pw1[j], conv_b_pw1[j], conv_w_dw[j], conv_b_dw[j],
                                 conv_ln_g[j], conv_ln_b[j], conv_w_pw2[j], conv_b_pw2[j])
        x = x + mix
        h = rmsnorm(x, norm_ffn[i])
        x = x + peer(h, peer_w_q[i], peer_sub_keys[i], peer_u[i], peer_v[i])
    return rmsnorm(x, norm_final)
```

```python
import contextlib
import math
import numpy as np
import concourse.bass as bass
import concourse.mybir as mybir
from concourse.bass_utils import run_bass_kernel_spmd

F32 = mybir.dt.float32
BF16 = mybir.dt.bfloat16
ALU = mybir.AluOpType
AF = mybir.ActivationFunctionType
AX = mybir.AxisListType

D = 1024
NH = 16
HD = 64
SEQ = 8192
BLK = 256
NB = 32
NQT = 34
NG = 17
NT0 = 33
CW = 31
NEGB = 240000.0
EPS = 1e-6

COMPUTE = ("pe", "act", "dve", "pool")
STREAM_OF = {"pe": "tensor", "act": "scalar", "dve": "vector", "pool": "gpsimd",
             "sp": "sync", "actq": "scalar", "poolq": "gpsimd"}


class Buf:
    __slots__ = ("name", "w", "r")

    def __init__(self, name=""):
        self.name = name
        self.w = None
        self.r = {}


class Prog:
    def __init__(self, nc):
        self.nc = nc
        self.streams = {s: [] for s in ("tensor", "scalar", "vector", "gpsimd", "sync")}
        self.cnt = {e: 0 for e in COMPUTE}
        self.sems = {}
        self._ctx = []
        self.dpool = {}
        self.drr = {}
        self.DPOOL = {"sp": 28, "actq": 6, "poolq": 12}
        self.nsem = 0
        for e in COMPUTE:
            self.sems[e] = self._sem("c_" + e)

    def _sem(self, name):
        cm = self.nc.semaphore(name)
        s = cm.__enter__()
        self._ctx.append(cm)
        return s

    def _deps(self, eng, reads, writes):
        waits = {}

        def need(tok):
            if tok is None:
                return
            kind, key, val = tok
            if kind == "c" and key == eng and eng == "pe":
                return
            k = (kind, key)
            if k not in waits or waits[k][2] < val:
                waits[k] = tok

        for b in reads:
            need(b.w)
        for b in writes:
            need(b.w)
            for t in b.r.values():
                need(t)
        return list(waits.values())

    def op(self, eng, fn, reads=(), writes=()):
        waits = self._deps(eng, reads, writes)
        self.cnt[eng] += 1
        tok = ("c", eng, self.cnt[eng])
        self.streams[STREAM_OF[eng]].append((waits, fn, ("c", eng)))
        for b in reads:
            b.r[("c", eng)] = tok
        for b in writes:
            b.w = tok
            b.r = {}
        return tok

    def dma(self, q, fn, reads=(), writes=()):
        waits = self._deps(q, reads, writes)
        pool = self.dpool.setdefault(q, [])
        if len(pool) < self.DPOOL[q]:
            pool.append([self._sem("d_%s%d" % (q, len(pool))), 0])
            idx = len(pool) - 1
        else:
            idx = self.drr.get(q, 0) % len(pool)
        self.drr[q] = idx + 1
        key = (q, idx)
        if pool[idx][1] > 0:
            waits = [t for t in waits if (t[0], t[1]) != ("d", key)] + [("d", key, pool[idx][1])]
        pool[idx][1] += 16
        tok = ("d", key, pool[idx][1])
        self.streams[STREAM_OF[q]].append((waits, fn, ("d", key)))
        for b in reads:
            b.r[("d", key)] = tok
        for b in writes:
            b.w = tok
            b.r = {}
        return tok

    def barrier(self):
        toks = [("c", e, self.cnt[e]) for e in COMPUTE if self.cnt[e] > 0]
        for q, pool in self.dpool.items():
            for idx, (s, v) in enumerate(pool):
                if v > 0:
                    toks.append(("d", (q, idx), v))
        for s in self.streams:
            self.streams[s].append((list(toks), None, None))

    def emit(self, final_bufs=()):
        nc = self.nc
        finals = []
        for b in final_bufs:
            if b.w is not None:
                finals.append(b.w)
            finals.extend(b.r.values())

        def semof(tok):
            kind, key, val = tok
            return (self.sems[key] if kind == "c" else self.dpool[key[0]][key[1]][0]), val

        def replay(handle, items, extra=()):
            seen = {}
            for waits, fn, me in list(items) + [(list(extra), None, None)]:
                for t in waits:
                    k = (t[0], t[1])
                    if seen.get(k, 0) >= t[2]:
                        continue
                    seen[k] = t[2]
                    s, v = semof(t)
                    handle.wait_ge(s, v)
                if fn is None:
                    continue
                ins = fn(handle)
                if me[0] == "c":
                    ins.then_inc(self.sems[me[1]], 1)
                    seen[me] = max(seen.get(me, 0), 0)
                else:
                    ins.then_inc(self.dpool[me[1][0]][me[1][1]][0], 16)

        streams = self.streams
        with nc.Block() as block:
            @block.tensor
            def _(e):
                replay(e, streams["tensor"])

            @block.scalar
            def _(e):
                replay(e, streams["scalar"])

            @block.vector
            def _(e):
                replay(e, streams["vector"])

            @block.gpsimd
            def _(e):
                replay(e, streams["gpsimd"])

            @block.sync
            def _(e):
                replay(e, streams["sync"], finals)

    def close(self):
        for cm in reversed(self._ctx):
            cm.__exit__(None, None, None)


def t5_runs():
    d = np.arange(768, dtype=np.int32)
    df = np.maximum(d, 1).astype(np.float32)
    large = 16 + (np.log(df / np.float32(16)) / np.float32(math.log(128 / 16)) * np.float32(16)).astype(np.int32)
    large = np.minimum(large, 31)
    return np.where(d < 16, d, large)


def build_program(dbg=False):
    nc = bass.Bass("TRN2", target_bir_lowering=False)
    P = Prog(nc)

    def din(name, shape, dt=F32):
        return nc.dram_tensor(name, list(shape), dt, kind="ExternalInput").ap()

    def dscr(name, shape, dt, out=False):
        return nc.dram_tensor(name, list(shape), dt, kind=("ExternalOutput" if out else "Internal")).ap()

    xv = din("xv", [SEQ, D])
    rel_bias = din("rel_bias", [NH, 32])
    norm_mix = din("norm_mix", [2, D])
    norm_ffn = din("norm_ffn", [2, D])
    norm_final = din("norm_final", [1, D])
    w_qkv = din("w_qkv", [D, 3 * D])
    w_o = din("w_o", [D, D])
    w_pw1 = din("w_pw1", [D, 2 * D])
    b_pw1 = din("b_pw1", [128, 16])
    w_dwT = din("w_dwT", [D, CW])
    b_dw = din("b_dw", [128, 8])
    ln_g = din("ln_g", [128, 8])
    ln_b = din("ln_b", [128, 8])
    w_pw2 = din("w_pw2", [D, D])
    b_pw2 = din("b_pw2", [1, D])
    w_pq = din("w_pq", [2, D, 2048])
    skT = din("skT", [2, 16, 128, 128])
    uT = din("uT", [2, D, 16384])
    pv = din("pv", [2, 16384, D])
    ident_in = din("ident", [128, 128])
    boh_in = din("boh", [NB, SEQ])
    pm_in = din("pm", [1, NG * NB])
    a1_in = din("a1", [1, NG * NB])
    a2_in = din("a2", [1, NG * NB])
    a3_in = din("a3", [1, NG * NB])
    halo_in = din("halo", [1, 1])
    y = dscr("y", [4096, D], F32, out=True)

    wqkv_b = dscr("wqkv_b", [D, 3 * D], BF16)
    wo_b = dscr("wo_b", [D, D], BF16)
    wpw1_b = dscr("wpw1_b", [D, 2 * D], BF16)
    wpw2_b = dscr("wpw2_b", [D, D], BF16)
    wpq_b = dscr("wpq_b", [2, D, 2048], BF16)
    skT_b = dscr("skT_b", [2, 16, 128, 128], BF16)
    uT_b = dscr("uT_b", [2, D, 16384], BF16)
    pv_b = dscr("pv_b", [2, 16384, D], BF16)
    boh_b = dscr("boh_b", [NB, SEQ], BF16)
    KT = dscr("KT", [NH, HD, SEQ], BF16)
    QT = dscr("QT", [NH, HD, NQT * 128], BF16)
    VS = dscr("VS", [SEQ, D], BF16)
    OA = dscr("OA", [NQT * 128, D], F32, out=dbg)
    X1 = dscr("X1", [NT0 * 128, D], F32, out=dbg)
    X2 = dscr("X2", [NT0 * 128, D], F32, out=dbg)
    X3 = dscr("X3", [4096, D], F32, out=dbg)
    TT = dscr("TT", [NH, 128, 1024], F32)

    es = contextlib.ExitStack()
    with es:
        def sb(name, shape, dt, st=es):
            return st.enter_context(nc.sbuf_tensor(name, list(shape), dt))

        pb = [es.enter_context(nc.psum_tensor("pb%d" % i, [128, 512], F32)) for i in range(8)]
        PB = [Buf("pb%d" % i) for i in range(8)]

        idf = sb("idf", [128, 128], F32)
        idb = sb("idb", [128, 128], BF16)
        B_id = Buf("id")
        P.dma("sp", lambda e: e.dma_start(out=idf[:], in_=ident_in), writes=[B_id])
        P.op("dve", lambda e: e.tensor_copy(out=idb[:], in_=idf[:]), reads=[B_id], writes=[B_id])

        B_w = {k: Buf(k) for k in ("wqkv", "wo", "wpw1", "wpw2", "wpq", "skT", "boh")}
        B_uT = [[Buf() for _ in range(16)] for _ in range(2)]
        B_pv = [[Buf() for _ in range(16)] for _ in range(2)]

        def cast(dst, src, buf, rows=None):
            n = dst.shape[0]
            step = rows or n
            for r0 in range(0, n, step):
                P.dma("poolq", lambda e, r0=r0: e.dma_start(out=dst[r0:r0 + step], in_=src[r0:r0 + step]), writes=[buf])

        cast(wqkv_b, w_qkv, B_w["wqkv"], 256)
        cast(boh_b, boh_in, B_w["boh"])
        cast(wo_b, w_o, B_w["wo"], 512)
        for l in range(2):
            cast(wpq_b[l], w_pq[l], B_w["wpq"], 512)
            cast(skT_b[l].rearrange("a b c -> (a b) c"), skT[l].rearrange("a b c -> (a b) c"), B_w["skT"])
        cast(wpw1_b, w_pw1, B_w["wpw1"], 512)
        cast(wpw2_b, w_pw2, B_w["wpw2"], 512)
        for l in range(2):
            for eg in range(16):
                P.dma("poolq", lambda e, l=l, eg=eg: e.dma_start(
                    out=uT_b[l][:, eg * 1024:(eg + 1) * 1024], in_=uT[l][:, eg * 1024:(eg + 1) * 1024]),
                    writes=[B_uT[l][eg]])
                P.dma("poolq", lambda e, l=l, eg=eg: e.dma_start(
                    out=pv_b[l][eg * 1024:(eg + 1) * 1024, :], in_=pv[l][eg * 1024:(eg + 1) * 1024, :]),
                    writes=[B_pv[l][eg]])

        def norm_T(st, pfx, x_sb, Bx, g_rep, Bg, hT_dst, B_hT, tpbank, nbufs):
            sq, ss, rs, hb, Bs = nbufs
            P.op("act", lambda e: e.activation(out=sq[:], in_=x_sb, func=AF.Square, accum_out=ss[:]),
                 reads=[Bx], writes=[Bs])
            P.op("dve", lambda e: e.tensor_scalar(out=rs[:], in0=ss[:], scalar1=1.0 / D, scalar2=EPS,
                                                  op0=ALU.mult, op1=ALU.add), reads=[Bs], writes=[Bs])
            P.op("act", lambda e: e.activation(out=rs[:], in_=rs[:], func=AF.Sqrt), reads=[Bs], writes=[Bs])
            P.op("dve", lambda e: e.reciprocal(out=rs[:], in_=rs[:]), reads=[Bs], writes=[Bs])
            P.op("dve", lambda e: e.scalar_tensor_tensor(out=hb[:], in0=x_sb, scalar=rs[:, 0:1], in1=g_rep,
                                                         op0=ALU.mult, op1=ALU.mult),
                 reads=[Bx, Bs, Bg], writes=[Bs])
            tpv = pb[tpbank][:].bitcast(BF16)
            for c in range(8):
                P.op("pe", lambda e, c=c: e.transpose(out=tpv[:, c * 128:(c + 1) * 128],
                                                      in_=hb[:, c * 128:(c + 1) * 128], identity=idb[:]),
                     reads=[Bs, B_id], writes=[PB[tpbank]])
            P.op("act", lambda e: e.copy(out=hT_dst, in_=tpv.rearrange("p (c t) -> p c t", c=8)),
                 reads=[PB[tpbank]], writes=[B_hT])

        def norm_bufs(st, pfx):
            return (sb(pfx + "sq", [128, D], F32, st), sb(pfx + "ss", [128, 1], F32, st),
                    sb(pfx + "rs", [128, 1], F32, st), sb(pfx + "hb", [128, D], BF16, st), Buf(pfx + "nb"))

        def load_rep(st, name, src_row, n=D):
            t = sb(name, [128, n], F32, st)
            b = Buf(name)
            P.dma("sp", lambda e: e.dma_start(out=t[:], in_=src_row.to_broadcast([128, n])), writes=[b])
            return t, b

        with contextlib.ExitStack() as st:
            wq = sb("wqkv_sb", [128, 8, 3 * D], BF16, st)
            B_wq = Buf()
            for c in range(8):
                P.dma("sp", lambda e, c=c: e.dma_start(out=wq[:, c, :], in_=wqkv_b[c * 128:(c + 1) * 128, :]),
                      reads=[B_w["wqkv"]], writes=[B_wq])
            g0, Bg0 = load_rep(st, "g0", norm_mix[0:1, :])
            nb = [norm_bufs(st, "p1n%d" % i) for i in range(2)]
            xts = [sb("p1x%d" % i, [128, D], F32, st) for i in range(2)]
            Bxs = [Buf() for _ in range(2)]
            hTs = [sb("p1hT%d" % i, [128, 8, 128], BF16, st) for i in range(2)]
            BhT = [Buf() for _ in range(2)]
            kts = [sb("p1kt%d" % i, [128, 8, 128], BF16, st) for i in range(2)]
            Bkt = [Buf() for _ in range(2)]
            qts = [sb("p1qt%d" % i, [128, 8, 128], BF16, st) for i in range(2)]
            Bqt = [Buf() for _ in range(2)]
            vts = [sb("p1vt%d" % i, [128, D], BF16, st) for i in range(2)]
            Bvt = [Buf() for _ in range(2)]
            B_KT = [Buf() for _ in range(64)]
            B_QT = [Buf() for _ in range(NQT)]
            B_VS = [Buf() for _ in range(64)]
            KTv = KT.rearrange("(hp two) d t -> (two d) hp t", two=2)
            QTv = QT.rearrange("(hp two) d t -> (two d) hp t", two=2)
            for vt in range(64):
                i = vt % 2
                P.dma("sp", lambda e, vt=vt, i=i: e.dma_start(out=xts[i][:], in_=xv[vt * 128:(vt + 1) * 128, :]),
                      writes=[Bxs[i]])
                norm_T(st, "p1", xts[i][:], Bxs[i], g0[:], Bg0, hTs[i][:], BhT[i], 0, nb[i])
                for (woff, dst, Bdst, bank0, do) in ((D, kts[i], Bkt[i], 1, True), (0, qts[i], Bqt[i], 3, vt >= 30)):
                    if not do:
                        continue
                    for hp in range(8):
                        bank = bank0 + hp // 4
                        for c in range(8):
                            P.op("pe", lambda e, hp=hp, c=c, bank=bank, woff=woff, i=i: e.matmul(
                                out=pb[bank][:, (hp % 4) * 128:(hp % 4 + 1) * 128],
                                lhsT=wq[:, c, woff + hp * 128: woff + (hp + 1) * 128], rhs=hTs[i][:, c, :],
                                start=(c == 0), stop=(c == 7)), reads=[B_wq, BhT[i]], writes=[PB[bank]])
                    for half in range(2):
                        P.op("dve" if half == 0 else "act",
                             (lambda e, half=half, dst=dst, bank0=bank0: e.tensor_copy(
                                 out=dst[:, half * 4:(half + 1) * 4, :],
                                 in_=pb[bank0 + half][:].rearrange("p (a t) -> p a t", a=4))) if half == 0 else
                             (lambda e, half=half, dst=dst, bank0=bank0: e.copy(
                                 out=dst[:, half * 4:(half + 1) * 4, :],
                                 in_=pb[bank0 + half][:].rearrange("p (a t) -> p a t", a=4))),
                             reads=[PB[bank0 + half]], writes=[Bdst])
                P.dma("sp", lambda e, vt=vt, i=i: e.dma_start(out=KTv[:, :, vt * 128:(vt + 1) * 128], in_=kts[i][:]),
                      reads=[Bkt[i]], writes=[B_KT[vt]])
                if vt >= 30:
                    qi = vt - 30
                    P.dma("sp", lambda e, qi=qi, i=i: e.dma_start(out=QTv[:, :, qi * 128:(qi + 1) * 128], in_=qts[i][:]),
                          reads=[Bqt[i]], writes=[B_QT[qi]])
                for half in range(2):
                    bank = 5 + half
                    for c in range(8):
                        P.op("pe", lambda e, half=half, c=c, bank=bank, i=i: e.matmul(
                            out=pb[bank][:], lhsT=hTs[i][:, c, :],
                            rhs=wq[:, c, 2 * D + half * 512: 2 * D + (half + 1) * 512],
                            start=(c == 0), stop=(c == 7)), reads=[B_wq, BhT[i]], writes=[PB[bank]])
                    P.op("dve" if half == 0 else "act",
                         (lambda e, half=half, bank=bank, i=i: e.tensor_copy(out=vts[i][:, half * 512:(half + 1) * 512], in_=pb[bank][:]))
                         if half == 0 else
                         (lambda e, half=half, bank=bank, i=i: e.copy(out=vts[i][:, half * 512:(half + 1) * 512], in_=pb[bank][:])),
                         reads=[PB[bank]], writes=[Bvt[i]])
                P.dma("sp", lambda e, vt=vt, i=i: e.dma_start(out=VS[vt * 128:(vt + 1) * 128, :], in_=vts[i][:]),
                      reads=[Bvt[i]], writes=[B_VS[vt]])
        P.barrier()

        bk = t5_runs()
        with contextlib.ExitStack() as st:
            rb = sb("rb", [NH, 32], F32, st)
            tt = sb("tt", [NH, 1024], F32, st)
            B_tt = Buf()
            P.dma("sp", lambda e: e.dma_start(out=rb[:], in_=rel_bias), writes=[B_tt])
            P.op("dve", lambda e: e.memset(tt[:, 0:256], -NEGB / 8.0), reads=[B_tt], writes=[B_tt])
            P.op("dve", lambda e: e.tensor_copy(out=tt[:, 256:272], in_=rb[:, 0:16]), reads=[B_tt], writes=[B_tt])
            dlt = 16
            while dlt < 768:
                b_ = int(bk[dlt])
                e_ = dlt
                while e_ < 768 and int(bk[e_]) == b_:
                    e_ += 1
                P.op("dve", lambda e, dlt=dlt, e_=e_, b_=b_: e.tensor_copy(
                    out=tt[:, 256 + dlt:256 + e_], in_=rb[:, b_:b_ + 1].to_broadcast([NH, e_ - dlt])),
                    reads=[B_tt], writes=[B_tt])
                dlt = e_
            P.op("dve", lambda e: e.tensor_scalar(out=tt[:], in0=tt[:], scalar1=8.0, scalar2=None, op0=ALU.mult),
                 reads=[B_tt], writes=[B_tt])
            B_TT = Buf()
            P.dma("sp", lambda e: e.dma_start(out=TT, in_=tt[:].unsqueeze(1).to_broadcast([NH, 128, 1024])), reads=[B_tt], writes=[B_TT])

            kaug2 = [sb("kaug%d" % i, [96, SEQ], BF16, st) for i in range(2)]
            B_ka2 = [Buf() for _ in range(2)]
            B_boh = Buf()
            for i_ in range(2):
                P.dma("sp", lambda e, i_=i_: e.dma_start(out=kaug2[i_][64:96, :], in_=boh_b), reads=[B_w["boh"]], writes=[B_boh])
            vaug2 = [sb("vaug%d" % i, [128, 64, 65], BF16, st) for i in range(2)]
            B_va2 = [Buf() for _ in range(2)]
            for i_ in range(2):
                P.op("pool", lambda e, i_=i_: e.memset(vaug2[i_][:], 1.0), writes=[B_va2[i_]])
            qaug2 = [sb("qaug%d" % i, [96, NQT * 128], BF16, st) for i in range(2)]
            B_qa2 = [Buf() for _ in range(2)]
            B_qm2 = [[Buf() for _ in range(NG)] for _ in range(2)]
            btab2 = [sb("btab%d" % i, [128, 2, 2, 256], F32, st) for i in range(2)]
            B_bt2 = [Buf() for _ in range(2)]
            km = sb("km", [64, NB], F32, st)
            kmb = sb("kmb", [64, NB], BF16, st)
            B_km = Buf()
            pmr, B_pm = load_rep(st, "pmr", pm_in, NG * NB)
            a1r, B_a1 = load_rep(st, "a1r", a1_in, NG * NB)
            a2r, B_a2 = load_rep(st, "a2r", a2_in, NG * NB)
            a3r, B_a3 = load_rep(st, "a3r", a3_in, NG * NB)
            rb31 = sb("rb31", [128, NH], F32, st)
            B_rb31 = Buf()
            P.dma("sp", lambda e: e.dma_start(out=rb31[:], in_=rel_bias[:, 31:32].rearrange("h o -> o h").to_broadcast([128, NH]), allow_slow_non_contiguous=True),
                  writes=[B_rb31])
            mulv = sb("mulv", [128, NG * NB], F32, st)
            B_mulv = Buf()
            gm = [sb("gm%d" % i, [128, 2, NB], F32, st) for i in range(2)]
            mx = [sb("mx%d" % i, [128, 2, 8], F32, st) for i in range(2)]
            sel = [sb("sel%d" % i, [128, 2, 96], F32, st) for i in range(2)]
            B_g = [Buf() for _ in range(2)]
            for i_ in range(2):
                P.op("dve", lambda e, i_=i_: e.memset(sel[i_][:], 0.0), writes=[B_g[i_]])
            stmp = [sb("stmp%d" % i, [128, 512], F32, st) for i in range(2)]
            B_stmp = [Buf() for _ in range(2)]
            pT = [sb("pT%d" % i, [128, 512], BF16, st) for i in range(3)]
            B_pT = [Buf() for _ in range(3)]
            oall = sb("oall", [128, NQT, HD], F32, st)
            B_oall = Buf()
            rec = [sb("rec%d" % i, [128, 2, 1], F32, st) for i in range(2)]
            B_rec = [Buf() for _ in range(2)]
            B_OA = Buf()
            OAv = OA.rearrange("(t p) (h d) -> p t h d", p=128, h=NH)
            VSv = VS.rearrange("(c p) (h d) -> p c h d", p=128, h=NH)
            def emit_loads(h):
                p_ = h % 2
                P.dma("sp", lambda e, h=h, p_=p_: e.dma_start(out=kaug2[p_][0:64, :], in_=KT[h]), reads=B_KT, writes=[B_ka2[p_]])
                for cq in range(4):
                    P.dma("sp", lambda e, h=h, cq=cq, p_=p_: e.dma_start(out=vaug2[p_][:, cq * 16:(cq + 1) * 16, 0:64],
                                                                         in_=VSv[:, cq * 16:(cq + 1) * 16, h, :]),
                          reads=B_VS, writes=[B_va2[p_]])
                P.dma("sp", lambda e, h=h, p_=p_: e.dma_start(out=qaug2[p_][0:64, :], in_=QT[h]), reads=B_QT, writes=[B_qa2[p_]])
                for which in range(2):
                    for kc in range(2):
                        def bsrc(h=h, which=which, kc=kc):
                            base = TT[h, 0:1, 256 * (1 + which) - kc * 128: 256 * (1 + which) - kc * 128 + 256]
                            return bass.AP(tensor=base.tensor, offset=base.offset, ap=[[1023, 128], [1, 256]])
                        P.dma("sp", lambda e, which=which, kc=kc, bsrc=bsrc, p_=p_: e.dma_start(out=btab2[p_][:, which, kc, :], in_=bsrc()),
                              reads=[B_TT], writes=[B_bt2[p_]])

            def head_body(h, kaug, vaug, qaug, btab, B_ka, B_va, B_qa, B_qm, B_bt):
                P.op("dve", lambda e: e.tensor_reduce(out=km[:, :], in_=kaug[0:64, :].rearrange("p (n k) -> p n k", k=BLK),
                                                      axis=AX.X, op=ALU.add), reads=[B_ka], writes=[B_km])
                P.op("dve", lambda e: e.tensor_scalar(out=kmb[:, :], in0=km[:, :], scalar1=1.0 / BLK, scalar2=None,
                                                      op0=ALU.mult), reads=[B_km], writes=[B_km])
                P.op("dve", lambda e, h=h: e.scalar_tensor_tensor(out=mulv[:], in0=a2r[:], scalar=rb31[:, h:h + 1], in1=a1r[:],
                                                                  op0=ALU.mult, op1=ALU.add),
                     reads=[B_a1, B_a2, B_rb31], writes=[B_mulv])
                def prologue(g, h=h):
                    own = 15 + g
                    gi = g % 2
                    for t in range(2):
                        P.op("pe", lambda e, g=g, t=t: e.matmul(
                            out=pb[7][:, t * NB:(t + 1) * NB], lhsT=qaug[0:64, (2 * g + t) * 128:(2 * g + t + 1) * 128],
                            rhs=kmb[:, :], start=True, stop=True), reads=[B_qa, B_km], writes=[PB[7]])
                    P.op("dve", lambda e, g=g, gi=gi: e.tensor_tensor(
                        out=gm[gi][:], in0=pb[7][:, 0:2 * NB].rearrange("p (t n) -> p t n", t=2),
                        in1=pmr[:, g * NB:(g + 1) * NB].unsqueeze(1).to_broadcast([128, 2, NB]), op=ALU.add),
                        reads=[PB[7], B_pm], writes=[B_g[gi]])
                    for t in range(2):
                        P.op("dve", lambda e, gi=gi, t=t: e.max(out=mx[gi][:, t, :], in_=gm[gi][:, t, :]),
                             reads=[B_g[gi]], writes=[B_g[gi]])
                    for t in range(2):
                        P.op("dve", lambda e, gi=gi, t=t: e.tensor_scalar(
                            out=sel[gi][:, t, 64:96], in0=gm[gi][:, t, :], scalar1=mx[gi][:, t, 2:3], scalar2=None, op0=ALU.is_ge),
                            reads=[B_g[gi]], writes=[B_g[gi]])
                    P.op("dve", lambda e, gi=gi: e.scalar_tensor_tensor(
                        out=sel[gi][:, :, 64:96], in0=gm[gi][:], scalar=-1e29, in1=sel[gi][:, :, 64:96], op0=ALU.is_gt, op1=ALU.mult),
                        reads=[B_g[gi]], writes=[B_g[gi]])
                    P.op("dve", lambda e, gi=gi, g=g: e.tensor_tensor(
                        out=sel[gi][:, :, 64:96], in0=sel[gi][:, :, 64:96],
                        in1=mulv[:, g * NB:(g + 1) * NB].unsqueeze(1).to_broadcast([128, 2, NB]), op=ALU.mult),
                        reads=[B_g[gi], B_mulv], writes=[B_g[gi]])
                    P.op("dve", lambda e, gi=gi, g=g: e.tensor_tensor(
                        out=sel[gi][:, :, 64:96], in0=sel[gi][:, :, 64:96],
                        in1=a3r[:, g * NB:(g + 1) * NB].unsqueeze(1).to_broadcast([128, 2, NB]), op=ALU.add),
                        reads=[B_g[gi], B_a3], writes=[B_g[gi]])

                def prologue2(g):
                    gi = g % 2
                    for t in range(2):
                        P.op("pe", lambda e, gi=gi, t=t: e.transpose(out=pb[7][0:96, 128 + t * 128:128 + (t + 1) * 128],
                                                                     in_=sel[gi][:, t, :], identity=idf[:]),
                             reads=[B_g[gi], B_id], writes=[PB[7]])
                    P.op("act", lambda e, g=g: e.copy(out=qaug[64:96, g * 256:(g + 1) * 256], in_=pb[7][64:96, 128:384]),
                         reads=[PB[7]], writes=[B_qm[g]])

                iters = [(g, n) for g in range(NG) for n in range(15 + g + 1)]

                def stageA(idx):
                    g, n = iters[idx]
                    own = 15 + g
                    sbk = idx % 3
                    pi = idx % 3
                    for c in range(2):
                        P.op("pe", lambda e, n=n, c=c, g=g, sbk=sbk: e.matmul(
                            out=pb[sbk][:, c * 256:(c + 1) * 256], lhsT=kaug[0:96, (2 * n + c) * 128:(2 * n + c + 1) * 128],
                            rhs=qaug[0:96, g * 256:(g + 1) * 256], start=True, stop=True),
                            reads=[B_ka, B_boh, B_qa, B_qm[g]], writes=[PB[sbk]])
                    if n >= own - 1:
                        which = 0 if n == own else 1
                        si = n % 2
                        P.op("dve", lambda e, sbk=sbk, which=which, si=si: e.tensor_tensor(
                            out=stmp[si][:], in0=pb[sbk][:], in1=btab[:, which, :, :].rearrange("p c q -> p (c q)"), op=ALU.add),
                            reads=[PB[sbk], B_bt], writes=[B_stmp[si]])
                        P.op("act", lambda e, si=si, pi=pi: e.activation(out=pT[pi][:], in_=stmp[si][:], func=AF.Exp, scale=0.125),
                             reads=[B_stmp[si]], writes=[B_pT[pi]])
                    else:
                        P.op("act", lambda e, sbk=sbk, pi=pi: e.activation(out=pT[pi][:], in_=pb[sbk][:], func=AF.Exp, scale=0.125),
                             reads=[PB[sbk]], writes=[B_pT[pi]])

                def stageB(idx):
                    g, n = iters[idx]
                    own = 15 + g
                    gi = g % 2
                    pi = idx % 3
                    ob = 3 + 2 * (g % 2)
                    for c in range(2):
                        for t in range(2):
                            P.op("pe", lambda e, n=n, c=c, t=t, pi=pi, ob=ob, own=own: e.matmul(
                                out=pb[ob + t][:, 0:65], lhsT=pT[pi][:, c * 256 + t * 128:c * 256 + (t + 1) * 128],
                                rhs=vaug[:, 2 * n + c, :], start=(n == 0 and c == 0), stop=(n == own and c == 1)),
                                reads=[B_pT[pi], B_va], writes=[PB[ob + t]])
                    if n == own:
                        for t in range(2):
                            P.op("dve", lambda e, ob=ob, gi=gi, t=t: e.reciprocal(out=rec[gi][:, t, :], in_=pb[ob + t][:, 64:65]),
                                 reads=[PB[ob + t]], writes=[B_rec[gi]])
                            P.op("dve", lambda e, ob=ob, gi=gi, g=g, t=t: e.tensor_scalar(
                                out=oall[:, 2 * g + t, :], in0=pb[ob + t][:, 0:64], scalar1=rec[gi][:, t, 0:1], scalar2=None, op0=ALU.mult),
                                reads=[PB[ob + t], B_rec[gi]], writes=[B_oall])

                prologue(0)
                prologue2(0)
                stageA(0)
                for idx in range(len(iters)):
                    if idx + 1 < len(iters):
                        stageA(idx + 1)
                    g_, n_ = iters[idx]
                    if n_ == 0 and g_ + 1 < NG:
                        prologue(g_ + 1)
                    if n_ == 8 and g_ + 1 < NG:
                        prologue2(g_ + 1)
                    stageB(idx)
                P.dma("sp", lambda e, h=h: e.dma_start(out=OAv[:, :, h, :], in_=oall[:]), reads=[B_oall], writes=[B_OA])

            emit_loads(0)
            for h in range(NH):
                p_ = h % 2
                if h + 1 < NH:
                    emit_loads(h + 1)
                head_body(h, kaug2[p_], vaug2[p_], qaug2[p_], btab2[p_], B_ka2[p_], B_va2[p_], B_qa2[p_], B_qm2[p_], B_bt2[p_])
        P.barrier()

        B_X1 = [Buf() for _ in range(NT0)]
        with contextlib.ExitStack() as st:
            wo = sb("wo_sb", [128, 8, D], BF16, st)
            B_wo = Buf()
            P.dma("sp", lambda e: e.dma_start(out=wo[:], in_=wo_b.rearrange("(c p) n -> p c n", p=128)),
                  reads=[B_w["wo"]], writes=[B_wo])
            ot = [sb("p3o%d" % i, [128, D], F32, st) for i in range(2)]
            obf = [sb("p3ob%d" % i, [128, D], BF16, st) for i in range(2)]
            oT = [sb("p3oT%d" % i, [128, 8, 128], BF16, st) for i in range(2)]
            xt3 = [sb("p3x%d" % i, [128, D], F32, st) for i in range(2)]
            B3 = [[Buf() for _ in range(4)] for _ in range(2)]
            for ti in range(NT0):
                i = ti % 2
                Bo, Bob, BoT, Bx = B3[i]
                P.dma("sp", lambda e, ti=ti, i=i: e.dma_start(out=ot[i][:], in_=OA[(ti + 1) * 128:(ti + 2) * 128, :]),
                      reads=[B_OA], writes=[Bo])
                P.dma("sp", lambda e, ti=ti, i=i: e.dma_start(out=xt3[i][:], in_=xv[(31 + ti) * 128:(32 + ti) * 128, :]),
                      writes=[Bx])
                P.op("act", lambda e, i=i: e.copy(out=obf[i][:], in_=ot[i][:]), reads=[Bo], writes=[Bob])
                tpv = pb[0][:].bitcast(BF16)
                for c in range(8):
                    P.op("pe", lambda e, c=c, i=i, tpv=tpv: e.transpose(out=tpv[:, c * 128:(c + 1) * 128],
                                                                       in_=obf[i][:, c * 128:(c + 1) * 128], identity=idb[:]),
                         reads=[Bob, B_id], writes=[PB[0]])
                P.op("act", lambda e, i=i, tpv=tpv: e.copy(out=oT[i][:], in_=tpv.rearrange("p (c t) -> p c t", c=8)),
                     reads=[PB[0]], writes=[BoT])
                for half in range(2):
                    bank = 1 + 2 * i + half
                    for c in range(8):
                        P.op("pe", lambda e, c=c, i=i, half=half, bank=bank: e.matmul(
                            out=pb[bank][:], lhsT=oT[i][:, c, :], rhs=wo[:, c, half * 512:(half + 1) * 512],
                            start=(c == 0), stop=(c == 7)), reads=[BoT, B_wo], writes=[PB[bank]])
                    P.op("dve", lambda e, i=i, half=half, bank=bank: e.tensor_tensor(
                        out=xt3[i][:, half * 512:(half + 1) * 512], in0=pb[bank][:], in1=xt3[i][:, half * 512:(half + 1) * 512],
                        op=ALU.add), reads=[PB[bank], Bx], writes=[Bx])
                P.dma("sp", lambda e, ti=ti, i=i: e.dma_start(out=X1[ti * 128:(ti + 1) * 128, :], in_=xt3[i][:]),
                      reads=[Bx], writes=[B_X1[ti]])
        P.barrier()

        def peer_phase(l, Xin, B_in, ntiles, Xout, B_out, out_row0, final_norm):
            groups = []
            t0 = 0
            if ntiles % 4 == 1:
                groups.append((0, 1))
                t0 = 1
            while t0 < ntiles:
                groups.append((t0, 4))
                t0 += 4
            with contextlib.ExitStack() as st:
                gf, Bgf = load_rep(st, "pgf%d" % l, norm_ffn[l:l + 1, :])
                if final_norm:
                    gfin, Bgfin = load_rep(st, "pgfin", norm_final[0:1, :])
                skt = sb("skt%d" % l, [128, 16, 128], BF16, st)
                B_skt = Buf()
                P.dma("sp", lambda e: e.dma_start(out=skt[:], in_=skT_b[l].rearrange("a k n -> k a n")),
                      reads=[B_w["skT"]], writes=[B_skt])
                hT = sb("phT%d" % l, [128, 8, 512], BF16, st)
                B_hT = Buf()
                s_sb = sb("ps%d" % l, [128, 4, 16, 128], F32, st)
                B_s = [Buf() for _ in range(4)]
                acc = sb("pacc%d" % l, [128, 4, D], F32, st)
                B_acc = [Buf() for _ in range(4)]
                th = sb("pth%d" % l, [128, 4, 8], F32, st)
                eb = sb("peb%d" % l, [128, 4, 8], F32, st)
                B_st = [Buf() for _ in range(4)]
                for (gt0, gn) in groups:
                    Tg = gn * 128
                    with contextlib.ExitStack() as sa:
                        wpq = sb("wpq%d_%d" % (l, gt0), [128, 8, 2048], BF16, sa)
                        B_wpq = Buf()
                        for c in range(8):
                            P.dma("sp", lambda e, c=c: e.dma_start(out=wpq[:, c, :], in_=wpq_b[l][c * 128:(c + 1) * 128, :]),
                                  reads=[B_w["wpq"]], writes=[B_wpq])
                        nbp = norm_bufs(sa, "pn%d_%d" % (l, gt0))
                        qT = sb("pqT%d_%d" % (l, gt0), [128, 16, 512], BF16, sa)
                        B_qT = Buf()
                        v16 = sb("pv16%d_%d" % (l, gt0), [128, 2, 16], F32, sa)
                        t128 = sb("pt128%d_%d" % (l, gt0), [128, 128], F32, sa)
                        cand = sb("pcand%d_%d" % (l, gt0), [128, 256], F32, sa)
                        cand2 = sb("pcand2%d_%d" % (l, gt0), [128, 256], F32, sa)
                        b16 = sb("pb16%d_%d" % (l, gt0), [128, 8, 16], F32, sa)
                        e16 = sb("pe16%d_%d" % (l, gt0), [128, 8, 16], F32, sa)
                        zz = sb("pzz%d_%d" % (l, gt0), [128, 8], F32, sa)
                        B_tk = Buf()
                        for j in range(gn):
                            ti = gt0 + j
                            P.dma("sp", lambda e, ti=ti, j=j: e.dma_start(out=acc[:, j, :], in_=Xin[ti * 128:(ti + 1) * 128, :]),
                                  reads=[B_in[ti]], writes=[B_acc[j]])
                            norm_T(sa, "pp", acc[:, j, :], B_acc[j], gf[:], Bgf, hT[:, :, j * 128:(j + 1) * 128], B_hT, 0, nbp)
                        for hc in range(16):
                            bank = 1 + hc % 2
                            for c in range(8):
                                P.op("pe", lambda e, hc=hc, c=c, bank=bank, Tg=Tg: e.matmul(
                                    out=pb[bank][:, 0:Tg], lhsT=wpq[:, c, hc * 128:(hc + 1) * 128], rhs=hT[:, c, 0:Tg],
                                    start=(c == 0), stop=(c == 7)), reads=[B_wpq, B_hT], writes=[PB[bank]])
                            P.op("act", lambda e, hc=hc, bank=bank, Tg=Tg: e.copy(out=qT[:, hc, 0:Tg], in_=pb[bank][:, 0:Tg]),
                                 reads=[PB[bank]], writes=[B_qT])
                        for j in range(gn):
                            for q4 in range(4):
                                bank = 3 + q4 % 2
                                for k in range(4):
                                    hc = q4 * 4 + k
                                    P.op("pe", lambda e, hc=hc, k=k, j=j, bank=bank: e.matmul(
                                        out=pb[bank][:, k * 128:(k + 1) * 128], lhsT=qT[:, hc, j * 128:(j + 1) * 128],
                                        rhs=skt[:, hc, :], start=True, stop=True), reads=[B_qT, B_skt], writes=[PB[bank]])
                                P.op("act", lambda e, j=j, q4=q4, bank=bank: e.copy(
                                    out=s_sb[:, j, q4 * 4:(q4 + 1) * 4, :], in_=pb[bank][:].rearrange("p (a n) -> p a n", a=4)),
                                    reads=[PB[bank]], writes=[B_s[j]])
                        for j in range(gn):
                            for hh in range(8):
                                for c2 in range(2):
                                    src = s_sb[:, j, 2 * hh + c2, :]
                                    P.op("dve", lambda e, src=src, c2=c2: e.max(out=v16[:, c2, 0:8], in_=src),
                                         reads=[B_s[j]], writes=[B_tk])
                                    P.op("dve", lambda e, src=src, c2=c2: e.match_replace(
                                        out=t128[:], in_to_replace=v16[:, c2, 0:8], in_values=src, imm_value=-1e30),
                                        reads=[B_s[j], B_tk], writes=[B_tk])
                                    P.op("dve", lambda e, c2=c2: e.max(out=v16[:, c2, 8:16], in_=t128[:]),
                                         reads=[B_tk], writes=[B_tk])
                                P.op("dve", lambda e: e.tensor_tensor(
                                    out=cand[:].rearrange("p (a b) -> p a b", a=16),
                                    in0=v16[:, 0, :].unsqueeze(2).to_broadcast([128, 16, 16]),
                                    in1=v16[:, 1, :].unsqueeze(1).to_broadcast([128, 16, 16]), op=ALU.add),
                                    reads=[B_tk], writes=[B_tk])
                                P.op("dve", lambda e, hh=hh: e.max(out=b16[:, hh, 0:8], in_=cand[:]), reads=[B_tk], writes=[B_tk])
                                P.op("dve", lambda e, hh=hh: e.match_replace(
                                    out=cand2[:], in_to_replace=b16[:, hh, 0:8], in_values=cand[:], imm_value=-1e30),
                                    reads=[B_tk], writes=[B_tk])
                                P.op("dve", lambda e, hh=hh: e.max(out=b16[:, hh, 8:16], in_=cand2[:]), reads=[B_tk], writes=[B_tk])
                            P.op("dve", lambda e, j=j: e.tensor_copy(out=th[:, j, :], in_=b16[:, :, 15]), reads=[B_tk], writes=[B_st[j]])
                            P.op("dve", lambda e: e.tensor_tensor(out=e16[:], in0=b16[:], in1=b16[:, :, 0:1].to_broadcast([128, 8, 16]),
                                                                  op=ALU.subtract), reads=[B_tk], writes=[B_tk])
                            P.op("act", lambda e: e.activation(out=e16[:], in_=e16[:], func=AF.Exp), reads=[B_tk], writes=[B_tk])
                            P.op("dve", lambda e: e.tensor_reduce(out=zz[:], in_=e16[:], axis=AX.X, op=ALU.add), reads=[B_tk], writes=[B_tk])
                            P.op("act", lambda e: e.activation(out=zz[:], in_=zz[:], func=AF.Ln), reads=[B_tk], writes=[B_tk])
                            P.op("dve", lambda e, j=j: e.scalar_tensor_tensor(
                                out=eb[:, j, :], in0=b16[:, :, 0], scalar=-1.0, in1=zz[:], op0=ALU.mult, op1=ALU.subtract),
                                reads=[B_tk], writes=[B_st[j]])
                            P.op("dve", lambda e, j=j: e.tensor_tensor(
                                out=s_sb[:, j, :, :].rearrange("p (h c) n -> p h c n", c=2)[:, :, 0, :],
                                in0=s_sb[:, j, :, :].rearrange("p (h c) n -> p h c n", c=2)[:, :, 0, :],
                                in1=eb[:, j, :].unsqueeze(2).to_broadcast([128, 8, 128]), op=ALU.add),
                                reads=[B_st[j]], writes=[B_s[j]])
                            P.op("dve", lambda e, j=j: e.tensor_tensor(out=th[:, j, :], in0=th[:, j, :], in1=eb[:, j, :], op=ALU.add),
                                 reads=[B_st[j]], writes=[B_st[j]])
                            P.op("act", lambda e, j=j: e.activation(out=th[:, j, :], in_=th[:, j, :], func=AF.Exp),
                                 reads=[B_st[j]], writes=[B_st[j]])
                            P.op("dve", lambda e, j=j: e.tensor_scalar(out=th[:, j, :], in0=th[:, j, :], scalar1=1.0 - 1e-5, scalar2=None,
                                                                       op0=ALU.mult), reads=[B_st[j]], writes=[B_st[j]])
                    P.barrier()
                    with contextlib.ExitStack() as se:
                        utg = [sb("utg%d_%d_%d" % (l, gt0, i), [128, 8, 1024], BF16, se) for i in range(2)]
                        vg = [sb("vg%d_%d_%d" % (l, gt0, i), [128, 8, D], BF16, se) for i in range(2)]
                        B_ut = [Buf() for _ in range(2)]
                        B_vg = [Buf() for _ in range(2)]
                        gA = [sb("gA%d_%d_%d" % (l, gt0, i), [128, 8, 512], BF16, se) for i in range(2)]
                        B_gA = [[Buf() for _ in range(8)] for _ in range(2)]
                        cc = [sb("cc%d_%d_%d" % (l, gt0, i), [128, 8, 128], F32, se) for i in range(2)]
                        ee = [sb("ee%d_%d_%d" % (l, gt0, i), [128, 8, 128], F32, se) for i in range(3)]
                        wm = [sb("wm%d_%d_%d" % (l, gt0, i), [128, 8, 128], BF16, se) for i in range(3)]
                        B_cc = [Buf() for _ in range(3)]
                        B_ee = [Buf() for _ in range(3)]
                        B_wm = [Buf() for _ in range(3)]
                        GT = [sb("GT%d_%d_%d" % (l, gt0, i), [128, 8, 128], BF16, se) for i in range(2)]
                        B_GT = [Buf() for _ in range(2)]
                        uTv = uT_b[l].rearrange("(c p) e -> p c e", p=128)
                        pvv = pv_b[l].rearrange("(i j) d -> j i d", j=128)

                        def load_eg(eg):
                            i = eg % 2
                            for c in range(0, 8, 4):
                                P.dma("sp", lambda e, eg=eg, i=i, c=c: e.dma_start(
                                    out=utg[i][:, c:c + 4, :], in_=uTv[:, c:c + 4, eg * 1024:(eg + 1) * 1024]),
                                    reads=[B_uT[l][eg]], writes=[B_ut[i]])
                                P.dma("sp", lambda e, eg=eg, i=i, c=c: e.dma_start(
                                    out=vg[i][:, c:c + 4, :], in_=pvv[:, eg * 8 + c:eg * 8 + c + 4, :]),
                                    reads=[B_pv[l][eg]], writes=[B_vg[i]])

                        zt = sb("zt%d_%d" % (l, gt0), [128, 512], F32, se)
                        B_zt = Buf()
                        P.op("pool", lambda e: e.memset(zt[:], 0.0), writes=[B_zt])
                        bgq = []

                        pend = []

                        def pump(k):
                            for _ in range(k):
                                while pend:
                                    pend.pop(0)()
                                if bgq:
                                    ep = bgq.pop(0)()
                                    if ep is not None:
                                        pend.append(ep)
                            if not bgq:
                                while pend:
                                    pend.pop(0)()

                        def stage1_tasks(eg):
                            i = eg % 2
                            tasks = []
                            for ch in range(8):
                                for part in range(2):
                                    def task(ch=ch, i=i, part=part):
                                        bank = ch % 2
                                        for c in range(part * 4, part * 4 + 4):
                                            P.op("pe", lambda e, ch=ch, c=c, i=i, bank=bank: e.matmul(
                                                out=pb[bank][:, 0:Tg], lhsT=utg[i][:, c, ch * 128:(ch + 1) * 128], rhs=hT[:, c, 0:Tg],
                                                start=(c == 0), stop=(c == 7)), reads=[B_ut[i], B_hT], writes=[PB[bank]])
                                        if part == 1:
                                            return lambda: P.op("act", lambda e, ch=ch, bank=bank, i=i: e.copy(
                                                out=gA[i][:, ch, 0:Tg], in_=pb[bank][:, 0:Tg]),
                                                reads=[PB[bank]], writes=[B_gA[i][ch]])
                                        return None
                                    tasks.append(task)
                            return tasks

                        def gv_tasks(j, gk, i):
                            tasks = []
                            for half in range(2):
                                for part in range(2):
                                    def task(half=half, j=j, gk=gk, i=i, part=part):
                                        bank = 6 + half
                                        for ch in range(part * 4, part * 4 + 4):
                                            P.op("pe", lambda e, ch=ch, gk=gk, i=i, half=half, bank=bank: e.matmul(
                                                out=pb[bank][:], lhsT=GT[gk][:, ch, :], rhs=vg[i][:, ch, half * 512:(half + 1) * 512],
                                                start=(ch == 0), stop=(ch == 7)), reads=[B_GT[gk], B_vg[i]], writes=[PB[bank]])
                                        if part == 1:
                                            return lambda: P.op("dve", lambda e, j=j, half=half, bank=bank: e.tensor_tensor(
                                                out=acc[:, j, half * 512:(half + 1) * 512], in0=pb[bank][:],
                                                in1=acc[:, j, half * 512:(half + 1) * 512], op=ALU.add),
                                                reads=[PB[bank]], writes=[B_acc[j]])
                                        return None
                                    tasks.append(task)
                            return tasks

                        load_eg(0)
                        for t_ in stage1_tasks(0):
                            ep_ = t_()
                            if ep_ is not None:
                                ep_()
                        hcnt = 0
                        for eg in range(16):
                            i = eg % 2
                            if eg + 1 < 16 and gn < 4:
                                load_eg(eg + 1)
                                bgq.extend(stage1_tasks(eg + 1))
                            P.op("act", lambda e, i=i: e.activation(out=gA[i][:, :, 0:Tg], in_=gA[i][:, :, 0:Tg], func=AF.Gelu_apprx_tanh),
                                 reads=B_gA[i], writes=B_gA[i])
                            for j in range(gn):
                                wb0 = 2 + 2 * (j % 2)
                                for wb_ in (wb0, wb0 + 1):
                                    P.op("act", lambda e, wb_=wb_: e.copy(out=pb[wb_][:], in_=zt[:]), reads=[B_zt], writes=[PB[wb_]])
                                for hh in range(8):
                                    k = hcnt % 3
                                    hcnt += 1
                                    if hh in (0, 2, 4, 6, 7):
                                        for i8 in range(8):
                                            P.op("act", lambda e, j=j, hh=hh, k=k, eg=eg, i8=i8: e.activation(
                                                out=ee[k][:, i8, :], in_=s_sb[:, j, 2 * hh + 1, :], func=AF.Exp,
                                                bias=s_sb[:, j, 2 * hh, eg * 8 + i8:eg * 8 + i8 + 1]),
                                                reads=[B_s[j]], writes=[B_ee[k]])
                                    else:
                                        kc = (hcnt // 2) % 2
                                        P.op("pool", lambda e, j=j, hh=hh, kc=kc, eg=eg: e.tensor_tensor(
                                            out=cc[kc][:], in0=s_sb[:, j, 2 * hh, eg * 8:(eg + 1) * 8].unsqueeze(2).to_broadcast([128, 8, 128]),
                                            in1=s_sb[:, j, 2 * hh + 1, :].unsqueeze(1).to_broadcast([128, 8, 128]), op=ALU.add),
                                            reads=[B_s[j]], writes=[B_cc[kc]])
                                        P.op("act", lambda e, k=k, kc=kc: e.activation(out=ee[k][:], in_=cc[kc][:], func=AF.Exp),
                                             reads=[B_cc[kc]], writes=[B_ee[k]])
                                    P.op("dve", lambda e, j=j, hh=hh, k=k: e.scalar_tensor_tensor(
                                        out=wm[k][:], in0=ee[k][:], scalar=th[:, j, hh:hh + 1], in1=ee[k][:],
                                        op0=ALU.is_ge, op1=ALU.mult), reads=[B_ee[k], B_st[j]], writes=[B_wm[k]])
                                    for ch in range(8):
                                        bank = wb0 + ch // 4
                                        P.op("pe", lambda e, ch=ch, k=k, bank=bank, hh=hh: e.matmul(
                                            out=pb[bank][:, (ch % 4) * 128:(ch % 4 + 1) * 128], lhsT=wm[k][:, ch, :], rhs=idb[:],
                                            start=False, stop=(hh == 7), skip_group_check=True), reads=[B_wm[k], B_id], writes=[PB[bank]])
                                    pump(1)
                                gk = j % 2
                                for half in range(2):
                                    P.op("dve", lambda e, j=j, gk=gk, half=half, wb0=wb0, i=i: e.tensor_tensor(
                                        out=GT[gk][:, half * 4:(half + 1) * 4, :],
                                        in0=pb[wb0 + half][:].rearrange("p (a t) -> p a t", a=4),
                                        in1=gA[i][:, half * 4:(half + 1) * 4, j * 128:(j + 1) * 128], op=ALU.mult),
                                        reads=[PB[wb0 + half]] + B_gA[i][half * 4:(half + 1) * 4], writes=[B_GT[gk]])
                                bgq.extend(gv_tasks(j, gk, i))
                                if gn == 4 and j == 0 and eg + 1 < 16:
                                    assert len(bgq) <= 4, len(bgq)
                                    load_eg(eg + 1)
                                    bgq.extend(stage1_tasks(eg + 1))
                            if gn < 4 or eg == 15:
                                pump(len(bgq))
                            else:
                                while pend:
                                    pend.pop(0)()
                        if final_norm:
                            fsq = sb("fsq%d" % gt0, [128, D], F32, se)
                            fss = sb("fss%d" % gt0, [128, 1], F32, se)
                            B_f = Buf()
                        for j in range(gn):
                            ti = gt0 + j
                            if final_norm:
                                P.op("act", lambda e, j=j: e.activation(out=fsq[:], in_=acc[:, j, :], func=AF.Square, accum_out=fss[:]),
                                     reads=[B_acc[j]], writes=[B_f])
                                P.op("dve", lambda e: e.tensor_scalar(out=fss[:], in0=fss[:], scalar1=1.0 / D, scalar2=EPS,
                                                                      op0=ALU.mult, op1=ALU.add), reads=[B_f], writes=[B_f])
                                P.op("act", lambda e: e.activation(out=fss[:], in_=fss[:], func=AF.Sqrt), reads=[B_f], writes=[B_f])
                                P.op("dve", lambda e: e.reciprocal(out=fss[:], in_=fss[:]), reads=[B_f], writes=[B_f])
                                P.op("dve", lambda e, j=j: e.scalar_tensor_tensor(
                                    out=acc[:, j, :], in0=acc[:, j, :], scalar=fss[:, 0:1], in1=gfin[:], op0=ALU.mult, op1=ALU.mult),
                                    reads=[B_f, Bgfin], writes=[B_acc[j]])
                            P.dma("sp", lambda e, ti=ti, j=j: e.dma_start(
                                out=Xout[(out_row0 + ti) * 128:(out_row0 + ti + 1) * 128, :], in_=acc[:, j, :]),
                                reads=[B_acc[j]], writes=[B_out[out_row0 + ti]])
                    P.barrier()

        B_X2 = [Buf() for _ in range(NT0)]
        peer_phase(0, X1, B_X1, NT0, X2, B_X2, 0, False)

        B_X3 = [Buf() for _ in range(32)]
        with contextlib.ExitStack() as st:
            w1 = sb("w1_sb", [128, 8, 2 * D], BF16, st)
            w2 = sb("w2_sb", [128, 8, D], BF16, st)
            B_w1, B_w2 = Buf(), Buf()
            for c in range(8):
                P.dma("sp", lambda e, c=c: e.dma_start(out=w1[:, c, :], in_=wpw1_b[c * 128:(c + 1) * 128, :]),
                      reads=[B_w["wpw1"]], writes=[B_w1])
            P.dma("sp", lambda e: e.dma_start(out=w2[:], in_=wpw2_b.rearrange("(c p) n -> p c n", p=128)),
                  reads=[B_w["wpw2"]], writes=[B_w2])
            g1, Bg1 = load_rep(st, "g1", norm_mix[1:2, :])
            b2r, Bb2 = load_rep(st, "b2r", b_pw2[0:1, :])
            halo, B_halo = load_rep(st, "halo_sb", halo_in[0:1, :], 1)
            bp1 = sb("bp1", [128, 16], F32, st)
            wdw = sb("wdw", [128, 8, CW], F32, st)
            bdw = sb("bdw", [128, 8], F32, st)
            lng = sb("lng", [128, 8], F32, st)
            lnb = sb("lnb", [128, 8], F32, st)
            B_cp = Buf()
            P.dma("sp", lambda e: e.dma_start(out=bp1[:], in_=b_pw1), writes=[B_cp])
            P.dma("sp", lambda e: e.dma_start(out=wdw[:], in_=w_dwT.rearrange("(c p) k -> p c k", p=128)), writes=[B_cp])
            P.dma("sp", lambda e: e.dma_start(out=bdw[:], in_=b_dw), writes=[B_cp])
            P.dma("sp", lambda e: e.dma_start(out=lng[:], in_=ln_g), writes=[B_cp])
            P.dma("sp", lambda e: e.dma_start(out=lnb[:], in_=ln_b), writes=[B_cp])
            ones = sb("ones", [128, 128], F32, st)
            P.op("dve", lambda e: e.memset(ones[:], 1.0), writes=[B_cp])
            xg = sb("cxg", [128, 4, D], F32, st)
            B_xg = [Buf() for _ in range(4)]
            hTc = sb("chT", [128, 8, 512], BF16, st)
            B_hTc = Buf()
            nbc = norm_bufs(st, "cn")
            uTb = sb("cuT", [128, 8, 32 + 512], F32, st)
            B_u = [Buf() for _ in range(8)]
            sg = [sb("csg%d" % i, [128, 512], F32, st) for i in range(2)]
            B_sg = [Buf() for _ in range(2)]
            cv = sb("ccv", [128, 8, 512], F32, st)
            B_cv = [Buf() for _ in range(8)]
            sq2 = [sb("csq%d" % i, [128, 512], F32, st) for i in range(2)]
            B_sq2 = [Buf() for _ in range(2)]
            mean = sb("cmean", [128, 512], F32, st)
            var = sb("cvar", [128, 512], F32, st)
            B_mv = Buf()
            t1 = [sb("ct1%d" % i, [128, 512], F32, st) for i in range(2)]
            B_t1 = [Buf() for _ in range(2)]
            zT = sb("czT", [128, 8, 512], BF16, st)
            B_zT = [Buf() for _ in range(8)]
            groups = [(0, 1)] + [(1 + 4 * k, 4) for k in range(8)]
            for (gt0, gn) in groups:
                Tg = gn * 128
                for j in range(gn):
                    ti = gt0 + j
                    P.dma("sp", lambda e, ti=ti, j=j: e.dma_start(out=xg[:, j, :], in_=X2[ti * 128:(ti + 1) * 128, :]),
                          reads=[B_X2[ti]], writes=[B_xg[j]])
                    norm_T(st, "cc", xg[:, j, :], B_xg[j], g1[:], Bg1, hTc[:, :, j * 128:(j + 1) * 128], B_hTc, 0, nbc)
                if gt0 > 0:
                    prevT = 128 if gt0 == 1 else 512
                    for ch in range(8):
                        P.op("pool", lambda e, ch=ch, prevT=prevT: e.tensor_copy(out=uTb[:, ch, 0:32], in_=uTb[:, ch, prevT:prevT + 32]),
                             reads=[B_u[ch]], writes=[B_u[ch]])
                for ch in range(8):
                    bv_, bg_ = 1 + (ch % 2) * 2, 2 + (ch % 2) * 2
                    for (bank, col) in ((bv_, ch), (bg_, 8 + ch)):
                        for c in range(8):
                            P.op("pe", lambda e, c=c, bank=bank, col=col, Tg=Tg: e.matmul(
                                out=pb[bank][:, 0:Tg], lhsT=w1[:, c, col * 128:(col + 1) * 128], rhs=hTc[:, c, 0:Tg],
                                start=(c == 0), stop=(c == 7)), reads=[B_w1, B_hTc], writes=[PB[bank]])
                    si = ch % 2
                    P.op("act", lambda e, ch=ch, bg_=bg_, si=si, Tg=Tg: e.activation(
                        out=sg[si][:, 0:Tg], in_=pb[bg_][:, 0:Tg], func=AF.Sigmoid, bias=bp1[:, 8 + ch:9 + ch]),
                        reads=[PB[bg_], B_cp], writes=[B_sg[si]])
                    P.op("dve", lambda e, ch=ch, bv_=bv_, si=si, Tg=Tg: e.scalar_tensor_tensor(
                        out=uTb[:, ch, 32:32 + Tg], in0=pb[bv_][:, 0:Tg], scalar=bp1[:, ch:ch + 1], in1=sg[si][:, 0:Tg],
                        op0=ALU.add, op1=ALU.mult), reads=[PB[bv_], B_sg[si], B_cp], writes=[B_u[ch]])
                    if gt0 == 0:
                        P.op("dve", lambda e, ch=ch, Tg=Tg: e.tensor_scalar(
                            out=uTb[:, ch, 32:32 + Tg], in0=uTb[:, ch, 32:32 + Tg], scalar1=halo[:, 0:1], scalar2=None, op0=ALU.mult),
                            reads=[B_u[ch], B_halo], writes=[B_u[ch]])
                if gt0 == 0:
                    continue
                for ch in range(8):
                    eng = "dve"
                    P.op(eng, lambda e, ch=ch: e.tensor_scalar(
                        out=cv[:, ch, :], in0=uTb[:, ch, 2:514], scalar1=wdw[:, ch, 0:1], scalar2=bdw[:, ch:ch + 1],
                        op0=ALU.mult, op1=ALU.add), reads=[B_u[ch], B_cp], writes=[B_cv[ch]])
                    for k in range(1, CW):
                        P.op(eng, lambda e, ch=ch, k=k: e.scalar_tensor_tensor(
                            out=cv[:, ch, :], in0=uTb[:, ch, 2 + k:514 + k], scalar=wdw[:, ch, k:k + 1], in1=cv[:, ch, :],
                            op0=ALU.mult, op1=ALU.add), reads=[B_u[ch], B_cp, B_cv[ch]], writes=[B_cv[ch]])
                for ch in range(8):
                    si = ch % 2
                    P.op("act", lambda e, ch=ch, si=si: e.activation(out=sq2[si][:], in_=cv[:, ch, :], func=AF.Square),
                         reads=[B_cv[ch]], writes=[B_sq2[si]])
                    P.op("pe", lambda e, ch=ch: e.matmul(out=pb[5][:], lhsT=ones[:], rhs=cv[:, ch, :], start=(ch == 0), stop=(ch == 7)),
                         reads=[B_cv[ch], B_cp], writes=[PB[5]])
                    P.op("pe", lambda e, ch=ch, si=si: e.matmul(out=pb[6][:], lhsT=ones[:], rhs=sq2[si][:], start=(ch == 0), stop=(ch == 7)),
                         reads=[B_sq2[si], B_cp], writes=[PB[6]])
                P.op("dve", lambda e: e.tensor_scalar(out=mean[:], in0=pb[5][:], scalar1=1.0 / D, scalar2=None, op0=ALU.mult),
                     reads=[PB[5]], writes=[B_mv])
                P.op("dve", lambda e: e.tensor_tensor(out=var[:], in0=mean[:], in1=mean[:], op=ALU.mult), reads=[B_mv], writes=[B_mv])
                P.op("dve", lambda e: e.scalar_tensor_tensor(out=var[:], in0=pb[6][:], scalar=1.0 / D, in1=var[:],
                                                             op0=ALU.mult, op1=ALU.subtract), reads=[PB[6], B_mv], writes=[B_mv])
                P.op("dve", lambda e: e.tensor_scalar(out=var[:], in0=var[:], scalar1=EPS, scalar2=None, op0=ALU.add),
                     reads=[B_mv], writes=[B_mv])
                P.op("act", lambda e: e.activation(out=var[:], in_=var[:], func=AF.Sqrt), reads=[B_mv], writes=[B_mv])
                P.op("dve", lambda e: e.reciprocal(out=var[:], in_=var[:]), reads=[B_mv], writes=[B_mv])
                for ch in range(8):
                    si = ch % 2
                    P.op("dve", lambda e, ch=ch, si=si: e.tensor_tensor(out=t1[si][:], in0=cv[:, ch, :], in1=mean[:], op=ALU.subtract),
                         reads=[B_cv[ch], B_mv], writes=[B_t1[si]])
                    P.op("dve", lambda e, si=si: e.tensor_tensor(out=t1[si][:], in0=t1[si][:], in1=var[:], op=ALU.mult),
                         reads=[B_mv], writes=[B_t1[si]])
                    P.op("act", lambda e, ch=ch, si=si: e.activation(out=zT[:, ch, :], in_=t1[si][:], func=AF.Silu,
                                                                    bias=lnb[:, ch:ch + 1], scale=lng[:, ch:ch + 1]),
                         reads=[B_t1[si], B_cp], writes=[B_zT[ch]])
                for j in range(gn):
                    ti = gt0 + j
                    P.op("pool", lambda e, j=j: e.tensor_tensor(out=xg[:, j, :], in0=xg[:, j, :], in1=b2r[:], op=ALU.add),
                         reads=[Bb2], writes=[B_xg[j]])
                    for half in range(2):
                        bank = 3 + half if j % 2 == 0 else 1 + half
                        bank = 7 if half == 0 else 0
                        for c in range(8):
                            P.op("pe", lambda e, c=c, j=j, half=half, bank=bank: e.matmul(
                                out=pb[bank][:], lhsT=zT[:, c, j * 128:(j + 1) * 128], rhs=w2[:, c, half * 512:(half + 1) * 512],
                                start=(c == 0), stop=(c == 7)), reads=[B_zT[c], B_w2], writes=[PB[bank]])
                        P.op("dve", lambda e, j=j, half=half, bank=bank: e.tensor_tensor(
                            out=xg[:, j, half * 512:(half + 1) * 512], in0=pb[bank][:], in1=xg[:, j, half * 512:(half + 1) * 512],
                            op=ALU.add), reads=[PB[bank]], writes=[B_xg[j]])
                    P.dma("sp", lambda e, ti=ti, j=j: e.dma_start(out=X3[(ti - 1) * 128:ti * 128, :], in_=xg[:, j, :]),
                          reads=[B_xg[j]], writes=[B_X3[ti - 1]])
        P.barrier()

        B_Y = [Buf() for _ in range(32)]
        peer_phase(1, X3, B_X3, 32, y, B_Y, 0, True)
        P.emit(final_bufs=B_Y)
    P.close()
    return nc


_CACHE = {}


def host_tables(half):
    valid = np.ones(NB, np.float32) if half == 1 else (np.arange(NB) >= 16).astype(np.float32)
    pm = np.zeros((NG, NB), np.float32)
    a1 = np.zeros((NG, NB), np.float32)
    a2 = np.zeros((NG, NB), np.float32)
    for g in range(NG):
        own = 15 + g
        for n in range(NB):
            past = (n < own) and valid[n] > 0
            pm[g, n] = 0.0 if past else -1e30
            if n <= own - 1:
                a1[g, n] = NEGB
            if n <= own - 2:
                a2[g, n] = 8.0
    a3 = -a1
    return pm.reshape(1, -1), a1.reshape(1, -1), a2.reshape(1, -1), a3.reshape(1, -1)


def kernel(x, rel_bias, norm_mix, norm_ffn, attn_w_qkv, attn_w_o, conv_w_pw1, conv_b_pw1, conv_w_dw, conv_b_dw,
           conv_ln_g, conv_ln_b, conv_w_pw2, conv_b_pw2, peer_w_q, peer_sub_keys, peer_u, peer_v, norm_final, _dbg=False):
    f = lambda a: np.ascontiguousarray(np.asarray(a, dtype=np.float32))
    x = f(x)
    boh = np.zeros((NB, SEQ), np.float32)
    for n in range(NB):
        boh[n, n * BLK:(n + 1) * BLK] = 1.0
    shared = {
        "rel_bias": f(rel_bias), "norm_mix": f(norm_mix), "norm_ffn": f(norm_ffn), "norm_final": f(norm_final).reshape(1, D),
        "w_qkv": f(attn_w_qkv[0]), "w_o": f(attn_w_o[0]), "w_pw1": f(conv_w_pw1[0]),
        "b_pw1": f(np.asarray(conv_b_pw1[0]).reshape(16, 128).T), "w_dwT": f(np.asarray(conv_w_dw[0]).T),
        "b_dw": f(np.asarray(conv_b_dw[0]).reshape(8, 128).T), "ln_g": f(np.asarray(conv_ln_g[0]).reshape(8, 128).T), "ln_b": f(np.asarray(conv_ln_b[0]).reshape(8, 128).T),
        "w_pw2": f(conv_w_pw2[0]), "b_pw2": f(conv_b_pw2[0]).reshape(1, D), "w_pq": f(peer_w_q),
        "skT": f(np.asarray(peer_sub_keys).reshape(2, 16, 128, 128).transpose(0, 1, 3, 2)),
        "uT": f(np.asarray(peer_u).transpose(0, 2, 1)), "pv": f(peer_v),
        "ident": np.eye(128, dtype=np.float32), "boh": boh,
    }
    in_maps = []
    for c in range(8):
        b, half = c // 2, c % 2
        xvv = np.zeros((SEQ, D), np.float32)
        if half == 1:
            xvv[:] = x[b]
        else:
            xvv[4096:] = x[b, :4096]
        pm, a1, a2, a3 = host_tables(half)
        m = dict(shared)
        m.update({"xv": xvv, "pm": pm, "a1": a1, "a2": a2, "a3": a3, "halo": np.full((1, 1), float(half), np.float32)})
        in_maps.append(m)
    key = bool(_dbg)
    if key not in _CACHE:
        _CACHE[key] = build_program(dbg=key)
    nc = _CACHE[key]
    res = run_bass_kernel_spmd(nc, in_maps, core_ids=list(range(8)))
    out = np.zeros((4, SEQ, D), np.float32)
    for c in range(8):
        b, half = c // 2, c % 2
        out[b, half * 4096:(half + 1) * 4096] = res.results[c]["y"]
    if _dbg:
        return out, res.results
    return out
```

```python
import contextlib
import math
import numpy as np
import concourse.bass as bass
import concourse.mybir as mybir
from concourse.bass_utils import run_bass_kernel_spmd

F32 = mybir.dt.float32
BF16 = mybir.dt.bfloat16
ALU = mybir.AluOpType
AF = mybir.ActivationFunctionType
AX = mybir.AxisListType

D = 1024
NH = 16
HD = 64
SEQ = 8192
BLK = 256
NB = 32
NQT = 34
NG = 17
NT0 = 33
CW = 31
NEGB = 240000.0
EPS = 1e-6

COMPUTE = ("pe", "act", "dve", "pool")
STREAM_OF = {"pe": "tensor", "act": "scalar", "dve": "vector", "pool": "gpsimd",
             "sp": "sync", "actq": "scalar", "poolq": "gpsimd"}


class Buf:
    __slots__ = ("name", "w", "r")

    def __init__(self, name=""):
        self.name = name
        self.w = None
        self.r = {}


class Prog:
    def __init__(self, nc):
        self.nc = nc
        self.streams = {s: [] for s in ("tensor", "scalar", "vector", "gpsimd", "sync")}
        self.cnt = {e: 0 for e in COMPUTE}
        self.sems = {}
        self._ctx = []
        self.dpool = {}
        self.drr = {}
        self.DPOOL = {"sp": 28, "actq": 6, "poolq": 12}
        self.nsem = 0
        for e in COMPUTE:
            self.sems[e] = self._sem("c_" + e)

    def _sem(self, name):
        cm = self.nc.semaphore(name)
        s = cm.__enter__()
        self._ctx.append(cm)
        return s

    def _deps(self, eng, reads, writes):
        waits = {}

        def need(tok):
            if tok is None:
                return
            kind, key, val = tok
            if kind == "c" and key == eng and eng == "pe":
                return
            k = (kind, key)
            if k not in waits or waits[k][2] < val:
                waits[k] = tok

        for b in reads:
            need(b.w)
        for b in writes:
            need(b.w)
            for t in b.r.values():
                need(t)
        return list(waits.values())

    def op(self, eng, fn, reads=(), writes=()):
        waits = self._deps(eng, reads, writes)
        self.cnt[eng] += 1
        tok = ("c", eng, self.cnt[eng])
        self.streams[STREAM_OF[eng]].append((waits, fn, ("c", eng)))
        for b in reads:
            b.r[("c", eng)] = tok
        for b in writes:
            b.w = tok
            b.r = {}
        return tok

    def dma(self, q, fn, reads=(), writes=()):
        waits = self._deps(q, reads, writes)
        pool = self.dpool.setdefault(q, [])
        if len(pool) < self.DPOOL[q]:
            pool.append([self._sem("d_%s%d" % (q, len(pool))), 0])
            idx = len(pool) - 1
        else:
            idx = self.drr.get(q, 0) % len(pool)
        self.drr[q] = idx + 1
        key = (q, idx)
        if pool[idx][1] > 0:
            waits = [t for t in waits if (t[0], t[1]) != ("d", key)] + [("d", key, pool[idx][1])]
        pool[idx][1] += 16
        tok = ("d", key, pool[idx][1])
        self.streams[STREAM_OF[q]].append((waits, fn, ("d", key)))
        for b in reads:
            b.r[("d", key)] = tok
        for b in writes:
            b.w = tok
            b.r = {}
        return tok

    def barrier(self):
        toks = [("c", e, self.cnt[e]) for e in COMPUTE if self.cnt[e] > 0]
        for q, pool in self.dpool.items():
            for idx, (s, v) in enumerate(pool):
                if v > 0:
                    toks.append(("d", (q, idx), v))
        for s in self.streams:
            self.streams[s].append((list(toks), None, None))

    def emit(self, final_bufs=()):
        nc = self.nc
        finals = []
        for b in final_bufs:
            if b.w is not None:
                finals.append(b.w)
            finals.extend(b.r.values())

        def semof(tok):
            kind, key, val = tok
            return (self.sems[key] if kind == "c" else self.dpool[key[0]][key[1]][0]), val

        def replay(handle, items, extra=()):
            seen = {}
            for waits, fn, me in list(items) + [(list(extra), None, None)]:
                for t in waits:
                    k = (t[0], t[1])
                    if seen.get(k, 0) >= t[2]:
                        continue
                    seen[k] = t[2]
                    s, v = semof(t)
                    handle.wait_ge(s, v)
                if fn is None:
                    continue
                ins = fn(handle)
                if me[0] == "c":
                    ins.then_inc(self.sems[me[1]], 1)
                    seen[me] = max(seen.get(me, 0), 0)
                else:
                    ins.then_inc(self.dpool[me[1][0]][me[1][1]][0], 16)

        streams = self.streams
        with nc.Block() as block:
            @block.tensor
            def _(e):
                replay(e, streams["tensor"])

            @block.scalar
            def _(e):
                replay(e, streams["scalar"])

            @block.vector
            def _(e):
                replay(e, streams["vector"])

            @block.gpsimd
            def _(e):
                replay(e, streams["gpsimd"])

            @block.sync
            def _(e):
                replay(e, streams["sync"], finals)

    def close(self):
        for cm in reversed(self._ctx):
            cm.__exit__(None, None, None)


def t5_runs():
    d = np.arange(768, dtype=np.int32)
    df = np.maximum(d, 1).astype(np.float32)
    large = 16 + (np.log(df / np.float32(16)) / np.float32(math.log(128 / 16)) * np.float32(16)).astype(np.int32)
    large = np.minimum(large, 31)
    return np.where(d < 16, d, large)


def build_program(dbg=False):
    nc = bass.Bass("TRN2", target_bir_lowering=False)
    P = Prog(nc)

    def din(name, shape, dt=F32):
        return nc.dram_tensor(name, list(shape), dt, kind="ExternalInput").ap()

    def dscr(name, shape, dt, out=False):
        return nc.dram_tensor(name, list(shape), dt, kind=("ExternalOutput" if out else "Internal")).ap()

    xv = din("xv", [SEQ, D])
    rel_bias = din("rel_bias", [NH, 32])
    norm_mix = din("norm_mix", [2, D])
    norm_ffn = din("norm_ffn", [2, D])
    norm_final = din("norm_final", [1, D])
    w_qkv = din("w_qkv", [D, 3 * D])
    w_o = din("w_o", [D, D])
    w_pw1 = din("w_pw1", [D, 2 * D])
    b_pw1 = din("b_pw1", [128, 16])
    w_dwT = din("w_dwT", [D, CW])
    b_dw = din("b_dw", [128, 8])
    ln_g = din("ln_g", [128, 8])
    ln_b = din("ln_b", [128, 8])
    w_pw2 = din("w_pw2", [D, D])
    b_pw2 = din("b_pw2", [1, D])
    w_pq = din("w_pq", [2, D, 2048])
    skT = din("skT", [2, 16, 128, 128])
    uT = din("uT", [2, D, 16384])
    pv = din("pv", [2, 16384, D])
    ident_in = din("ident", [128, 128])
    boh_in = din("boh", [NB, SEQ])
    pm_in = din("pm", [1, NG * NB])
    a1_in = din("a1", [1, NG * NB])
    a2_in = din("a2", [1, NG * NB])
    a3_in = din("a3", [1, NG * NB])
    halo_in = din("halo", [1, 1])
    y = dscr("y", [4096, D], F32, out=True)

    wqkv_b = dscr("wqkv_b", [D, 3 * D], BF16)
    wo_b = dscr("wo_b", [D, D], BF16)
    wpw1_b = dscr("wpw1_b", [D, 2 * D], BF16)
    wpw2_b = dscr("wpw2_b", [D, D], BF16)
    wpq_b = dscr("wpq_b", [2, D, 2048], BF16)
    skT_b = dscr("skT_b", [2, 16, 128, 128], BF16)
    uT_b = dscr("uT_b", [2, D, 16384], BF16)
    pv_b = dscr("pv_b", [2, 16384, D], BF16)
    boh_b = dscr("boh_b", [NB, SEQ], BF16)
    KT = dscr("KT", [NH, HD, SEQ], BF16)
    QT = dscr("QT", [NH, HD, NQT * 128], BF16)
    VS = dscr("VS", [SEQ, D], BF16)
    OA = dscr("OA", [NQT * 128, D], F32, out=dbg)
    X1 = dscr("X1", [NT0 * 128, D], F32, out=dbg)
    X2 = dscr("X2", [NT0 * 128, D], F32, out=dbg)
    X3 = dscr("X3", [4096, D], F32, out=dbg)
    TT = dscr("TT", [NH, 128, 1024], F32)

    es = contextlib.ExitStack()
    with es:
        def sb(name, shape, dt, st=es):
            return st.enter_context(nc.sbuf_tensor(name, list(shape), dt))

        pb = [es.enter_context(nc.psum_tensor("pb%d" % i, [128, 512], F32)) for i in range(8)]
        PB = [Buf("pb%d" % i) for i in range(8)]

        idf = sb("idf", [128, 128], F32)
        idb = sb("idb", [128, 128], BF16)
        B_id = Buf("id")
        P.dma("sp", lambda e: e.dma_start(out=idf[:], in_=ident_in), writes=[B_id])
        P.op("dve", lambda e: e.tensor_copy(out=idb[:], in_=idf[:]), reads=[B_id], writes=[B_id])

        B_w = {k: Buf(k) for k in ("wqkv", "wo", "wpw1", "wpw2", "wpq", "skT", "boh")}
        B_uT = [[Buf() for _ in range(16)] for _ in range(2)]
        B_pv = [[Buf() for _ in range(16)] for _ in range(2)]

        def cast(dst, src, buf, rows=None):
            n = dst.shape[0]
            step = rows or n
            for r0 in range(0, n, step):
                P.dma("poolq", lambda e, r0=r0: e.dma_start(out=dst[r0:r0 + step], in_=src[r0:r0 + step]), writes=[buf])

        cast(wqkv_b, w_qkv, B_w["wqkv"], 256)
        cast(boh_b, boh_in, B_w["boh"])
        cast(wo_b, w_o, B_w["wo"], 512)
        for l in range(2):
            cast(wpq_b[l], w_pq[l], B_w["wpq"], 512)
            cast(skT_b[l].rearrange("a b c -> (a b) c"), skT[l].rearrange("a b c -> (a b) c"), B_w["skT"])
        cast(wpw1_b, w_pw1, B_w["wpw1"], 512)
        cast(wpw2_b, w_pw2, B_w["wpw2"], 512)
        for l in range(2):
            for eg in range(16):
                P.dma("poolq", lambda e, l=l, eg=eg: e.dma_start(
                    out=uT_b[l][:, eg * 1024:(eg + 1) * 1024], in_=uT[l][:, eg * 1024:(eg + 1) * 1024]),
                    writes=[B_uT[l][eg]])
                P.dma("poolq", lambda e, l=l, eg=eg: e.dma_start(
                    out=pv_b[l][eg * 1024:(eg + 1) * 1024, :], in_=pv[l][eg * 1024:(eg + 1) * 1024, :]),
                    writes=[B_pv[l][eg]])

        def norm_T(st, pfx, x_sb, Bx, g_rep, Bg, hT_dst, B_hT, tpbank, nbufs):
            sq, ss, rs, hb, Bs = nbufs
            P.op("act", lambda e: e.activation(out=sq[:], in_=x_sb, func=AF.Square, accum_out=ss[:]),
                 reads=[Bx], writes=[Bs])
            P.op("dve", lambda e: e.tensor_scalar(out=rs[:], in0=ss[:], scalar1=1.0 / D, scalar2=EPS,
                                                  op0=ALU.mult, op1=ALU.add), reads=[Bs], writes=[Bs])
            P.op("act", lambda e: e.activation(out=rs[:], in_=rs[:], func=AF.Sqrt), reads=[Bs], writes=[Bs])
            P.op("dve", lambda e: e.reciprocal(out=rs[:], in_=rs[:]), reads=[Bs], writes=[Bs])
            P.op("dve", lambda e: e.scalar_tensor_tensor(out=hb[:], in0=x_sb, scalar=rs[:, 0:1], in1=g_rep,
                                                         op0=ALU.mult, op1=ALU.mult),
                 reads=[Bx, Bs, Bg], writes=[Bs])
            tpv = pb[tpbank][:].bitcast(BF16)
            for c in range(8):
                P.op("pe", lambda e, c=c: e.transpose(out=tpv[:, c * 128:(c + 1) * 128],
                                                      in_=hb[:, c * 128:(c + 1) * 128], identity=idb[:]),
                     reads=[Bs, B_id], writes=[PB[tpbank]])
            P.op("act", lambda e: e.copy(out=hT_dst, in_=tpv.rearrange("p (c t) -> p c t", c=8)),
                 reads=[PB[tpbank]], writes=[B_hT])

        def norm_bufs(st, pfx):
            return (sb(pfx + "sq", [128, D], F32, st), sb(pfx + "ss", [128, 1], F32, st),
                    sb(pfx + "rs", [128, 1], F32, st), sb(pfx + "hb", [128, D], BF16, st), Buf(pfx + "nb"))

        def load_rep(st, name, src_row, n=D):
            t = sb(name, [128, n], F32, st)
            b = Buf(name)
            P.dma("sp", lambda e: e.dma_start(out=t[:], in_=src_row.to_broadcast([128, n])), writes=[b])
            return t, b

        with contextlib.ExitStack() as st:
            wq = sb("wqkv_sb", [128, 8, 3 * D], BF16, st)
            B_wq = Buf()
            for c in range(8):
                P.dma("sp", lambda e, c=c: e.dma_start(out=wq[:, c, :], in_=wqkv_b[c * 128:(c + 1) * 128, :]),
                      reads=[B_w["wqkv"]], writes=[B_wq])
            g0, Bg0 = load_rep(st, "g0", norm_mix[0:1, :])
            nb = [norm_bufs(st, "p1n%d" % i) for i in range(2)]
            xts = [sb("p1x%d" % i, [128, D], F32, st) for i in range(2)]
            Bxs = [Buf() for _ in range(2)]
            hTs = [sb("p1hT%d" % i, [128, 8, 128], BF16, st) for i in range(2)]
            BhT = [Buf() for _ in range(2)]
            kts = [sb("p1kt%d" % i, [128, 8, 128], BF16, st) for i in range(2)]
            Bkt = [Buf() for _ in range(2)]
            qts = [sb("p1qt%d" % i, [128, 8, 128], BF16, st) for i in range(2)]
            Bqt = [Buf() for _ in range(2)]
            vts = [sb("p1vt%d" % i, [128, D], BF16, st) for i in range(2)]
            Bvt = [Buf() for _ in range(2)]
            B_KT = [Buf() for _ in range(64)]
            B_QT = [Buf() for _ in range(NQT)]
            B_VS = [Buf() for _ in range(64)]
            KTv = KT.rearrange("(hp two) d t -> (two d) hp t", two=2)
            QTv = QT.rearrange("(hp two) d t -> (two d) hp t", two=2)
            for vt in range(64):
                i = vt % 2
                P.dma("sp", lambda e, vt=vt, i=i: e.dma_start(out=xts[i][:], in_=xv[vt * 128:(vt + 1) * 128, :]),
                      writes=[Bxs[i]])
                norm_T(st, "p1", xts[i][:], Bxs[i], g0[:], Bg0, hTs[i][:], BhT[i], 0, nb[i])
                for (woff, dst, Bdst, bank0, do) in ((D, kts[i], Bkt[i], 1, True), (0, qts[i], Bqt[i], 3, vt >= 30)):
                    if not do:
                        continue
                    for hp in range(8):
                        bank = bank0 + hp // 4
                        for c in range(8):
                            P.op("pe", lambda e, hp=hp, c=c, bank=bank, woff=woff, i=i: e.matmul(
                                out=pb[bank][:, (hp % 4) * 128:(hp % 4 + 1) * 128],
                                lhsT=wq[:, c, woff + hp * 128: woff + (hp + 1) * 128], rhs=hTs[i][:, c, :],
                                start=(c == 0), stop=(c == 7)), reads=[B_wq, BhT[i]], writes=[PB[bank]])
                    for half in range(2):
                        P.op("dve" if half == 0 else "act",
                             (lambda e, half=half, dst=dst, bank0=bank0: e.tensor_copy(
                                 out=dst[:, half * 4:(half + 1) * 4, :],
                                 in_=pb[bank0 + half][:].rearrange("p (a t) -> p a t", a=4))) if half == 0 else
                             (lambda e, half=half, dst=dst, bank0=bank0: e.copy(
                                 out=dst[:, half * 4:(half + 1) * 4, :],
                                 in_=pb[bank0 + half][:].rearrange("p (a t) -> p a t", a=4))),
                             reads=[PB[bank0 + half]], writes=[Bdst])
                P.dma("sp", lambda e, vt=vt, i=i: e.dma_start(out=KTv[:, :, vt * 128:(vt + 1) * 128], in_=kts[i][:]),
                      reads=[Bkt[i]], writes=[B_KT[vt]])
                if vt >= 30:
                    qi = vt - 30
                    P.dma("sp", lambda e, qi=qi, i=i: e.dma_start(out=QTv[:, :, qi * 128:(qi + 1) * 128], in_=qts[i][:]),
                          reads=[Bqt[i]], writes=[B_QT[qi]])
                for half in range(2):
                    bank = 5 + half
                    for c in range(8):
                        P.op("pe", lambda e, half=half, c=c, bank=bank, i=i: e.matmul(
                            out=pb[bank][:], lhsT=hTs[i][:, c, :],
                            rhs=wq[:, c, 2 * D + half * 512: 2 * D + (half + 1) * 512],
                            start=(c == 0), stop=(c == 7)), reads=[B_wq, BhT[i]], writes=[PB[bank]])
                    P.op("dve" if half == 0 else "act",
                         (lambda e, half=half, bank=bank, i=i: e.tensor_copy(out=vts[i][:, half * 512:(half + 1) * 512], in_=pb[bank][:]))
                         if half == 0 else
                         (lambda e, half=half, bank=bank, i=i: e.copy(out=vts[i][:, half * 512:(half + 1) * 512], in_=pb[bank][:])),
                         reads=[PB[bank]], writes=[Bvt[i]])
                P.dma("sp", lambda e, vt=vt, i=i: e.dma_start(out=VS[vt * 128:(vt + 1) * 128, :], in_=vts[i][:]),
                      reads=[Bvt[i]], writes=[B_VS[vt]])
        P.barrier()

        bk = t5_runs()
        with contextlib.ExitStack() as st:
            rb = sb("rb", [NH, 32], F32, st)
            tt = sb("tt", [NH, 1024], F32, st)
            B_tt = Buf()
            P.dma("sp", lambda e: e.dma_start(out=rb[:], in_=rel_bias), writes=[B_tt])
            P.op("dve", lambda e: e.memset(tt[:, 0:256], -NEGB / 8.0), reads=[B_tt], writes=[B_tt])
            P.op("dve", lambda e: e.tensor_copy(out=tt[:, 256:272], in_=rb[:, 0:16]), reads=[B_tt], writes=[B_tt])
            dlt = 16
            while dlt < 768:
                b_ = int(bk[dlt])
                e_ = dlt
                while e_ < 768 and int(bk[e_]) == b_:
                    e_ += 1
                P.op("dve", lambda e, dlt=dlt, e_=e_, b_=b_: e.tensor_copy(
                    out=tt[:, 256 + dlt:256 + e_], in_=rb[:, b_:b_ + 1].to_broadcast([NH, e_ - dlt])),
                    reads=[B_tt], writes=[B_tt])
                dlt = e_
            P.op("dve", lambda e: e.tensor_scalar(out=tt[:], in0=tt[:], scalar1=8.0, scalar2=None, op0=ALU.mult),
                 reads=[B_tt], writes=[B_tt])
            B_TT = Buf()
            P.dma("sp", lambda e: e.dma_start(out=TT, in_=tt[:].unsqueeze(1).to_broadcast([NH, 128, 1024])), reads=[B_tt], writes=[B_TT])

            kaug2 = [sb("kaug%d" % i, [96, SEQ], BF16, st) for i in range(2)]
            B_ka2 = [Buf() for _ in range(2)]
            B_boh = Buf()
            for i_ in range(2):
                P.dma("sp", lambda e, i_=i_: e.dma_start(out=kaug2[i_][64:96, :], in_=boh_b), reads=[B_w["boh"]], writes=[B_boh])
            vaug2 = [sb("vaug%d" % i, [128, 64, 65], BF16, st) for i in range(2)]
            B_va2 = [Buf() for _ in range(2)]
            for i_ in range(2):
                P.op("pool", lambda e, i_=i_: e.memset(vaug2[i_][:], 1.0), writes=[B_va2[i_]])
            qaug2 = [sb("qaug%d" % i, [96, NQT * 128], BF16, st) for i in range(2)]
            B_qa2 = [Buf() for _ in range(2)]
            B_qm2 = [[Buf() for _ in range(NG)] for _ in range(2)]
            btab2 = [sb("btab%d" % i, [128, 2, 2, 256], F32, st) for i in range(2)]
            B_bt2 = [Buf() for _ in range(2)]
            km = sb("km", [64, NB], F32, st)
            kmb = sb("kmb", [64, NB], BF16, st)
            B_km = Buf()
            pmr, B_pm = load_rep(st, "pmr", pm_in, NG * NB)
            a1r, B_a1 = load_rep(st, "a1r", a1_in, NG * NB)
            a2r, B_a2 = load_rep(st, "a2r", a2_in, NG * NB)
            a3r, B_a3 = load_rep(st, "a3r", a3_in, NG * NB)
            rb31 = sb("rb31", [128, NH], F32, st)
            B_rb31 = Buf()
            P.dma("sp", lambda e: e.dma_start(out=rb31[:], in_=rel_bias[:, 31:32].rearrange("h o -> o h").to_broadcast([128, NH]), allow_slow_non_contiguous=True),
                  writes=[B_rb31])
            mulv = sb("mulv", [128, NG * NB], F32, st)
            B_mulv = Buf()
            gm = [sb("gm%d" % i, [128, 2, NB], F32, st) for i in range(2)]
            mx = [sb("mx%d" % i, [128, 2, 8], F32, st) for i in range(2)]
            sel = [sb("sel%d" % i, [128, 2, 96], F32, st) for i in range(2)]
            B_g = [Buf() for _ in range(2)]
            for i_ in range(2):
                P.op("dve", lambda e, i_=i_: e.memset(sel[i_][:], 0.0), writes=[B_g[i_]])
            stmp = [sb("stmp%d" % i, [128, 512], F32, st) for i in range(2)]
            B_stmp = [Buf() for _ in range(2)]
            pT = [sb("pT%d" % i, [128, 512], BF16, st) for i in range(3)]
            B_pT = [Buf() for _ in range(3)]
            oall = sb("oall", [128, NQT, HD], F32, st)
            B_oall = Buf()
            rec = [sb("rec%d" % i, [128, 2, 1], F32, st) for i in range(2)]
            B_rec = [Buf() for _ in range(2)]
            B_OA = Buf()
            OAv = OA.rearrange("(t p) (h d) -> p t h d", p=128, h=NH)
            VSv = VS.rearrange("(c p) (h d) -> p c h d", p=128, h=NH)
            def emit_loads(h):
                p_ = h % 2
                P.dma("sp", lambda e, h=h, p_=p_: e.dma_start(out=kaug2[p_][0:64, :], in_=KT[h]), reads=B_KT, writes=[B_ka2[p_]])
                for cq in range(4):
                    P.dma("sp", lambda e, h=h, cq=cq, p_=p_: e.dma_start(out=vaug2[p_][:, cq * 16:(cq + 1) * 16, 0:64],
                                                                         in_=VSv[:, cq * 16:(cq + 1) * 16, h, :]),
                          reads=B_VS, writes=[B_va2[p_]])
                P.dma("sp", lambda e, h=h, p_=p_: e.dma_start(out=qaug2[p_][0:64, :], in_=QT[h]), reads=B_QT, writes=[B_qa2[p_]])
                for which in range(2):
                    for kc in range(2):
                        def bsrc(h=h, which=which, kc=kc):
                            base = TT[h, 0:1, 256 * (1 + which) - kc * 128: 256 * (1 + which) - kc * 128 + 256]
                            return bass.AP(tensor=base.tensor, offset=base.offset, ap=[[1023, 128], [1, 256]])
                        P.dma("sp", lambda e, which=which, kc=kc, bsrc=bsrc, p_=p_: e.dma_start(out=btab2[p_][:, which, kc, :], in_=bsrc()),
                              reads=[B_TT], writes=[B_bt2[p_]])

            def head_body(h, kaug, vaug, qaug, btab, B_ka, B_va, B_qa, B_qm, B_bt):
                P.op("dve", lambda e: e.tensor_reduce(out=km[:, :], in_=kaug[0:64, :].rearrange("p (n k) -> p n k", k=BLK),
                                                      axis=AX.X, op=ALU.add), reads=[B_ka], writes=[B_km])
                P.op("dve", lambda e: e.tensor_scalar(out=kmb[:, :], in0=km[:, :], scalar1=1.0 / BLK, scalar2=None,
                                                      op0=ALU.mult), reads=[B_km], writes=[B_km])
                P.op("dve", lambda e, h=h: e.scalar_tensor_tensor(out=mulv[:], in0=a2r[:], scalar=rb31[:, h:h + 1], in1=a1r[:],
                                                                  op0=ALU.mult, op1=ALU.add),
                     reads=[B_a1, B_a2, B_rb31], writes=[B_mulv])
                def prologue(g, h=h):
                    own = 15 + g
                    gi = g % 2
                    for t in range(2):
                        P.op("pe", lambda e, g=g, t=t: e.matmul(
                            out=pb[7][:, t * NB:(t + 1) * NB], lhsT=qaug[0:64, (2 * g + t) * 128:(2 * g + t + 1) * 128],
                            rhs=kmb[:, :], start=True, stop=True), reads=[B_qa, B_km], writes=[PB[7]])
                    P.op("dve", lambda e, g=g, gi=gi: e.tensor_tensor(
                        out=gm[gi][:], in0=pb[7][:, 0:2 * NB].rearrange("p (t n) -> p t n", t=2),
                        in1=pmr[:, g * NB:(g + 1) * NB].unsqueeze(1).to_broadcast([128, 2, NB]), op=ALU.add),
                        reads=[PB[7], B_pm], writes=[B_g[gi]])
                    for t in range(2):
                        P.op("dve", lambda e, gi=gi, t=t: e.max(out=mx[gi][:, t, :], in_=gm[gi][:, t, :]),
                             reads=[B_g[gi]], writes=[B_g[gi]])
                    for t in range(2):
                        P.op("dve", lambda e, gi=gi, t=t: e.tensor_scalar(
                            out=sel[gi][:, t, 64:96], in0=gm[gi][:, t, :], scalar1=mx[gi][:, t, 2:3], scalar2=None, op0=ALU.is_ge),
                            reads=[B_g[gi]], writes=[B_g[gi]])
                    P.op("dve", lambda e, gi=gi: e.scalar_tensor_tensor(
                        out=sel[gi][:, :, 64:96], in0=gm[gi][:], scalar=-1e29, in1=sel[gi][:, :, 64:96], op0=ALU.is_gt, op1=ALU.mult),
                        reads=[B_g[gi]], writes=[B_g[gi]])
                    P.op("dve", lambda e, gi=gi, g=g: e.tensor_tensor(
                        out=sel[gi][:, :, 64:96], in0=sel[gi][:, :, 64:96],
                        in1=mulv[:, g * NB:(g + 1) * NB].unsqueeze(1).to_broadcast([128, 2, NB]), op=ALU.mult),
                        reads=[B_g[gi], B_mulv], writes=[B_g[gi]])
                    P.op("dve", lambda e, gi=gi, g=g: e.tensor_tensor(
                        out=sel[gi][:, :, 64:96], in0=sel[gi][:, :, 64:96],
                        in1=a3r[:, g * NB:(g + 1) * NB].unsqueeze(1).to_broadcast([128, 2, NB]), op=ALU.add),
                        reads=[B_g[gi], B_a3], writes=[B_g[gi]])

                def prologue2(g):
                    gi = g % 2
                    for t in range(2):
                        P.op("pe", lambda e, gi=gi, t=t: e.transpose(out=pb[7][0:96, 128 + t * 128:128 + (t + 1) * 128],
                                                                     in_=sel[gi][:, t, :], identity=idf[:]),
                             reads=[B_g[gi], B_id], writes=[PB[7]])
                    P.op("act", lambda e, g=g: e.copy(out=qaug[64:96, g * 256:(g + 1) * 256], in_=pb[7][64:96, 128:384]),
                         reads=[PB[7]], writes=[B_qm[g]])

                iters = [(g, n) for g in range(NG) for n in range(15 + g + 1)]

                def stageA(idx):
                    g, n = iters[idx]
                    own = 15 + g
                    sbk = idx % 3
                    pi = idx % 3
                    for c in range(2):
                        P.op("pe", lambda e, n=n, c=c, g=g, sbk=sbk: e.matmul(
                            out=pb[sbk][:, c * 256:(c + 1) * 256], lhsT=kaug[0:96, (2 * n + c) * 128:(2 * n + c + 1) * 128],
                            rhs=qaug[0:96, g * 256:(g + 1) * 256], start=True, stop=True),
                            reads=[B_ka, B_boh, B_qa, B_qm[g]], writes=[PB[sbk]])
                    if n >= own - 1:
                        which = 0 if n == own else 1
                        si = n % 2
                        P.op("dve", lambda e, sbk=sbk, which=which, si=si: e.tensor_tensor(
                            out=stmp[si][:], in0=pb[sbk][:], in1=btab[:, which, :, :].rearrange("p c q -> p (c q)"), op=ALU.add),
                            reads=[PB[sbk], B_bt], writes=[B_stmp[si]])
                        P.op("act", lambda e, si=si, pi=pi: e.activation(out=pT[pi][:], in_=stmp[si][:], func=AF.Exp, scale=0.125),
                             reads=[B_stmp[si]], writes=[B_pT[pi]])
                    else:
                        P.op("act", lambda e, sbk=sbk, pi=pi: e.activation(out=pT[pi][:], in_=pb[sbk][:], func=AF.Exp, scale=0.125),
                             reads=[PB[sbk]], writes=[B_pT[pi]])

                def stageB(idx):
                    g, n = iters[idx]
                    own = 15 + g
                    gi = g % 2
                    pi = idx % 3
                    ob = 3 + 2 * (g % 2)
                    for c in range(2):
                        for t in range(2):
                            P.op("pe", lambda e, n=n, c=c, t=t, pi=pi, ob=ob, own=own: e.matmul(
                                out=pb[ob + t][:, 0:65], lhsT=pT[pi][:, c * 256 + t * 128:c * 256 + (t + 1) * 128],
                                rhs=vaug[:, 2 * n + c, :], start=(n == 0 and c == 0), stop=(n == own and c == 1)),
                                reads=[B_pT[pi], B_va], writes=[PB[ob + t]])
                    if n == own:
                        for t in range(2):
                            P.op("dve", lambda e, ob=ob, gi=gi, t=t: e.reciprocal(out=rec[gi][:, t, :], in_=pb[ob + t][:, 64:65]),
                                 reads=[PB[ob + t]], writes=[B_rec[gi]])
                            P.op("dve", lambda e, ob=ob, gi=gi, g=g, t=t: e.tensor_scalar(
                                out=oall[:, 2 * g + t, :], in0=pb[ob + t][:, 0:64], scalar1=rec[gi][:, t, 0:1], scalar2=None, op0=ALU.mult),
                                reads=[PB[ob + t], B_rec[gi]], writes=[B_oall])

                prologue(0)
                prologue2(0)
                stageA(0)
                for idx in range(len(iters)):
                    if idx + 1 < len(iters):
                        stageA(idx + 1)
                    g_, n_ = iters[idx]
                    if n_ == 0 and g_ + 1 < NG:
                        prologue(g_ + 1)
                    if n_ == 8 and g_ + 1 < NG:
                        prologue2(g_ + 1)
                    stageB(idx)
                P.dma("sp", lambda e, h=h: e.dma_start(out=OAv[:, :, h, :], in_=oall[:]), reads=[B_oall], writes=[B_OA])

            emit_loads(0)
            for h in range(NH):
                p_ = h % 2
                if h + 1 < NH:
                    emit_loads(h + 1)
                head_body(h, kaug2[p_], vaug2[p_], qaug2[p_], btab2[p_], B_ka2[p_], B_va2[p_], B_qa2[p_], B_qm2[p_], B_bt2[p_])
        P.barrier()

        B_X1 = [Buf() for _ in range(NT0)]
        with contextlib.ExitStack() as st:
            wo = sb("wo_sb", [128, 8, D], BF16, st)
            B_wo = Buf()
            P.dma("sp", lambda e: e.dma_start(out=wo[:], in_=wo_b.rearrange("(c p) n -> p c n", p=128)),
                  reads=[B_w["wo"]], writes=[B_wo])
            ot = [sb("p3o%d" % i, [128, D], F32, st) for i in range(2)]
            obf = [sb("p3ob%d" % i, [128, D], BF16, st) for i in range(2)]
            oT = [sb("p3oT%d" % i, [128, 8, 128], BF16, st) for i in range(2)]
            xt3 = [sb("p3x%d" % i, [128, D], F32, st) for i in range(2)]
            B3 = [[Buf() for _ in range(4)] for _ in range(2)]
            for ti in range(NT0):
                i = ti % 2
                Bo, Bob, BoT, Bx = B3[i]
                P.dma("sp", lambda e, ti=ti, i=i: e.dma_start(out=ot[i][:], in_=OA[(ti + 1) * 128:(ti + 2) * 128, :]),
                      reads=[B_OA], writes=[Bo])
                P.dma("sp", lambda e, ti=ti, i=i: e.dma_start(out=xt3[i][:], in_=xv[(31 + ti) * 128:(32 + ti) * 128, :]),
                      writes=[Bx])
                P.op("act", lambda e, i=i: e.copy(out=obf[i][:], in_=ot[i][:]), reads=[Bo], writes=[Bob])
                tpv = pb[0][:].bitcast(BF16)
                for c in range(8):
                    P.op("pe", lambda e, c=c, i=i, tpv=tpv: e.transpose(out=tpv[:, c * 128:(c + 1) * 128],
                                                                       in_=obf[i][:, c * 128:(c + 1) * 128], identity=idb[:]),
                         reads=[Bob, B_id], writes=[PB[0]])
                P.op("act", lambda e, i=i, tpv=tpv: e.copy(out=oT[i][:], in_=tpv.rearrange("p (c t) -> p c t", c=8)),
                     reads=[PB[0]], writes=[BoT])
                for half in range(2):
                    bank = 1 + 2 * i + half
                    for c in range(8):
                        P.op("pe", lambda e, c=c, i=i, half=half, bank=bank: e.matmul(
                            out=pb[bank][:], lhsT=oT[i][:, c, :], rhs=wo[:, c, half * 512:(half + 1) * 512],
                            start=(c == 0), stop=(c == 7)), reads=[BoT, B_wo], writes=[PB[bank]])
                    P.op("dve", lambda e, i=i, half=half, bank=bank: e.tensor_tensor(
                        out=xt3[i][:, half * 512:(half + 1) * 512], in0=pb[bank][:], in1=xt3[i][:, half * 512:(half + 1) * 512],
                        op=ALU.add), reads=[PB[bank], Bx], writes=[Bx])
                P.dma("sp", lambda e, ti=ti, i=i: e.dma_start(out=X1[ti * 128:(ti + 1) * 128, :], in_=xt3[i][:]),
                      reads=[Bx], writes=[B_X1[ti]])
        P.barrier()

        def peer_phase(l, Xin, B_in, ntiles, Xout, B_out, out_row0, final_norm):
            groups = []
            t0 = 0
            if ntiles % 4 == 1:
                groups.append((0, 1))
                t0 = 1
            while t0 < ntiles:
                groups.append((t0, 4))
                t0 += 4
            with contextlib.ExitStack() as st:
                gf, Bgf = load_rep(st, "pgf%d" % l, norm_ffn[l:l + 1, :])
                if final_norm:
                    gfin, Bgfin = load_rep(st, "pgfin", norm_final[0:1, :])
                skt = sb("skt%d" % l, [128, 16, 128], BF16, st)
                B_skt = Buf()
                P.dma("sp", lambda e: e.dma_start(out=skt[:], in_=skT_b[l].rearrange("a k n -> k a n")),
                      reads=[B_w["skT"]], writes=[B_skt])
                hT = sb("phT%d" % l, [128, 8, 512], BF16, st)
                B_hT = Buf()
                s_sb = sb("ps%d" % l, [128, 4, 16, 128], F32, st)
                B_s = [Buf() for _ in range(4)]
                acc = sb("pacc%d" % l, [128, 4, D], F32, st)
                B_acc = [Buf() for _ in range(4)]
                th = sb("pth%d" % l, [128, 4, 8], F32, st)
                eb = sb("peb%d" % l, [128, 4, 8], F32, st)
                B_st = [Buf() for _ in range(4)]
                for (gt0, gn) in groups:
                    Tg = gn * 128
                    with contextlib.ExitStack() as sa:
                        wpq = sb("wpq%d_%d" % (l, gt0), [128, 8, 2048], BF16, sa)
                        B_wpq = Buf()
                        for c in range(8):
                            P.dma("sp", lambda e, c=c: e.dma_start(out=wpq[:, c, :], in_=wpq_b[l][c * 128:(c + 1) * 128, :]),
                                  reads=[B_w["wpq"]], writes=[B_wpq])
                        nbp = norm_bufs(sa, "pn%d_%d" % (l, gt0))
                        qT = sb("pqT%d_%d" % (l, gt0), [128, 16, 512], BF16, sa)
                        B_qT = Buf()
                        v16 = sb("pv16%d_%d" % (l, gt0), [128, 2, 16], F32, sa)
                        t128 = sb("pt128%d_%d" % (l, gt0), [128, 128], F32, sa)
                        cand = sb("pcand%d_%d" % (l, gt0), [128, 256], F32, sa)
                        cand2 = sb("pcand2%d_%d" % (l, gt0), [128, 256], F32, sa)
                        b16 = sb("pb16%d_%d" % (l, gt0), [128, 8, 16], F32, sa)
                        e16 = sb("pe16%d_%d" % (l, gt0), [128, 8, 16], F32, sa)
                        zz = sb("pzz%d_%d" % (l, gt0), [128, 8], F32, sa)
                        B_tk = Buf()
                        for j in range(gn):
                            ti = gt0 + j
                            P.dma("sp", lambda e, ti=ti, j=j: e.dma_start(out=acc[:, j, :], in_=Xin[ti * 128:(ti + 1) * 128, :]),
                                  reads=[B_in[ti]], writes=[B_acc[j]])
                            norm_T(sa, "pp", acc[:, j, :], B_acc[j], gf[:], Bgf, hT[:, :, j * 128:(j + 1) * 128], B_hT, 0, nbp)
                        for hc in range(16):
                            bank = 1 + hc % 2
                            for c in range(8):
                                P.op("pe", lambda e, hc=hc, c=c, bank=bank, Tg=Tg: e.matmul(
                                    out=pb[bank][:, 0:Tg], lhsT=wpq[:, c, hc * 128:(hc + 1) * 128], rhs=hT[:, c, 0:Tg],
                                    start=(c == 0), stop=(c == 7)), reads=[B_wpq, B_hT], writes=[PB[bank]])
                            P.op("act", lambda e, hc=hc, bank=bank, Tg=Tg: e.copy(out=qT[:, hc, 0:Tg], in_=pb[bank][:, 0:Tg]),
                                 reads=[PB[bank]], writes=[B_qT])
                        for j in range(gn):
                            for q4 in range(4):
                                bank = 3 + q4 % 2
                                for k in range(4):
                                    hc = q4 * 4 + k
                                    P.op("pe", lambda e, hc=hc, k=k, j=j, bank=bank: e.matmul(
                                        out=pb[bank][:, k * 128:(k + 1) * 128], lhsT=qT[:, hc, j * 128:(j + 1) * 128],
                                        rhs=skt[:, hc, :], start=True, stop=True), reads=[B_qT, B_skt], writes=[PB[bank]])
                                P.op("act", lambda e, j=j, q4=q4, bank=bank: e.copy(
                                    out=s_sb[:, j, q4 * 4:(q4 + 1) * 4, :], in_=pb[bank][:].rearrange("p (a n) -> p a n", a=4)),
                                    reads=[PB[bank]], writes=[B_s[j]])
                        for j in range(gn):
                            for hh in range(8):
                                for c2 in range(2):
                                    src = s_sb[:, j, 2 * hh + c2, :]
                                    P.op("dve", lambda e, src=src, c2=c2: e.max(out=v16[:, c2, 0:8], in_=src),
                                         reads=[B_s[j]], writes=[B_tk])
                                    P.op("dve", lambda e, src=src, c2=c2: e.match_replace(
                                        out=t128[:], in_to_replace=v16[:, c2, 0:8], in_values=src, imm_value=-1e30),
                                        reads=[B_s[j], B_tk], writes=[B_tk])
                                    P.op("dve", lambda e, c2=c2: e.max(out=v16[:, c2, 8:16], in_=t128[:]),
                                         reads=[B_tk], writes=[B_tk])
                                P.op("dve", lambda e: e.tensor_tensor(
                                    out=cand[:].rearrange("p (a b) -> p a b", a=16),
                                    in0=v16[:, 0, :].unsqueeze(2).to_broadcast([128, 16, 16]),
                                    in1=v16[:, 1, :].unsqueeze(1).to_broadcast([128, 16, 16]), op=ALU.add),
                                    reads=[B_tk], writes=[B_tk])
                                P.op("dve", lambda e, hh=hh: e.max(out=b16[:, hh, 0:8], in_=cand[:]), reads=[B_tk], writes=[B_tk])
                                P.op("dve", lambda e, hh=hh: e.match_replace(
                                    out=cand2[:], in_to_replace=b16[:, hh, 0:8], in_values=cand[:], imm_value=-1e30),
                                    reads=[B_tk], writes=[B_tk])
                                P.op("dve", lambda e, hh=hh: e.max(out=b16[:, hh, 8:16], in_=cand2[:]), reads=[B_tk], writes=[B_tk])
                            P.op("dve", lambda e, j=j: e.tensor_copy(out=th[:, j, :], in_=b16[:, :, 15]), reads=[B_tk], writes=[B_st[j]])
                            P.op("dve", lambda e: e.tensor_tensor(out=e16[:], in0=b16[:], in1=b16[:, :, 0:1].to_broadcast([128, 8, 16]),
                                                                  op=ALU.subtract), reads=[B_tk], writes=[B_tk])
                            P.op("act", lambda e: e.activation(out=e16[:], in_=e16[:], func=AF.Exp), reads=[B_tk], writes=[B_tk])
                            P.op("dve", lambda e: e.tensor_reduce(out=zz[:], in_=e16[:], axis=AX.X, op=ALU.add), reads=[B_tk], writes=[B_tk])
                            P.op("act", lambda e: e.activation(out=zz[:], in_=zz[:], func=AF.Ln), reads=[B_tk], writes=[B_tk])
                            P.op("dve", lambda e, j=j: e.scalar_tensor_tensor(
                                out=eb[:, j, :], in0=b16[:, :, 0], scalar=-1.0, in1=zz[:], op0=ALU.mult, op1=ALU.subtract),
                                reads=[B_tk], writes=[B_st[j]])
                            P.op("dve", lambda e, j=j: e.tensor_tensor(
                                out=s_sb[:, j, :, :].rearrange("p (h c) n -> p h c n", c=2)[:, :, 0, :],
                                in0=s_sb[:, j, :, :].rearrange("p (h c) n -> p h c n", c=2)[:, :, 0, :],
                                in1=eb[:, j, :].unsqueeze(2).to_broadcast([128, 8, 128]), op=ALU.add),
                                reads=[B_st[j]], writes=[B_s[j]])
                            P.op("dve", lambda e, j=j: e.tensor_tensor(out=th[:, j, :], in0=th[:, j, :], in1=eb[:, j, :], op=ALU.add),
                                 reads=[B_st[j]], writes=[B_st[j]])
                            P.op("act", lambda e, j=j: e.activation(out=th[:, j, :], in_=th[:, j, :], func=AF.Exp),
                                 reads=[B_st[j]], writes=[B_st[j]])
                            P.op("dve", lambda e, j=j: e.tensor_scalar(out=th[:, j, :], in0=th[:, j, :], scalar1=1.0 - 1e-5, scalar2=None,
                                                                       op0=ALU.mult), reads=[B_st[j]], writes=[B_st[j]])
                    P.barrier()
                    with contextlib.ExitStack() as se:
                        utg = [sb("utg%d_%d_%d" % (l, gt0, i), [128, 8, 1024], BF16, se) for i in range(2)]
                        vg = [sb("vg%d_%d_%d" % (l, gt0, i), [128, 8, D], BF16, se) for i in range(2)]
                        B_ut = [Buf() for _ in range(2)]
                        B_vg = [Buf() for _ in range(2)]
                        gA = [sb("gA%d_%d_%d" % (l, gt0, i), [128, 8, 512], BF16, se) for i in range(2)]
                        B_gA = [[Buf() for _ in range(8)] for _ in range(2)]
                        cc = [sb("cc%d_%d_%d" % (l, gt0, i), [128, 8, 128], F32, se) for i in range(2)]
                        ee = [sb("ee%d_%d_%d" % (l, gt0, i), [128, 8, 128], F32, se) for i in range(3)]
                        wm = [sb("wm%d_%d_%d" % (l, gt0, i), [128, 8, 128], BF16, se) for i in range(3)]
                        B_cc = [Buf() for _ in range(3)]
                        B_ee = [Buf() for _ in range(3)]
                        B_wm = [Buf() for _ in range(3)]
                        GT = [sb("GT%d_%d_%d" % (l, gt0, i), [128, 8, 128], BF16, se) for i in range(2)]
                        B_GT = [Buf() for _ in range(2)]
                        uTv = uT_b[l].rearrange("(c p) e -> p c e", p=128)
                        pvv = pv_b[l].rearrange("(i j) d -> j i d", j=128)

                        def load_eg(eg):
                            i = eg % 2
                            for c in range(0, 8, 4):
                                P.dma("sp", lambda e, eg=eg, i=i, c=c: e.dma_start(
                                    out=utg[i][:, c:c + 4, :], in_=uTv[:, c:c + 4, eg * 1024:(eg + 1) * 1024]),
                                    reads=[B_uT[l][eg]], writes=[B_ut[i]])
                                P.dma("sp", lambda e, eg=eg, i=i, c=c: e.dma_start(
                                    out=vg[i][:, c:c + 4, :], in_=pvv[:, eg * 8 + c:eg * 8 + c + 4, :]),
                                    reads=[B_pv[l][eg]], writes=[B_vg[i]])

                        zt = sb("zt%d_%d" % (l, gt0), [128, 512], F32, se)
                        B_zt = Buf()
                        P.op("pool", lambda e: e.memset(zt[:], 0.0), writes=[B_zt])
                        bgq = []

                        pend = []

                        def pump(k):
                            for _ in range(k):
                                while pend:
                                    pend.pop(0)()
                                if bgq:
                                    ep = bgq.pop(0)()
                                    if ep is not None:
                                        pend.append(ep)
                            if not bgq:
                                while pend:
                                    pend.pop(0)()

                        def stage1_tasks(eg):
                            i = eg % 2
                            tasks = []
                            for ch in range(8):
                                for part in range(2):
                                    def task(ch=ch, i=i, part=part):
                                        bank = ch % 2
                                        for c in range(part * 4, part * 4 + 4):
                                            P.op("pe", lambda e, ch=ch, c=c, i=i, bank=bank: e.matmul(
                                                out=pb[bank][:, 0:Tg], lhsT=utg[i][:, c, ch * 128:(ch + 1) * 128], rhs=hT[:, c, 0:Tg],
                                                start=(c == 0), stop=(c == 7)), reads=[B_ut[i], B_hT], writes=[PB[bank]])
                                        if part == 1:
                                            return lambda: P.op("act", lambda e, ch=ch, bank=bank, i=i: e.copy(
                                                out=gA[i][:, ch, 0:Tg], in_=pb[bank][:, 0:Tg]),
                                                reads=[PB[bank]], writes=[B_gA[i][ch]])
                                        return None
                                    tasks.append(task)
                            return tasks

                        def gv_tasks(j, gk, i):
                            tasks = []
                            for half in range(2):
                                for part in range(2):
                                    def task(half=half, j=j, gk=gk, i=i, part=part):
                                        bank = 6 + half
                                        for ch in range(part * 4, part * 4 + 4):
                                            P.op("pe", lambda e, ch=ch, gk=gk, i=i, half=half, bank=bank: e.matmul(
                                                out=pb[bank][:], lhsT=GT[gk][:, ch, :], rhs=vg[i][:, ch, half * 512:(half + 1) * 512],
                                                start=(ch == 0), stop=(ch == 7)), reads=[B_GT[gk], B_vg[i]], writes=[PB[bank]])
                                        if part == 1:
                                            return lambda: P.op("dve", lambda e, j=j, half=half, bank=bank: e.tensor_tensor(
                                                out=acc[:, j, half * 512:(half + 1) * 512], in0=pb[bank][:],
                                                in1=acc[:, j, half * 512:(half + 1) * 512], op=ALU.add),
                                                reads=[PB[bank]], writes=[B_acc[j]])
                                        return None
                                    tasks.append(task)
                            return tasks

                        load_eg(0)
                        for t_ in stage1_tasks(0):
                            ep_ = t_()
                            if ep_ is not None:
                                ep_()
                        hcnt = 0
                        for eg in range(16):
                            i = eg % 2
                            if eg + 1 < 16 and gn < 4:
                                load_eg(eg + 1)
                                bgq.extend(stage1_tasks(eg + 1))
                            P.op("act", lambda e, i=i: e.activation(out=gA[i][:, :, 0:Tg], in_=gA[i][:, :, 0:Tg], func=AF.Gelu_apprx_tanh),
                                 reads=B_gA[i], writes=B_gA[i])
                            for j in range(gn):
                                wb0 = 2 + 2 * (j % 2)
                                for hh in range(8):
                                    k = hcnt % 3
                                    hcnt += 1
                                    if hh % 2 == 0:
                                        for i8 in range(8):
                                            P.op("act", lambda e, j=j, hh=hh, k=k, eg=eg, i8=i8: e.activation(
                                                out=ee[k][:, i8, :], in_=s_sb[:, j, 2 * hh + 1, :], func=AF.Exp,
                                                bias=s_sb[:, j, 2 * hh, eg * 8 + i8:eg * 8 + i8 + 1]),
                                                reads=[B_s[j]], writes=[B_ee[k]])
                                    else:
                                        kc = (hcnt // 2) % 2
                                        P.op("pool", lambda e, j=j, hh=hh, kc=kc, eg=eg: e.tensor_tensor(
                                            out=cc[kc][:], in0=s_sb[:, j, 2 * hh, eg * 8:(eg + 1) * 8].unsqueeze(2).to_broadcast([128, 8, 128]),
                                            in1=s_sb[:, j, 2 * hh + 1, :].unsqueeze(1).to_broadcast([128, 8, 128]), op=ALU.add),
                                            reads=[B_s[j]], writes=[B_cc[kc]])
                                        P.op("act", lambda e, k=k, kc=kc: e.activation(out=ee[k][:], in_=cc[kc][:], func=AF.Exp),
                                             reads=[B_cc[kc]], writes=[B_ee[k]])
                                    P.op("dve", lambda e, j=j, hh=hh, k=k: e.scalar_tensor_tensor(
                                        out=wm[k][:], in0=ee[k][:], scalar=th[:, j, hh:hh + 1], in1=ee[k][:],
                                        op0=ALU.is_ge, op1=ALU.mult), reads=[B_ee[k], B_st[j]], writes=[B_wm[k]])
                                    for ch in range(8):
                                        bank = wb0 + ch // 4
                                        P.op("pe", lambda e, ch=ch, k=k, bank=bank, hh=hh: e.matmul(
                                            out=pb[bank][:, (ch % 4) * 128:(ch % 4 + 1) * 128], lhsT=wm[k][:, ch, :], rhs=idb[:],
                                            start=(hh == 0 and ch % 4 == 0), stop=(hh == 7), skip_group_check=True), reads=[B_wm[k], B_id], writes=[PB[bank]])
                                    pump(1)
                                gk = j % 2
                                for half in range(2):
                                    P.op("dve", lambda e, j=j, gk=gk, half=half, wb0=wb0, i=i: e.tensor_tensor(
                                        out=GT[gk][:, half * 4:(half + 1) * 4, :],
                                        in0=pb[wb0 + half][:].rearrange("p (a t) -> p a t", a=4),
                                        in1=gA[i][:, half * 4:(half + 1) * 4, j * 128:(j + 1) * 128], op=ALU.mult),
                                        reads=[PB[wb0 + half]] + B_gA[i][half * 4:(half + 1) * 4], writes=[B_GT[gk]])
                                bgq.extend(gv_tasks(j, gk, i))
                                if gn == 4 and j == 0 and eg + 1 < 16:
                                    assert len(bgq) <= 4, len(bgq)
                                    load_eg(eg + 1)
                                    bgq.extend(stage1_tasks(eg + 1))
                            if gn < 4 or eg == 15:
                                pump(len(bgq))
                            else:
                                while pend:
                                    pend.pop(0)()
                        if final_norm:
                            fsq = sb("fsq%d" % gt0, [128, D], F32, se)
                            fss = sb("fss%d" % gt0, [128, 1], F32, se)
                            B_f = Buf()
                        for j in range(gn):
                            ti = gt0 + j
                            if final_norm:
                                P.op("act", lambda e, j=j: e.activation(out=fsq[:], in_=acc[:, j, :], func=AF.Square, accum_out=fss[:]),
                                     reads=[B_acc[j]], writes=[B_f])
                                P.op("dve", lambda e: e.tensor_scalar(out=fss[:], in0=fss[:], scalar1=1.0 / D, scalar2=EPS,
                                                                      op0=ALU.mult, op1=ALU.add), reads=[B_f], writes=[B_f])
                                P.op("act", lambda e: e.activation(out=fss[:], in_=fss[:], func=AF.Sqrt), reads=[B_f], writes=[B_f])
                                P.op("dve", lambda e: e.reciprocal(out=fss[:], in_=fss[:]), reads=[B_f], writes=[B_f])
                                P.op("dve", lambda e, j=j: e.scalar_tensor_tensor(
                                    out=acc[:, j, :], in0=acc[:, j, :], scalar=fss[:, 0:1], in1=gfin[:], op0=ALU.mult, op1=ALU.mult),
                                    reads=[B_f, Bgfin], writes=[B_acc[j]])
                            P.dma("sp", lambda e, ti=ti, j=j: e.dma_start(
                                out=Xout[(out_row0 + ti) * 128:(out_row0 + ti + 1) * 128, :], in_=acc[:, j, :]),
                                reads=[B_acc[j]], writes=[B_out[out_row0 + ti]])
                    P.barrier()

        B_X2 = [Buf() for _ in range(NT0)]
        peer_phase(0, X1, B_X1, NT0, X2, B_X2, 0, False)

        B_X3 = [Buf() for _ in range(32)]
        with contextlib.ExitStack() as st:
            w1 = sb("w1_sb", [128, 8, 2 * D], BF16, st)
            w2 = sb("w2_sb", [128, 8, D], BF16, st)
            B_w1, B_w2 = Buf(), Buf()
            for c in range(8):
                P.dma("sp", lambda e, c=c: e.dma_start(out=w1[:, c, :], in_=wpw1_b[c * 128:(c + 1) * 128, :]),
                      reads=[B_w["wpw1"]], writes=[B_w1])
            P.dma("sp", lambda e: e.dma_start(out=w2[:], in_=wpw2_b.rearrange("(c p) n -> p c n", p=128)),
                  reads=[B_w["wpw2"]], writes=[B_w2])
            g1, Bg1 = load_rep(st, "g1", norm_mix[1:2, :])
            b2r, Bb2 = load_rep(st, "b2r", b_pw2[0:1, :])
            halo, B_halo = load_rep(st, "halo_sb", halo_in[0:1, :], 1)
            bp1 = sb("bp1", [128, 16], F32, st)
            wdw = sb("wdw", [128, 8, CW], F32, st)
            bdw = sb("bdw", [128, 8], F32, st)
            lng = sb("lng", [128, 8], F32, st)
            lnb = sb("lnb", [128, 8], F32, st)
            B_cp = Buf()
            P.dma("sp", lambda e: e.dma_start(out=bp1[:], in_=b_pw1), writes=[B_cp])
            P.dma("sp", lambda e: e.dma_start(out=wdw[:], in_=w_dwT.rearrange("(c p) k -> p c k", p=128)), writes=[B_cp])
            P.dma("sp", lambda e: e.dma_start(out=bdw[:], in_=b_dw), writes=[B_cp])
            P.dma("sp", lambda e: e.dma_start(out=lng[:], in_=ln_g), writes=[B_cp])
            P.dma("sp", lambda e: e.dma_start(out=lnb[:], in_=ln_b), writes=[B_cp])
            ones = sb("ones", [128, 128], F32, st)
            P.op("dve", lambda e: e.memset(ones[:], 1.0), writes=[B_cp])
            xg = sb("cxg", [128, 4, D], F32, st)
            B_xg = [Buf() for _ in range(4)]
            hTc = sb("chT", [128, 8, 512], BF16, st)
            B_hTc = Buf()
            nbc = norm_bufs(st, "cn")
            uTb = sb("cuT", [128, 8, 32 + 512], F32, st)
            B_u = [Buf() for _ in range(8)]
            sg = [sb("csg%d" % i, [128, 512], F32, st) for i in range(2)]
            B_sg = [Buf() for _ in range(2)]
            cv = sb("ccv", [128, 8, 512], F32, st)
            B_cv = [Buf() for _ in range(8)]
            sq2 = [sb("csq%d" % i, [128, 512], F32, st) for i in range(2)]
            B_sq2 = [Buf() for _ in range(2)]
            mean = sb("cmean", [128, 512], F32, st)
            var = sb("cvar", [128, 512], F32, st)
            B_mv = Buf()
            t1 = [sb("ct1%d" % i, [128, 512], F32, st) for i in range(2)]
            B_t1 = [Buf() for _ in range(2)]
            zT = sb("czT", [128, 8, 512], BF16, st)
            B_zT = [Buf() for _ in range(8)]
            groups = [(0, 1)] + [(1 + 4 * k, 4) for k in range(8)]
            for (gt0, gn) in groups:
                Tg = gn * 128
                for j in range(gn):
                    ti = gt0 + j
                    P.dma("sp", lambda e, ti=ti, j=j: e.dma_start(out=xg[:, j, :], in_=X2[ti * 128:(ti + 1) * 128, :]),
                          reads=[B_X2[ti]], writes=[B_xg[j]])
                    norm_T(st, "cc", xg[:, j, :], B_xg[j], g1[:], Bg1, hTc[:, :, j * 128:(j + 1) * 128], B_hTc, 0, nbc)
                if gt0 > 0:
                    prevT = 128 if gt0 == 1 else 512
                    for ch in range(8):
                        P.op("pool", lambda e, ch=ch, prevT=prevT: e.tensor_copy(out=uTb[:, ch, 0:32], in_=uTb[:, ch, prevT:prevT + 32]),
                             reads=[B_u[ch]], writes=[B_u[ch]])
                for ch in range(8):
                    bv_, bg_ = 1 + (ch % 2) * 2, 2 + (ch % 2) * 2
                    for (bank, col) in ((bv_, ch), (bg_, 8 + ch)):
                        for c in range(8):
                            P.op("pe", lambda e, c=c, bank=bank, col=col, Tg=Tg: e.matmul(
                                out=pb[bank][:, 0:Tg], lhsT=w1[:, c, col * 128:(col + 1) * 128], rhs=hTc[:, c, 0:Tg],
                                start=(c == 0), stop=(c == 7)), reads=[B_w1, B_hTc], writes=[PB[bank]])
                    si = ch % 2
                    P.op("act", lambda e, ch=ch, bg_=bg_, si=si, Tg=Tg: e.activation(
                        out=sg[si][:, 0:Tg], in_=pb[bg_][:, 0:Tg], func=AF.Sigmoid, bias=bp1[:, 8 + ch:9 + ch]),
                        reads=[PB[bg_], B_cp], writes=[B_sg[si]])
                    P.op("dve", lambda e, ch=ch, bv_=bv_, si=si, Tg=Tg: e.scalar_tensor_tensor(
                        out=uTb[:, ch, 32:32 + Tg], in0=pb[bv_][:, 0:Tg], scalar=bp1[:, ch:ch + 1], in1=sg[si][:, 0:Tg],
                        op0=ALU.add, op1=ALU.mult), reads=[PB[bv_], B_sg[si], B_cp], writes=[B_u[ch]])
                    if gt0 == 0:
                        P.op("dve", lambda e, ch=ch, Tg=Tg: e.tensor_scalar(
                            out=uTb[:, ch, 32:32 + Tg], in0=uTb[:, ch, 32:32 + Tg], scalar1=halo[:, 0:1], scalar2=None, op0=ALU.mult),
                            reads=[B_u[ch], B_halo], writes=[B_u[ch]])
                if gt0 == 0:
                    continue
                for ch in range(8):
                    eng = "dve"
                    P.op(eng, lambda e, ch=ch: e.tensor_scalar(
                        out=cv[:, ch, :], in0=uTb[:, ch, 2:514], scalar1=wdw[:, ch, 0:1], scalar2=bdw[:, ch:ch + 1],
                        op0=ALU.mult, op1=ALU.add), reads=[B_u[ch], B_cp], writes=[B_cv[ch]])
                    for k in range(1, CW):
                        P.op(eng, lambda e, ch=ch, k=k: e.scalar_tensor_tensor(
                            out=cv[:, ch, :], in0=uTb[:, ch, 2 + k:514 + k], scalar=wdw[:, ch, k:k + 1], in1=cv[:, ch, :],
                            op0=ALU.mult, op1=ALU.add), reads=[B_u[ch], B_cp, B_cv[ch]], writes=[B_cv[ch]])
                for ch in range(8):
                    si = ch % 2
                    P.op("act", lambda e, ch=ch, si=si: e.activation(out=sq2[si][:], in_=cv[:, ch, :], func=AF.Square),
                         reads=[B_cv[ch]], writes=[B_sq2[si]])
                    P.op("pe", lambda e, ch=ch: e.matmul(out=pb[5][:], lhsT=ones[:], rhs=cv[:, ch, :], start=(ch == 0), stop=(ch == 7)),
                         reads=[B_cv[ch], B_cp], writes=[PB[5]])
                    P.op("pe", lambda e, ch=ch, si=si: e.matmul(out=pb[6][:], lhsT=ones[:], rhs=sq2[si][:], start=(ch == 0), stop=(ch == 7)),
                         reads=[B_sq2[si], B_cp], writes=[PB[6]])
                P.op("dve", lambda e: e.tensor_scalar(out=mean[:], in0=pb[5][:], scalar1=1.0 / D, scalar2=None, op0=ALU.mult),
                     reads=[PB[5]], writes=[B_mv])
                P.op("dve", lambda e: e.tensor_tensor(out=var[:], in0=mean[:], in1=mean[:], op=ALU.mult), reads=[B_mv], writes=[B_mv])
                P.op("dve", lambda e: e.scalar_tensor_tensor(out=var[:], in0=pb[6][:], scalar=1.0 / D, in1=var[:],
                                                             op0=ALU.mult, op1=ALU.subtract), reads=[PB[6], B_mv], writes=[B_mv])
                P.op("dve", lambda e: e.tensor_scalar(out=var[:], in0=var[:], scalar1=EPS, scalar2=None, op0=ALU.add),
                     reads=[B_mv], writes=[B_mv])
                P.op("act", lambda e: e.activation(out=var[:], in_=var[:], func=AF.Sqrt), reads=[B_mv], writes=[B_mv])
                P.op("dve", lambda e: e.reciprocal(out=var[:], in_=var[:]), reads=[B_mv], writes=[B_mv])
                for ch in range(8):
                    si = ch % 2
                    P.op("dve", lambda e, ch=ch, si=si: e.tensor_tensor(out=t1[si][:], in0=cv[:, ch, :], in1=mean[:], op=ALU.subtract),
                         reads=[B_cv[ch], B_mv], writes=[B_t1[si]])
                    P.op("dve", lambda e, si=si: e.tensor_tensor(out=t1[si][:], in0=t1[si][:], in1=var[:], op=ALU.mult),
                         reads=[B_mv], writes=[B_t1[si]])
                    P.op("act", lambda e, ch=ch, si=si: e.activation(out=zT[:, ch, :], in_=t1[si][:], func=AF.Silu,
                                                                    bias=lnb[:, ch:ch + 1], scale=lng[:, ch:ch + 1]),
                         reads=[B_t1[si], B_cp], writes=[B_zT[ch]])
                for j in range(gn):
                    ti = gt0 + j
                    P.op("pool", lambda e, j=j: e.tensor_tensor(out=xg[:, j, :], in0=xg[:, j, :], in1=b2r[:], op=ALU.add),
                         reads=[Bb2], writes=[B_xg[j]])
                    for half in range(2):
                        bank = 3 + half if j % 2 == 0 else 1 + half
                        bank = 7 if half == 0 else 0
                        for c in range(8):
                            P.op("pe", lambda e, c=c, j=j, half=half, bank=bank: e.matmul(
                                out=pb[bank][:], lhsT=zT[:, c, j * 128:(j + 1) * 128], rhs=w2[:, c, half * 512:(half + 1) * 512],
                                start=(c == 0), stop=(c == 7)), reads=[B_zT[c], B_w2], writes=[PB[bank]])
                        P.op("dve", lambda e, j=j, half=half, bank=bank: e.tensor_tensor(
                            out=xg[:, j, half * 512:(half + 1) * 512], in0=pb[bank][:], in1=xg[:, j, half * 512:(half + 1) * 512],
                            op=ALU.add), reads=[PB[bank]], writes=[B_xg[j]])
                    P.dma("sp", lambda e, ti=ti, j=j: e.dma_start(out=X3[(ti - 1) * 128:ti * 128, :], in_=xg[:, j, :]),
                          reads=[B_xg[j]], writes=[B_X3[ti - 1]])
        P.barrier()

        B_Y = [Buf() for _ in range(32)]
        peer_phase(1, X3, B_X3, 32, y, B_Y, 0, True)
        P.emit(final_bufs=B_Y)
    P.close()
    return nc


_CACHE = {}


def host_tables(half):
    valid = np.ones(NB, np.float32) if half == 1 else (np.arange(NB) >= 16).astype(np.float32)
    pm = np.zeros((NG, NB), np.float32)
    a1 = np.zeros((NG, NB), np.float32)
    a2 = np.zeros((NG, NB), np.float32)
    for g in range(NG):
        own = 15 + g
        for n in range(NB):
            past = (n < own) and valid[n] > 0
            pm[g, n] = 0.0 if past else -1e30
            if n <= own - 1:
                a1[g, n] = NEGB
            if n <= own - 2:
                a2[g, n] = 8.0
    a3 = -a1
    return pm.reshape(1, -1), a1.reshape(1, -1), a2.reshape(1, -1), a3.reshape(1, -1)


def kernel(x, rel_bias, norm_mix, norm_ffn, attn_w_qkv, attn_w_o, conv_w_pw1, conv_b_pw1, conv_w_dw, conv_b_dw,
           conv_ln_g, conv_ln_b, conv_w_pw2, conv_b_pw2, peer_w_q, peer_sub_keys, peer_u, peer_v, norm_final, _dbg=False):
    f = lambda a: np.ascontiguousarray(np.asarray(a, dtype=np.float32))
    x = f(x)
    boh = np.zeros((NB, SEQ), np.float32)
    for n in range(NB):
        boh[n, n * BLK:(n + 1) * BLK] = 1.0
    shared = {
        "rel_bias": f(rel_bias), "norm_mix": f(norm_mix), "norm_ffn": f(norm_ffn), "norm_final": f(norm_final).reshape(1, D),
        "w_qkv": f(attn_w_qkv[0]), "w_o": f(attn_w_o[0]), "w_pw1": f(conv_w_pw1[0]),
        "b_pw1": f(np.asarray(conv_b_pw1[0]).reshape(16, 128).T), "w_dwT": f(np.asarray(conv_w_dw[0]).T),
        "b_dw": f(np.asarray(conv_b_dw[0]).reshape(8, 128).T), "ln_g": f(np.asarray(conv_ln_g[0]).reshape(8, 128).T), "ln_b": f(np.asarray(conv_ln_b[0]).reshape(8, 128).T),
        "w_pw2": f(conv_w_pw2[0]), "b_pw2": f(conv_b_pw2[0]).reshape(1, D), "w_pq": f(peer_w_q),
        "skT": f(np.asarray(peer_sub_keys).reshape(2, 16, 128, 128).transpose(0, 1, 3, 2)),
        "uT": f(np.asarray(peer_u).transpose(0, 2, 1)), "pv": f(peer_v),
        "ident": np.eye(128, dtype=np.float32), "boh": boh,
    }
    in_maps = []
    for c in range(8):
        b, half = c // 2, c % 2
        xvv = np.zeros((SEQ, D), np.float32)
        if half == 1:
            xvv[:] = x[b]
        else:
            xvv[4096:] = x[b, :4096]
        pm, a1, a2, a3 = host_tables(half)
        m = dict(shared)
        m.update({"xv": xvv, "pm": pm, "a1": a1, "a2": a2, "a3": a3, "halo": np.full((1, 1), float(half), np.float32)})
        in_maps.append(m)
    key = bool(_dbg)
    if key not in _CACHE:
        _CACHE[key] = build_program(dbg=key)
    nc = _CACHE[key]
    res = run_bass_kernel_spmd(nc, in_maps, core_ids=list(range(8)))
    out = np.zeros((4, SEQ, D), np.float32)
    for c in range(8):
        b, half = c // 2, c % 2
        out[b, half * 4096:(half + 1) * 4096] = res.results[c]["y"]
    if _dbg:
        return out, res.results
    return out
```

```python
import contextlib
import math
import numpy as np
import concourse.bass as bass
import concourse.mybir as mybir
from concourse.bass_utils import run_bass_kernel_spmd

F32 = mybir.dt.float32
BF16 = mybir.dt.bfloat16
ALU = mybir.AluOpType
AF = mybir.ActivationFunctionType
AX = mybir.AxisListType

D = 1024
NH = 16
HD = 64
SEQ = 8192
BLK = 256
NB = 32
NQT = 34
NG = 17
NT0 = 33
CW = 31
NEGB = 240000.0
EPS = 1e-6

COMPUTE = ("pe", "act", "dve", "pool")
STREAM_OF = {"pe": "tensor", "act": "scalar", "dve": "vector", "pool": "gpsimd",
             "sp": "sync", "actq": "scalar", "poolq": "gpsimd"}


class Buf:
    __slots__ = ("name", "w", "r")

    def __init__(self, name=""):
        self.name = name
        self.w = None
        self.r = {}


class Prog:
    def __init__(self, nc):
        self.nc = nc
        self.streams = {s: [] for s in ("tensor", "scalar", "vector", "gpsimd", "sync")}
        self.cnt = {e: 0 for e in COMPUTE}
        self.sems = {}
        self._ctx = []
        self.dpool = {}
        self.drr = {}
        self.DPOOL = {"sp": 28, "actq": 6, "poolq": 12}
        self.nsem = 0
        for e in COMPUTE:
            self.sems[e] = self._sem("c_" + e)

    def _sem(self, name):
        cm = self.nc.semaphore(name)
        s = cm.__enter__()
        self._ctx.append(cm)
        return s

    def _deps(self, eng, reads, writes):
        waits = {}

        def need(tok):
            if tok is None:
                return
            kind, key, val = tok
            if kind == "c" and key == eng and eng == "pe":
                return
            k = (kind, key)
            if k not in waits or waits[k][2] < val:
                waits[k] = tok

        for b in reads:
            need(b.w)
        for b in writes:
            need(b.w)
            for t in b.r.values():
                need(t)
        return list(waits.values())

    def op(self, eng, fn, reads=(), writes=()):
        waits = self._deps(eng, reads, writes)
        self.cnt[eng] += 1
        tok = ("c", eng, self.cnt[eng])
        self.streams[STREAM_OF[eng]].append((waits, fn, ("c", eng)))
        for b in reads:
            b.r[("c", eng)] = tok
        for b in writes:
            b.w = tok
            b.r = {}
        return tok

    def dma(self, q, fn, reads=(), writes=()):
        waits = self._deps(q, reads, writes)
        pool = self.dpool.setdefault(q, [])
        if len(pool) < self.DPOOL[q]:
            pool.append([self._sem("d_%s%d" % (q, len(pool))), 0])
            idx = len(pool) - 1
        else:
            idx = self.drr.get(q, 0) % len(pool)
        self.drr[q] = idx + 1
        key = (q, idx)
        if pool[idx][1] > 0:
            waits = [t for t in waits if (t[0], t[1]) != ("d", key)] + [("d", key, pool[idx][1])]
        pool[idx][1] += 16
        tok = ("d", key, pool[idx][1])
        self.streams[STREAM_OF[q]].append((waits, fn, ("d", key)))
        for b in reads:
            b.r[("d", key)] = tok
        for b in writes:
            b.w = tok
            b.r = {}
        return tok

    def barrier(self):
        toks = [("c", e, self.cnt[e]) for e in COMPUTE if self.cnt[e] > 0]
        for q, pool in self.dpool.items():
            for idx, (s, v) in enumerate(pool):
                if v > 0:
                    toks.append(("d", (q, idx), v))
        for s in self.streams:
            self.streams[s].append((list(toks), None, None))

    def emit(self, final_bufs=()):
        nc = self.nc
        finals = []
        for b in final_bufs:
            if b.w is not None:
                finals.append(b.w)
            finals.extend(b.r.values())

        def semof(tok):
            kind, key, val = tok
            return (self.sems[key] if kind == "c" else self.dpool[key[0]][key[1]][0]), val

        def replay(handle, items, extra=()):
            seen = {}
            for waits, fn, me in list(items) + [(list(extra), None, None)]:
                for t in waits:
                    k = (t[0], t[1])
                    if seen.get(k, 0) >= t[2]:
                        continue
                    seen[k] = t[2]
                    s, v = semof(t)
                    handle.wait_ge(s, v)
                if fn is None:
                    continue
                ins = fn(handle)
                if me[0] == "c":
                    ins.then_inc(self.sems[me[1]], 1)
                    seen[me] = max(seen.get(me, 0), 0)
                else:
                    ins.then_inc(self.dpool[me[1][0]][me[1][1]][0], 16)

        streams = self.streams
        with nc.Block() as block:
            @block.tensor
            def _(e):
                replay(e, streams["tensor"])

            @block.scalar
            def _(e):
                replay(e, streams["scalar"])

            @block.vector
            def _(e):
                replay(e, streams["vector"])

            @block.gpsimd
            def _(e):
                replay(e, streams["gpsimd"])

            @block.sync
            def _(e):
                replay(e, streams["sync"], finals)

    def close(self):
        for cm in reversed(self._ctx):
            cm.__exit__(None, None, None)


def t5_runs():
    d = np.arange(768, dtype=np.int32)
    df = np.maximum(d, 1).astype(np.float32)
    large = 16 + (np.log(df / np.float32(16)) / np.float32(math.log(128 / 16)) * np.float32(16)).astype(np.int32)
    large = np.minimum(large, 31)
    return np.where(d < 16, d, large)


def build_program(dbg=False):
    nc = bass.Bass("TRN2", target_bir_lowering=False)
    P = Prog(nc)

    def din(name, shape, dt=F32):
        return nc.dram_tensor(name, list(shape), dt, kind="ExternalInput").ap()

    def dscr(name, shape, dt, out=False):
        return nc.dram_tensor(name, list(shape), dt, kind=("ExternalOutput" if out else "Internal")).ap()

    xv = din("xv", [SEQ, D])
    rel_bias = din("rel_bias", [NH, 32])
    norm_mix = din("norm_mix", [2, D])
    norm_ffn = din("norm_ffn", [2, D])
    norm_final = din("norm_final", [1, D])
    w_qkv = din("w_qkv", [D, 3 * D])
    w_o = din("w_o", [D, D])
    w_pw1 = din("w_pw1", [D, 2 * D])
    b_pw1 = din("b_pw1", [128, 16])
    w_dwT = din("w_dwT", [D, CW])
    b_dw = din("b_dw", [128, 8])
    ln_g = din("ln_g", [128, 8])
    ln_b = din("ln_b", [128, 8])
    w_pw2 = din("w_pw2", [D, D])
    b_pw2 = din("b_pw2", [1, D])
    w_pq = din("w_pq", [2, D, 2048])
    skT = din("skT", [2, 16, 128, 128])
    uT = din("uT", [2, D, 16384])
    pv = din("pv", [2, 16384, D])
    ident_in = din("ident", [128, 128])
    boh_in = din("boh", [NB, SEQ])
    pm_in = din("pm", [1, NG * NB])
    a1_in = din("a1", [1, NG * NB])
    a2_in = din("a2", [1, NG * NB])
    a3_in = din("a3", [1, NG * NB])
    halo_in = din("halo", [1, 1])
    y = dscr("y", [4096, D], F32, out=True)

    wqkv_b = dscr("wqkv_b", [D, 3 * D], BF16)
    wo_b = dscr("wo_b", [D, D], BF16)
    wpw1_b = dscr("wpw1_b", [D, 2 * D], BF16)
    wpw2_b = dscr("wpw2_b", [D, D], BF16)
    wpq_b = dscr("wpq_b", [2, D, 2048], BF16)
    skT_b = dscr("skT_b", [2, 16, 128, 128], BF16)
    uT_b = dscr("uT_b", [2, D, 16384], BF16)
    pv_b = dscr("pv_b", [2, 16384, D], BF16)
    boh_b = dscr("boh_b", [NB, SEQ], BF16)
    KT = dscr("KT", [NH, HD, SEQ], BF16)
    QT = dscr("QT", [NH, HD, NQT * 128], BF16)
    VS = dscr("VS", [SEQ, D], BF16)
    OA = dscr("OA", [NQT * 128, D], F32, out=dbg)
    X1 = dscr("X1", [NT0 * 128, D], F32, out=dbg)
    X2 = dscr("X2", [NT0 * 128, D], F32, out=dbg)
    X3 = dscr("X3", [4096, D], F32, out=dbg)
    TT = dscr("TT", [NH, 128, 1024], F32)

    es = contextlib.ExitStack()
    with es:
        def sb(name, shape, dt, st=es):
            return st.enter_context(nc.sbuf_tensor(name, list(shape), dt))

        pb = [es.enter_context(nc.psum_tensor("pb%d" % i, [128, 512], F32)) for i in range(8)]
        PB = [Buf("pb%d" % i) for i in range(8)]

        idf = sb("idf", [128, 128], F32)
        idb = sb("idb", [128, 128], BF16)
        B_id = Buf("id")
        P.dma("sp", lambda e: e.dma_start(out=idf[:], in_=ident_in), writes=[B_id])
        P.op("dve", lambda e: e.tensor_copy(out=idb[:], in_=idf[:]), reads=[B_id], writes=[B_id])

        B_w = {k: Buf(k) for k in ("wqkv", "wo", "wpw1", "wpw2", "wpq", "skT", "boh")}
        B_uT = [[Buf() for _ in range(16)] for _ in range(2)]
        B_pv = [[Buf() for _ in range(16)] for _ in range(2)]

        def cast(dst, src, buf, rows=None):
            n = dst.shape[0]
            step = rows or n
            for r0 in range(0, n, step):
                P.dma("poolq", lambda e, r0=r0: e.dma_start(out=dst[r0:r0 + step], in_=src[r0:r0 + step]), writes=[buf])

        cast(wqkv_b, w_qkv, B_w["wqkv"], 256)
        cast(boh_b, boh_in, B_w["boh"])
        cast(wo_b, w_o, B_w["wo"], 512)
        for l in range(2):
            cast(wpq_b[l], w_pq[l], B_w["wpq"], 512)
            cast(skT_b[l].rearrange("a b c -> (a b) c"), skT[l].rearrange("a b c -> (a b) c"), B_w["skT"])
        cast(wpw1_b, w_pw1, B_w["wpw1"], 512)
        cast(wpw2_b, w_pw2, B_w["wpw2"], 512)
        for l in range(2):
            for eg in range(16):
                P.dma("poolq", lambda e, l=l, eg=eg: e.dma_start(
                    out=uT_b[l][:, eg * 1024:(eg + 1) * 1024], in_=uT[l][:, eg * 1024:(eg + 1) * 1024]),
                    writes=[B_uT[l][eg]])
                P.dma("poolq", lambda e, l=l, eg=eg: e.dma_start(
                    out=pv_b[l][eg * 1024:(eg + 1) * 1024, :], in_=pv[l][eg * 1024:(eg + 1) * 1024, :]),
                    writes=[B_pv[l][eg]])

        def norm_T(st, pfx, x_sb, Bx, g_rep, Bg, hT_dst, B_hT, tpbank, nbufs):
            sq, ss, rs, hb, Bs = nbufs
            P.op("act", lambda e: e.activation(out=sq[:], in_=x_sb, func=AF.Square, accum_out=ss[:]),
                 reads=[Bx], writes=[Bs])
            P.op("dve", lambda e: e.tensor_scalar(out=rs[:], in0=ss[:], scalar1=1.0 / D, scalar2=EPS,
                                                  op0=ALU.mult, op1=ALU.add), reads=[Bs], writes=[Bs])
            P.op("act", lambda e: e.activation(out=rs[:], in_=rs[:], func=AF.Sqrt), reads=[Bs], writes=[Bs])
            P.op("dve", lambda e: e.reciprocal(out=rs[:], in_=rs[:]), reads=[Bs], writes=[Bs])
            P.op("dve", lambda e: e.scalar_tensor_tensor(out=hb[:], in0=x_sb, scalar=rs[:, 0:1], in1=g_rep,
                                                         op0=ALU.mult, op1=ALU.mult),
                 reads=[Bx, Bs, Bg], writes=[Bs])
            tpv = pb[tpbank][:].bitcast(BF16)
            for c in range(8):
                P.op("pe", lambda e, c=c: e.transpose(out=tpv[:, c * 128:(c + 1) * 128],
                                                      in_=hb[:, c * 128:(c + 1) * 128], identity=idb[:]),
                     reads=[Bs, B_id], writes=[PB[tpbank]])
            P.op("act", lambda e: e.copy(out=hT_dst, in_=tpv.rearrange("p (c t) -> p c t", c=8)),
                 reads=[PB[tpbank]], writes=[B_hT])

        def norm_bufs(st, pfx):
            return (sb(pfx + "sq", [128, D], F32, st), sb(pfx + "ss", [128, 1], F32, st),
                    sb(pfx + "rs", [128, 1], F32, st), sb(pfx + "hb", [128, D], BF16, st), Buf(pfx + "nb"))

        def load_rep(st, name, src_row, n=D):
            t = sb(name, [128, n], F32, st)
            b = Buf(name)
            P.dma("sp", lambda e: e.dma_start(out=t[:], in_=src_row.to_broadcast([128, n])), writes=[b])
            return t, b

        with contextlib.ExitStack() as st:
            wq = sb("wqkv_sb", [128, 8, 3 * D], BF16, st)
            B_wq = Buf()
            for c in range(8):
                P.dma("sp", lambda e, c=c: e.dma_start(out=wq[:, c, :], in_=wqkv_b[c * 128:(c + 1) * 128, :]),
                      reads=[B_w["wqkv"]], writes=[B_wq])
            g0, Bg0 = load_rep(st, "g0", norm_mix[0:1, :])
            nb = [norm_bufs(st, "p1n%d" % i) for i in range(2)]
            xts = [sb("p1x%d" % i, [128, D], F32, st) for i in range(2)]
            Bxs = [Buf() for _ in range(2)]
            hTs = [sb("p1hT%d" % i, [128, 8, 128], BF16, st) for i in range(2)]
            BhT = [Buf() for _ in range(2)]
            kts = [sb("p1kt%d" % i, [128, 8, 128], BF16, st) for i in range(2)]
            Bkt = [Buf() for _ in range(2)]
            qts = [sb("p1qt%d" % i, [128, 8, 128], BF16, st) for i in range(2)]
            Bqt = [Buf() for _ in range(2)]
            vts = [sb("p1vt%d" % i, [128, D], BF16, st) for i in range(2)]
            Bvt = [Buf() for _ in range(2)]
            B_KT = [Buf() for _ in range(64)]
            B_QT = [Buf() for _ in range(NQT)]
            B_VS = [Buf() for _ in range(64)]
            KTv = KT.rearrange("(hp two) d t -> (two d) hp t", two=2)
            QTv = QT.rearrange("(hp two) d t -> (two d) hp t", two=2)
            for vt in range(64):
                i = vt % 2
                P.dma("sp", lambda e, vt=vt, i=i: e.dma_start(out=xts[i][:], in_=xv[vt * 128:(vt + 1) * 128, :]),
                      writes=[Bxs[i]])
                norm_T(st, "p1", xts[i][:], Bxs[i], g0[:], Bg0, hTs[i][:], BhT[i], 0, nb[i])
                for (woff, dst, Bdst, bank0, do) in ((D, kts[i], Bkt[i], 1, True), (0, qts[i], Bqt[i], 3, vt >= 30)):
                    if not do:
                        continue
                    for hp in range(8):
                        bank = bank0 + hp // 4
                        for c in range(8):
                            P.op("pe", lambda e, hp=hp, c=c, bank=bank, woff=woff, i=i: e.matmul(
                                out=pb[bank][:, (hp % 4) * 128:(hp % 4 + 1) * 128],
                                lhsT=wq[:, c, woff + hp * 128: woff + (hp + 1) * 128], rhs=hTs[i][:, c, :],
                                start=(c == 0), stop=(c == 7)), reads=[B_wq, BhT[i]], writes=[PB[bank]])
                    for half in range(2):
                        P.op("dve" if half == 0 else "act",
                             (lambda e, half=half, dst=dst, bank0=bank0: e.tensor_copy(
                                 out=dst[:, half * 4:(half + 1) * 4, :],
                                 in_=pb[bank0 + half][:].rearrange("p (a t) -> p a t", a=4))) if half == 0 else
                             (lambda e, half=half, dst=dst, bank0=bank0: e.copy(
                                 out=dst[:, half * 4:(half + 1) * 4, :],
                                 in_=pb[bank0 + half][:].rearrange("p (a t) -> p a t", a=4))),
                             reads=[PB[bank0 + half]], writes=[Bdst])
                P.dma("sp", lambda e, vt=vt, i=i: e.dma_start(out=KTv[:, :, vt * 128:(vt + 1) * 128], in_=kts[i][:]),
                      reads=[Bkt[i]], writes=[B_KT[vt]])
                if vt >= 30:
                    qi = vt - 30
                    P.dma("sp", lambda e, qi=qi, i=i: e.dma_start(out=QTv[:, :, qi * 128:(qi + 1) * 128], in_=qts[i][:]),
                          reads=[Bqt[i]], writes=[B_QT[qi]])
                for half in range(2):
                    bank = 5 + half
                    for c in range(8):
                        P.op("pe", lambda e, half=half, c=c, bank=bank, i=i: e.matmul(
                            out=pb[bank][:], lhsT=hTs[i][:, c, :],
                            rhs=wq[:, c, 2 * D + half * 512: 2 * D + (half + 1) * 512],
                            start=(c == 0), stop=(c == 7)), reads=[B_wq, BhT[i]], writes=[PB[bank]])
                    P.op("dve" if half == 0 else "act",
                         (lambda e, half=half, bank=bank, i=i: e.tensor_copy(out=vts[i][:, half * 512:(half + 1) * 512], in_=pb[bank][:]))
                         if half == 0 else
                         (lambda e, half=half, bank=bank, i=i: e.copy(out=vts[i][:, half * 512:(half + 1) * 512], in_=pb[bank][:])),
                         reads=[PB[bank]], writes=[Bvt[i]])
                P.dma("sp", lambda e, vt=vt, i=i: e.dma_start(out=VS[vt * 128:(vt + 1) * 128, :], in_=vts[i][:]),
                      reads=[Bvt[i]], writes=[B_VS[vt]])
        P.barrier()

        bk = t5_runs()
        with contextlib.ExitStack() as st:
            rb = sb("rb", [NH, 32], F32, st)
            tt = sb("tt", [NH, 1024], F32, st)
            B_tt = Buf()
            P.dma("sp", lambda e: e.dma_start(out=rb[:], in_=rel_bias), writes=[B_tt])
            P.op("dve", lambda e: e.memset(tt[:, 0:256], -NEGB / 8.0), reads=[B_tt], writes=[B_tt])
            P.op("dve", lambda e: e.tensor_copy(out=tt[:, 256:272], in_=rb[:, 0:16]), reads=[B_tt], writes=[B_tt])
            dlt = 16
            while dlt < 768:
                b_ = int(bk[dlt])
                e_ = dlt
                while e_ < 768 and int(bk[e_]) == b_:
                    e_ += 1
                P.op("dve", lambda e, dlt=dlt, e_=e_, b_=b_: e.tensor_copy(
                    out=tt[:, 256 + dlt:256 + e_], in_=rb[:, b_:b_ + 1].to_broadcast([NH, e_ - dlt])),
                    reads=[B_tt], writes=[B_tt])
                dlt = e_
            P.op("dve", lambda e: e.tensor_scalar(out=tt[:], in0=tt[:], scalar1=8.0, scalar2=None, op0=ALU.mult),
                 reads=[B_tt], writes=[B_tt])
            B_TT = Buf()
            P.dma("sp", lambda e: e.dma_start(out=TT, in_=tt[:].unsqueeze(1).to_broadcast([NH, 128, 1024])), reads=[B_tt], writes=[B_TT])

            kaug2 = [sb("kaug%d" % i, [96, SEQ], BF16, st) for i in range(2)]
            B_ka2 = [Buf() for _ in range(2)]
            B_boh = Buf()
            for i_ in range(2):
                P.dma("sp", lambda e, i_=i_: e.dma_start(out=kaug2[i_][64:96, :], in_=boh_b), reads=[B_w["boh"]], writes=[B_boh])
            vaug2 = [sb("vaug%d" % i, [128, 64, 65], BF16, st) for i in range(2)]
            B_va2 = [Buf() for _ in range(2)]
            for i_ in range(2):
                P.op("pool", lambda e, i_=i_: e.memset(vaug2[i_][:], 1.0), writes=[B_va2[i_]])
            qaug2 = [sb("qaug%d" % i, [96, NQT * 128], BF16, st) for i in range(2)]
            B_qa2 = [Buf() for _ in range(2)]
            B_qm2 = [[Buf() for _ in range(NG)] for _ in range(2)]
            btab2 = [sb("btab%d" % i, [128, 2, 2, 256], F32, st) for i in range(2)]
            B_bt2 = [Buf() for _ in range(2)]
            km = sb("km", [64, NB], F32, st)
            kmb = sb("kmb", [64, NB], BF16, st)
            B_km = Buf()
            pmr, B_pm = load_rep(st, "pmr", pm_in, NG * NB)
            a1r, B_a1 = load_rep(st, "a1r", a1_in, NG * NB)
            a2r, B_a2 = load_rep(st, "a2r", a2_in, NG * NB)
            a3r, B_a3 = load_rep(st, "a3r", a3_in, NG * NB)
            rb31 = sb("rb31", [128, NH], F32, st)
            B_rb31 = Buf()
            P.dma("sp", lambda e: e.dma_start(out=rb31[:], in_=rel_bias[:, 31:32].rearrange("h o -> o h").to_broadcast([128, NH]), allow_slow_non_contiguous=True),
                  writes=[B_rb31])
            mulv = sb("mulv", [128, NG * NB], F32, st)
            B_mulv = Buf()
            gm = [sb("gm%d" % i, [128, 2, NB], F32, st) for i in range(2)]
            mx = [sb("mx%d" % i, [128, 2, 8], F32, st) for i in range(2)]
            sel = [sb("sel%d" % i, [128, 2, 96], F32, st) for i in range(2)]
            B_g = [Buf() for _ in range(2)]
            for i_ in range(2):
                P.op("dve", lambda e, i_=i_: e.memset(sel[i_][:], 0.0), writes=[B_g[i_]])
            stmp = [sb("stmp%d" % i, [128, 512], F32, st) for i in range(2)]
            B_stmp = [Buf() for _ in range(2)]
            pT = [sb("pT%d" % i, [128, 512], BF16, st) for i in range(3)]
            B_pT = [Buf() for _ in range(3)]
            oall = sb("oall", [128, NQT, HD], F32, st)
            B_oall = Buf()
            rec = [sb("rec%d" % i, [128, 2, 1], F32, st) for i in range(2)]
            B_rec = [Buf() for _ in range(2)]
            B_OA = Buf()
            OAv = OA.rearrange("(t p) (h d) -> p t h d", p=128, h=NH)
            VSv = VS.rearrange("(c p) (h d) -> p c h d", p=128, h=NH)
            def emit_loads(h):
                p_ = h % 2
                P.dma("sp", lambda e, h=h, p_=p_: e.dma_start(out=kaug2[p_][0:64, :], in_=KT[h]), reads=B_KT, writes=[B_ka2[p_]])
                for cq in range(4):
                    P.dma("sp", lambda e, h=h, cq=cq, p_=p_: e.dma_start(out=vaug2[p_][:, cq * 16:(cq + 1) * 16, 0:64],
                                                                         in_=VSv[:, cq * 16:(cq + 1) * 16, h, :]),
                          reads=B_VS, writes=[B_va2[p_]])
                P.dma("sp", lambda e, h=h, p_=p_: e.dma_start(out=qaug2[p_][0:64, :], in_=QT[h]), reads=B_QT, writes=[B_qa2[p_]])
                for which in range(2):
                    for kc in range(2):
                        def bsrc(h=h, which=which, kc=kc):
                            base = TT[h, 0:1, 256 * (1 + which) - kc * 128: 256 * (1 + which) - kc * 128 + 256]
                            return bass.AP(tensor=base.tensor, offset=base.offset, ap=[[1023, 128], [1, 256]])
                        P.dma("sp", lambda e, which=which, kc=kc, bsrc=bsrc, p_=p_: e.dma_start(out=btab2[p_][:, which, kc, :], in_=bsrc()),
                              reads=[B_TT], writes=[B_bt2[p_]])

            def head_body(h, kaug, vaug, qaug, btab, B_ka, B_va, B_qa, B_qm, B_bt):
                P.op("dve", lambda e: e.tensor_reduce(out=km[:, :], in_=kaug[0:64, :].rearrange("p (n k) -> p n k", k=BLK),
                                                      axis=AX.X, op=ALU.add), reads=[B_ka], writes=[B_km])
                P.op("dve", lambda e: e.tensor_scalar(out=kmb[:, :], in0=km[:, :], scalar1=1.0 / BLK, scalar2=None,
                                                      op0=ALU.mult), reads=[B_km], writes=[B_km])
                P.op("dve", lambda e, h=h: e.scalar_tensor_tensor(out=mulv[:], in0=a2r[:], scalar=rb31[:, h:h + 1], in1=a1r[:],
                                                                  op0=ALU.mult, op1=ALU.add),
                     reads=[B_a1, B_a2, B_rb31], writes=[B_mulv])
                def prologue(g, h=h):
                    own = 15 + g
                    gi = g % 2
                    for t in range(2):
                        P.op("pe", lambda e, g=g, t=t: e.matmul(
                            out=pb[7][:, t * NB:(t + 1) * NB], lhsT=qaug[0:64, (2 * g + t) * 128:(2 * g + t + 1) * 128],
                            rhs=kmb[:, :], start=True, stop=True), reads=[B_qa, B_km], writes=[PB[7]])
                    P.op("dve", lambda e, g=g, gi=gi: e.tensor_tensor(
                        out=gm[gi][:], in0=pb[7][:, 0:2 * NB].rearrange("p (t n) -> p t n", t=2),
                        in1=pmr[:, g * NB:(g + 1) * NB].unsqueeze(1).to_broadcast([128, 2, NB]), op=ALU.add),
                        reads=[PB[7], B_pm], writes=[B_g[gi]])
                    for t in range(2):
                        P.op("dve", lambda e, gi=gi, t=t: e.max(out=mx[gi][:, t, :], in_=gm[gi][:, t, :]),
                             reads=[B_g[gi]], writes=[B_g[gi]])
                    for t in range(2):
                        P.op("dve", lambda e, gi=gi, t=t: e.tensor_scalar(
                            out=sel[gi][:, t, 64:96], in0=gm[gi][:, t, :], scalar1=mx[gi][:, t, 2:3], scalar2=None, op0=ALU.is_ge),
                            reads=[B_g[gi]], writes=[B_g[gi]])
                    P.op("dve", lambda e, gi=gi: e.scalar_tensor_tensor(
                        out=sel[gi][:, :, 64:96], in0=gm[gi][:], scalar=-1e29, in1=sel[gi][:, :, 64:96], op0=ALU.is_gt, op1=ALU.mult),
                        reads=[B_g[gi]], writes=[B_g[gi]])
                    P.op("dve", lambda e, gi=gi, g=g: e.tensor_tensor(
                        out=sel[gi][:, :, 64:96], in0=sel[gi][:, :, 64:96],
                        in1=mulv[:, g * NB:(g + 1) * NB].unsqueeze(1).to_broadcast([128, 2, NB]), op=ALU.mult),
                        reads=[B_g[gi], B_mulv], writes=[B_g[gi]])
                    P.op("dve", lambda e, gi=gi, g=g: e.tensor_tensor(
                        out=sel[gi][:, :, 64:96], in0=sel[gi][:, :, 64:96],
                        in1=a3r[:, g * NB:(g + 1) * NB].unsqueeze(1).to_broadcast([128, 2, NB]), op=ALU.add),
                        reads=[B_g[gi], B_a3], writes=[B_g[gi]])

                def prologue2(g):
                    gi = g % 2
                    for t in range(2):
                        P.op("pe", lambda e, gi=gi, t=t: e.transpose(out=pb[7][0:96, 128 + t * 128:128 + (t + 1) * 128],
                                                                     in_=sel[gi][:, t, :], identity=idf[:]),
                             reads=[B_g[gi], B_id], writes=[PB[7]])
                    P.op("act", lambda e, g=g: e.copy(out=qaug[64:96, g * 256:(g + 1) * 256], in_=pb[7][64:96, 128:384]),
                         reads=[PB[7]], writes=[B_qm[g]])

                iters = [(g, n) for g in range(NG) for n in range(15 + g + 1)]

                def stageA(idx):
                    g, n = iters[idx]
                    own = 15 + g
                    sbk = idx % 3
                    pi = idx % 3
                    for c in range(2):
                        P.op("pe", lambda e, n=n, c=c, g=g, sbk=sbk: e.matmul(
                            out=pb[sbk][:, c * 256:(c + 1) * 256], lhsT=kaug[0:96, (2 * n + c) * 128:(2 * n + c + 1) * 128],
                            rhs=qaug[0:96, g * 256:(g + 1) * 256], start=True, stop=True),
                            reads=[B_ka, B_boh, B_qa, B_qm[g]], writes=[PB[sbk]])
                    if n >= own - 1:
                        which = 0 if n == own else 1
                        si = n % 2
                        P.op("dve", lambda e, sbk=sbk, which=which, si=si: e.tensor_tensor(
                            out=stmp[si][:], in0=pb[sbk][:], in1=btab[:, which, :, :].rearrange("p c q -> p (c q)"), op=ALU.add),
                            reads=[PB[sbk], B_bt], writes=[B_stmp[si]])
                        P.op("act", lambda e, si=si, pi=pi: e.activation(out=pT[pi][:], in_=stmp[si][:], func=AF.Exp, scale=0.125),
                             reads=[B_stmp[si]], writes=[B_pT[pi]])
                    else:
                        P.op("act", lambda e, sbk=sbk, pi=pi: e.activation(out=pT[pi][:], in_=pb[sbk][:], func=AF.Exp, scale=0.125),
                             reads=[PB[sbk]], writes=[B_pT[pi]])

                def stageB(idx):
                    g, n = iters[idx]
                    own = 15 + g
                    gi = g % 2
                    pi = idx % 3
                    ob = 3 + 2 * (g % 2)
                    for c in range(2):
                        for t in range(2):
                            P.op("pe", lambda e, n=n, c=c, t=t, pi=pi, ob=ob, own=own: e.matmul(
                                out=pb[ob + t][:, 0:65], lhsT=pT[pi][:, c * 256 + t * 128:c * 256 + (t + 1) * 128],
                                rhs=vaug[:, 2 * n + c, :], start=(n == 0 and c == 0), stop=(n == own and c == 1)),
                                reads=[B_pT[pi], B_va], writes=[PB[ob + t]])
                    if n == own:
                        for t in range(2):
                            P.op("dve", lambda e, ob=ob, gi=gi, t=t: e.reciprocal(out=rec[gi][:, t, :], in_=pb[ob + t][:, 64:65]),
                                 reads=[PB[ob + t]], writes=[B_rec[gi]])
                            P.op("dve", lambda e, ob=ob, gi=gi, g=g, t=t: e.tensor_scalar(
                                out=oall[:, 2 * g + t, :], in0=pb[ob + t][:, 0:64], scalar1=rec[gi][:, t, 0:1], scalar2=None, op0=ALU.mult),
                                reads=[PB[ob + t], B_rec[gi]], writes=[B_oall])

                prologue(0)
                prologue2(0)
                stageA(0)
                for idx in range(len(iters)):
                    if idx + 1 < len(iters):
                        stageA(idx + 1)
                    g_, n_ = iters[idx]
                    if n_ == 0 and g_ + 1 < NG:
                        prologue(g_ + 1)
                    if n_ == 8 and g_ + 1 < NG:
                        prologue2(g_ + 1)
                    stageB(idx)
                P.dma("sp", lambda e, h=h: e.dma_start(out=OAv[:, :, h, :], in_=oall[:]), reads=[B_oall], writes=[B_OA])

            emit_loads(0)
            for h in range(NH):
                p_ = h % 2
                if h + 1 < NH:
                    emit_loads(h + 1)
                head_body(h, kaug2[p_], vaug2[p_], qaug2[p_], btab2[p_], B_ka2[p_], B_va2[p_], B_qa2[p_], B_qm2[p_], B_bt2[p_])
        P.barrier()

        B_X1 = [Buf() for _ in range(NT0)]
        with contextlib.ExitStack() as st:
            wo = sb("wo_sb", [128, 8, D], BF16, st)
            B_wo = Buf()
            P.dma("sp", lambda e: e.dma_start(out=wo[:], in_=wo_b.rearrange("(c p) n -> p c n", p=128)),
                  reads=[B_w["wo"]], writes=[B_wo])
            ot = [sb("p3o%d" % i, [128, D], F32, st) for i in range(2)]
            obf = [sb("p3ob%d" % i, [128, D], BF16, st) for i in range(2)]
            oT = [sb("p3oT%d" % i, [128, 8, 128], BF16, st) for i in range(2)]
            xt3 = [sb("p3x%d" % i, [128, D], F32, st) for i in range(2)]
            B3 = [[Buf() for _ in range(4)] for _ in range(2)]
            for ti in range(NT0):
                i = ti % 2
                Bo, Bob, BoT, Bx = B3[i]
                P.dma("sp", lambda e, ti=ti, i=i: e.dma_start(out=ot[i][:], in_=OA[(ti + 1) * 128:(ti + 2) * 128, :]),
                      reads=[B_OA], writes=[Bo])
                P.dma("sp", lambda e, ti=ti, i=i: e.dma_start(out=xt3[i][:], in_=xv[(31 + ti) * 128:(32 + ti) * 128, :]),
                      writes=[Bx])
                P.op("act", lambda e, i=i: e.copy(out=obf[i][:], in_=ot[i][:]), reads=[Bo], writes=[Bob])
                tpv = pb[0][:].bitcast(BF16)
                for c in range(8):
                    P.op("pe", lambda e, c=c, i=i, tpv=tpv: e.transpose(out=tpv[:, c * 128:(c + 1) * 128],
                                                                       in_=obf[i][:, c * 128:(c + 1) * 128], identity=idb[:]),
                         reads=[Bob, B_id], writes=[PB[0]])
                P.op("act", lambda e, i=i, tpv=tpv: e.copy(out=oT[i][:], in_=tpv.rearrange("p (c t) -> p c t", c=8)),
                     reads=[PB[0]], writes=[BoT])
                for half in range(2):
                    bank = 1 + 2 * i + half
                    for c in range(8):
                        P.op("pe", lambda e, c=c, i=i, half=half, bank=bank: e.matmul(
                            out=pb[bank][:], lhsT=oT[i][:, c, :], rhs=wo[:, c, half * 512:(half + 1) * 512],
                            start=(c == 0), stop=(c == 7)), reads=[BoT, B_wo], writes=[PB[bank]])
                    P.op("dve", lambda e, i=i, half=half, bank=bank: e.tensor_tensor(
                        out=xt3[i][:, half * 512:(half + 1) * 512], in0=pb[bank][:], in1=xt3[i][:, half * 512:(half + 1) * 512],
                        op=ALU.add), reads=[PB[bank], Bx], writes=[Bx])
                P.dma("sp", lambda e, ti=ti, i=i: e.dma_start(out=X1[ti * 128:(ti + 1) * 128, :], in_=xt3[i][:]),
                      reads=[Bx], writes=[B_X1[ti]])
        P.barrier()

        def peer_phase(l, Xin, B_in, ntiles, Xout, B_out, out_row0, final_norm):
            groups = []
            t0 = 0
            if ntiles % 4 == 1:
                groups.append((0, 1))
                t0 = 1
            while t0 < ntiles:
                groups.append((t0, 4))
                t0 += 4
            with contextlib.ExitStack() as st:
                gf, Bgf = load_rep(st, "pgf%d" % l, norm_ffn[l:l + 1, :])
                if final_norm:
                    gfin, Bgfin = load_rep(st, "pgfin", norm_final[0:1, :])
                skt = sb("skt%d" % l, [128, 16, 128], BF16, st)
                B_skt = Buf()
                P.dma("sp", lambda e: e.dma_start(out=skt[:], in_=skT_b[l].rearrange("a k n -> k a n")),
                      reads=[B_w["skT"]], writes=[B_skt])
                hT = sb("phT%d" % l, [128, 8, 512], BF16, st)
                B_hT = Buf()
                s_sb = sb("ps%d" % l, [128, 4, 16, 128], F32, st)
                B_s = [Buf() for _ in range(4)]
                acc = sb("pacc%d" % l, [128, 4, D], F32, st)
                B_acc = [Buf() for _ in range(4)]
                th = sb("pth%d" % l, [128, 4, 8], F32, st)
                eb = sb("peb%d" % l, [128, 4, 8], F32, st)
                B_st = [Buf() for _ in range(4)]
                for (gt0, gn) in groups:
                    Tg = gn * 128
                    with contextlib.ExitStack() as sa:
                        wpq = sb("wpq%d_%d" % (l, gt0), [128, 8, 2048], BF16, sa)
                        B_wpq = Buf()
                        for c in range(8):
                            P.dma("sp", lambda e, c=c: e.dma_start(out=wpq[:, c, :], in_=wpq_b[l][c * 128:(c + 1) * 128, :]),
                                  reads=[B_w["wpq"]], writes=[B_wpq])
                        nbp = norm_bufs(sa, "pn%d_%d" % (l, gt0))
                        qT = sb("pqT%d_%d" % (l, gt0), [128, 16, 512], BF16, sa)
                        B_qT = Buf()
                        v16 = sb("pv16%d_%d" % (l, gt0), [128, 2, 16], F32, sa)
                        t128 = sb("pt128%d_%d" % (l, gt0), [128, 128], F32, sa)
                        cand = sb("pcand%d_%d" % (l, gt0), [128, 256], F32, sa)
                        cand2 = sb("pcand2%d_%d" % (l, gt0), [128, 256], F32, sa)
                        b16 = sb("pb16%d_%d" % (l, gt0), [128, 8, 16], F32, sa)
                        e16 = sb("pe16%d_%d" % (l, gt0), [128, 8, 16], F32, sa)
                        zz = sb("pzz%d_%d" % (l, gt0), [128, 8], F32, sa)
                        B_tk = Buf()
                        for j in range(gn):
                            ti = gt0 + j
                            P.dma("sp", lambda e, ti=ti, j=j: e.dma_start(out=acc[:, j, :], in_=Xin[ti * 128:(ti + 1) * 128, :]),
                                  reads=[B_in[ti]], writes=[B_acc[j]])
                            norm_T(sa, "pp", acc[:, j, :], B_acc[j], gf[:], Bgf, hT[:, :, j * 128:(j + 1) * 128], B_hT, 0, nbp)
                        for hc in range(16):
                            bank = 1 + hc % 2
                            for c in range(8):
                                P.op("pe", lambda e, hc=hc, c=c, bank=bank, Tg=Tg: e.matmul(
                                    out=pb[bank][:, 0:Tg], lhsT=wpq[:, c, hc * 128:(hc + 1) * 128], rhs=hT[:, c, 0:Tg],
                                    start=(c == 0), stop=(c == 7)), reads=[B_wpq, B_hT], writes=[PB[bank]])
                            P.op("act", lambda e, hc=hc, bank=bank, Tg=Tg: e.copy(out=qT[:, hc, 0:Tg], in_=pb[bank][:, 0:Tg]),
                                 reads=[PB[bank]], writes=[B_qT])
                        for j in range(gn):
                            for q4 in range(4):
                                bank = 3 + q4 % 2
                                for k in range(4):
                                    hc = q4 * 4 + k
                                    P.op("pe", lambda e, hc=hc, k=k, j=j, bank=bank: e.matmul(
                                        out=pb[bank][:, k * 128:(k + 1) * 128], lhsT=qT[:, hc, j * 128:(j + 1) * 128],
                                        rhs=skt[:, hc, :], start=True, stop=True), reads=[B_qT, B_skt], writes=[PB[bank]])
                                P.op("act", lambda e, j=j, q4=q4, bank=bank: e.copy(
                                    out=s_sb[:, j, q4 * 4:(q4 + 1) * 4, :], in_=pb[bank][:].rearrange("p (a n) -> p a n", a=4)),
                                    reads=[PB[bank]], writes=[B_s[j]])
                        for j in range(gn):
                            for hh in range(8):
                                for c2 in range(2):
                                    src = s_sb[:, j, 2 * hh + c2, :]
                                    P.op("dve", lambda e, src=src, c2=c2: e.max(out=v16[:, c2, 0:8], in_=src),
                                         reads=[B_s[j]], writes=[B_tk])
                                    P.op("dve", lambda e, src=src, c2=c2: e.match_replace(
                                        out=t128[:], in_to_replace=v16[:, c2, 0:8], in_values=src, imm_value=-1e30),
                                        reads=[B_s[j], B_tk], writes=[B_tk])
                                    P.op("dve", lambda e, c2=c2: e.max(out=v16[:, c2, 8:16], in_=t128[:]),
                                         reads=[B_tk], writes=[B_tk])
                                P.op("dve", lambda e: e.tensor_tensor(
                                    out=cand[:].rearrange("p (a b) -> p a b", a=16),
                                    in0=v16[:, 0, :].unsqueeze(2).to_broadcast([128, 16, 16]),
                                    in1=v16[:, 1, :].unsqueeze(1).to_broadcast([128, 16, 16]), op=ALU.add),
                                    reads=[B_tk], writes=[B_tk])
                                P.op("dve", lambda e, hh=hh: e.max(out=b16[:, hh, 0:8], in_=cand[:]), reads=[B_tk], writes=[B_tk])
                                P.op("dve", lambda e, hh=hh: e.match_replace(
                                    out=cand2[:], in_to_replace=b16[:, hh, 0:8], in_values=cand[:], imm_value=-1e30),
                                    reads=[B_tk], writes=[B_tk])
                                P.op("dve", lambda e, hh=hh: e.max(out=b16[:, hh, 8:16], in_=cand2[:]), reads=[B_tk], writes=[B_tk])
                            P.op("dve", lambda e, j=j: e.tensor_copy(out=th[:, j, :], in_=b16[:, :, 15]), reads=[B_tk], writes=[B_st[j]])
                            P.op("dve", lambda e: e.tensor_tensor(out=e16[:], in0=b16[:], in1=b16[:, :, 0:1].to_broadcast([128, 8, 16]),
                                                                  op=ALU.subtract), reads=[B_tk], writes=[B_tk])
                            P.op("act", lambda e: e.activation(out=e16[:], in_=e16[:], func=AF.Exp), reads=[B_tk], writes=[B_tk])
                            P.op("dve", lambda e: e.tensor_reduce(out=zz[:], in_=e16[:], axis=AX.X, op=ALU.add), reads=[B_tk], writes=[B_tk])
                            P.op("act", lambda e: e.activation(out=zz[:], in_=zz[:], func=AF.Ln), reads=[B_tk], writes=[B_tk])
                            P.op("dve", lambda e, j=j: e.scalar_tensor_tensor(
                                out=eb[:, j, :], in0=b16[:, :, 0], scalar=-1.0, in1=zz[:], op0=ALU.mult, op1=ALU.subtract),
                                reads=[B_tk], writes=[B_st[j]])
                            P.op("dve", lambda e, j=j: e.tensor_tensor(
                                out=s_sb[:, j, :, :].rearrange("p (h c) n -> p h c n", c=2)[:, :, 0, :],
                                in0=s_sb[:, j, :, :].rearrange("p (h c) n -> p h c n", c=2)[:, :, 0, :],
                                in1=eb[:, j, :].unsqueeze(2).to_broadcast([128, 8, 128]), op=ALU.add),
                                reads=[B_st[j]], writes=[B_s[j]])
                            P.op("dve", lambda e, j=j: e.tensor_tensor(out=th[:, j, :], in0=th[:, j, :], in1=eb[:, j, :], op=ALU.add),
                                 reads=[B_st[j]], writes=[B_st[j]])
                            P.op("act", lambda e, j=j: e.activation(out=th[:, j, :], in_=th[:, j, :], func=AF.Exp),
                                 reads=[B_st[j]], writes=[B_st[j]])
                            P.op("dve", lambda e, j=j: e.tensor_scalar(out=th[:, j, :], in0=th[:, j, :], scalar1=1.0 - 1e-5, scalar2=None,
                                                                       op0=ALU.mult), reads=[B_st[j]], writes=[B_st[j]])
                    P.barrier()
                    with contextlib.ExitStack() as se:
                        utg = [sb("utg%d_%d_%d" % (l, gt0, i), [128, 8, 1024], BF16, se) for i in range(2)]
                        vg = [sb("vg%d_%d_%d" % (l, gt0, i), [128, 8, D], BF16, se) for i in range(2)]
                        B_ut = [Buf() for _ in range(2)]
                        B_vg = [Buf() for _ in range(2)]
                        gA = [sb("gA%d_%d_%d" % (l, gt0, i), [128, 8, 512], BF16, se) for i in range(2)]
                        B_gA = [[Buf() for _ in range(8)] for _ in range(2)]
                        cc = [sb("cc%d_%d_%d" % (l, gt0, i), [128, 8, 128], F32, se) for i in range(2)]
                        ee = [sb("ee%d_%d_%d" % (l, gt0, i), [128, 8, 128], F32, se) for i in range(3)]
                        wm = [sb("wm%d_%d_%d" % (l, gt0, i), [128, 8, 128], BF16, se) for i in range(3)]
                        B_cc = [Buf() for _ in range(3)]
                        B_ee = [Buf() for _ in range(3)]
                        B_wm = [Buf() for _ in range(3)]
                        GT = [sb("GT%d_%d_%d" % (l, gt0, i), [128, 8, 128], BF16, se) for i in range(2)]
                        B_GT = [Buf() for _ in range(2)]
                        uTv = uT_b[l].rearrange("(c p) e -> p c e", p=128)
                        pvv = pv_b[l].rearrange("(i j) d -> j i d", j=128)

                        def load_eg(eg):
                            i = eg % 2
                            for c in range(0, 8, 4):
                                P.dma("sp", lambda e, eg=eg, i=i, c=c: e.dma_start(
                                    out=utg[i][:, c:c + 4, :], in_=uTv[:, c:c + 4, eg * 1024:(eg + 1) * 1024]),
                                    reads=[B_uT[l][eg]], writes=[B_ut[i]])
                                P.dma("sp", lambda e, eg=eg, i=i, c=c: e.dma_start(
                                    out=vg[i][:, c:c + 4, :], in_=pvv[:, eg * 8 + c:eg * 8 + c + 4, :]),
                                    reads=[B_pv[l][eg]], writes=[B_vg[i]])

                        zt = sb("zt%d_%d" % (l, gt0), [128, 512], F32, se)
                        B_zt = Buf()
                        P.op("pool", lambda e: e.memset(zt[:], 0.0), writes=[B_zt])
                        bgq = []

                        pend = []

                        def pump(k):
                            for _ in range(k):
                                while pend:
                                    pend.pop(0)()
                                if bgq:
                                    ep = bgq.pop(0)()
                                    if ep is not None:
                                        pend.append(ep)
                            if not bgq:
                                while pend:
                                    pend.pop(0)()

                        def stage1_tasks(eg):
                            i = eg % 2
                            tasks = []
                            for ch in range(8):
                                for part in range(2):
                                    def task(ch=ch, i=i, part=part):
                                        bank = ch % 2
                                        for c in range(part * 4, part * 4 + 4):
                                            P.op("pe", lambda e, ch=ch, c=c, i=i, bank=bank: e.matmul(
                                                out=pb[bank][:, 0:Tg], lhsT=utg[i][:, c, ch * 128:(ch + 1) * 128], rhs=hT[:, c, 0:Tg],
                                                start=(c == 0), stop=(c == 7)), reads=[B_ut[i], B_hT], writes=[PB[bank]])
                                        if part == 1:
                                            return lambda: P.op("act", lambda e, ch=ch, bank=bank, i=i: e.copy(
                                                out=gA[i][:, ch, 0:Tg], in_=pb[bank][:, 0:Tg]),
                                                reads=[PB[bank]], writes=[B_gA[i][ch]])
                                        return None
                                    tasks.append(task)
                            return tasks

                        def gv_tasks(j, gk, i):
                            tasks = []
                            for half in range(2):
                                for part in range(2):
                                    def task(half=half, j=j, gk=gk, i=i, part=part):
                                        bank = 6 + half
                                        for ch in range(part * 4, part * 4 + 4):
                                            P.op("pe", lambda e, ch=ch, gk=gk, i=i, half=half, bank=bank: e.matmul(
                                                out=pb[bank][:], lhsT=GT[gk][:, ch, :], rhs=vg[i][:, ch, half * 512:(half + 1) * 512],
                                                start=(ch == 0), stop=(ch == 7)), reads=[B_GT[gk], B_vg[i]], writes=[PB[bank]])
                                        if part == 1:
                                            return lambda: P.op("dve", lambda e, j=j, half=half, bank=bank: e.tensor_tensor(
                                                out=acc[:, j, half * 512:(half + 1) * 512], in0=pb[bank][:],
                                                in1=acc[:, j, half * 512:(half + 1) * 512], op=ALU.add),
                                                reads=[PB[bank]], writes=[B_acc[j]])
                                        return None
                                    tasks.append(task)
                            return tasks

                        load_eg(0)
                        for t_ in stage1_tasks(0):
                            ep_ = t_()
                            if ep_ is not None:
                                ep_()
                        hcnt = 0
                        for eg in range(16):
                            i = eg % 2
                            if eg + 1 < 16 and gn < 4:
                                load_eg(eg + 1)
                                bgq.extend(stage1_tasks(eg + 1))
                            P.op("act", lambda e, i=i: e.activation(out=gA[i][:, :, 0:Tg], in_=gA[i][:, :, 0:Tg], func=AF.Gelu_apprx_tanh),
                                 reads=B_gA[i], writes=B_gA[i])
                            for j in range(gn):
                                wb0 = 2 + 2 * (j % 2)
                                for hh in range(8):
                                    k = hcnt % 3
                                    hcnt += 1
                                    if hh % 2 == 0:
                                        for i8 in range(8):
                                            P.op("act", lambda e, j=j, hh=hh, k=k, eg=eg, i8=i8: e.activation(
                                                out=ee[k][:, i8, :], in_=s_sb[:, j, 2 * hh + 1, :], func=AF.Exp,
                                                bias=s_sb[:, j, 2 * hh, eg * 8 + i8:eg * 8 + i8 + 1]),
                                                reads=[B_s[j]], writes=([B_ee[k]] if i8 in (0, 7) else []))
                                    else:
                                        kc = (hcnt // 2) % 2
                                        P.op("pool", lambda e, j=j, hh=hh, kc=kc, eg=eg: e.tensor_tensor(
                                            out=cc[kc][:], in0=s_sb[:, j, 2 * hh, eg * 8:(eg + 1) * 8].unsqueeze(2).to_broadcast([128, 8, 128]),
                                            in1=s_sb[:, j, 2 * hh + 1, :].unsqueeze(1).to_broadcast([128, 8, 128]), op=ALU.add),
                                            reads=[B_s[j]], writes=[B_cc[kc]])
                                        P.op("act", lambda e, k=k, kc=kc: e.activation(out=ee[k][:], in_=cc[kc][:], func=AF.Exp),
                                             reads=[B_cc[kc]], writes=[B_ee[k]])
                                    P.op("dve", lambda e, j=j, hh=hh, k=k: e.scalar_tensor_tensor(
                                        out=wm[k][:], in0=ee[k][:], scalar=th[:, j, hh:hh + 1], in1=ee[k][:],
                                        op0=ALU.is_ge, op1=ALU.mult), reads=[B_ee[k], B_st[j]], writes=[B_wm[k]])
                                    for ch in range(8):
                                        bank = wb0 + ch // 4
                                        P.op("pe", lambda e, ch=ch, k=k, bank=bank, hh=hh: e.matmul(
                                            out=pb[bank][:, (ch % 4) * 128:(ch % 4 + 1) * 128], lhsT=wm[k][:, ch, :], rhs=idb[:],
                                            start=(hh == 0 and ch % 4 == 0), stop=(hh == 7), skip_group_check=True), reads=[B_wm[k], B_id], writes=[PB[bank]])
                                    pump(1)
                                gk = j % 2
                                for half in range(2):
                                    P.op("dve", lambda e, j=j, gk=gk, half=half, wb0=wb0, i=i: e.tensor_tensor(
                                        out=GT[gk][:, half * 4:(half + 1) * 4, :],
                                        in0=pb[wb0 + half][:].rearrange("p (a t) -> p a t", a=4),
                                        in1=gA[i][:, half * 4:(half + 1) * 4, j * 128:(j + 1) * 128], op=ALU.mult),
                                        reads=[PB[wb0 + half]] + B_gA[i][half * 4:(half + 1) * 4], writes=[B_GT[gk]])
                                bgq.extend(gv_tasks(j, gk, i))
                                if gn == 4 and j == 0 and eg + 1 < 16:
                                    assert len(bgq) <= 4, len(bgq)
                                    load_eg(eg + 1)
                                    bgq.extend(stage1_tasks(eg + 1))
                            if gn < 4 or eg == 15:
                                pump(len(bgq))
                            else:
                                while pend:
                                    pend.pop(0)()
                        if final_norm:
                            fsq = sb("fsq%d" % gt0, [128, D], F32, se)
                            fss = sb("fss%d" % gt0, [128, 1], F32, se)
                            B_f = Buf()
                        for j in range(gn):
                            ti = gt0 + j
                            if final_norm:
                                P.op("act", lambda e, j=j: e.activation(out=fsq[:], in_=acc[:, j, :], func=AF.Square, accum_out=fss[:]),
                                     reads=[B_acc[j]], writes=[B_f])
                                P.op("dve", lambda e: e.tensor_scalar(out=fss[:], in0=fss[:], scalar1=1.0 / D, scalar2=EPS,
                                                                      op0=ALU.mult, op1=ALU.add), reads=[B_f], writes=[B_f])
                                P.op("act", lambda e: e.activation(out=fss[:], in_=fss[:], func=AF.Sqrt), reads=[B_f], writes=[B_f])
                                P.op("dve", lambda e: e.reciprocal(out=fss[:], in_=fss[:]), reads=[B_f], writes=[B_f])
                                P.op("dve", lambda e, j=j: e.scalar_tensor_tensor(
                                    out=acc[:, j, :], in0=acc[:, j, :], scalar=fss[:, 0:1], in1=gfin[:], op0=ALU.mult, op1=ALU.mult),
                                    reads=[B_f, Bgfin], writes=[B_acc[j]])
                            P.dma("sp", lambda e, ti=ti, j=j: e.dma_start(
                                out=Xout[(out_row0 + ti) * 128:(out_row0 + ti + 1) * 128, :], in_=acc[:, j, :]),
                                reads=[B_acc[j]], writes=[B_out[out_row0 + ti]])
                    P.barrier()

        B_X2 = [Buf() for _ in range(NT0)]
        peer_phase(0, X1, B_X1, NT0, X2, B_X2, 0, False)

        B_X3 = [Buf() for _ in range(32)]
        with contextlib.ExitStack() as st:
            w1 = sb("w1_sb", [128, 8, 2 * D], BF16, st)
            w2 = sb("w2_sb", [128, 8, D], BF16, st)
            B_w1, B_w2 = Buf(), Buf()
            for c in range(8):
                P.dma("sp", lambda e, c=c: e.dma_start(out=w1[:, c, :], in_=wpw1_b[c * 128:(c + 1) * 128, :]),
                      reads=[B_w["wpw1"]], writes=[B_w1])
            P.dma("sp", lambda e: e.dma_start(out=w2[:], in_=wpw2_b.rearrange("(c p) n -> p c n", p=128)),
                  reads=[B_w["wpw2"]], writes=[B_w2])
            g1, Bg1 = load_rep(st, "g1", norm_mix[1:2, :])
            b2r, Bb2 = load_rep(st, "b2r", b_pw2[0:1, :])
            halo, B_halo = load_rep(st, "halo_sb", halo_in[0:1, :], 1)
            bp1 = sb("bp1", [128, 16], F32, st)
            wdw = sb("wdw", [128, 8, CW], F32, st)
            bdw = sb("bdw", [128, 8], F32, st)
            lng = sb("lng", [128, 8], F32, st)
            lnb = sb("lnb", [128, 8], F32, st)
            B_cp = Buf()
            P.dma("sp", lambda e: e.dma_start(out=bp1[:], in_=b_pw1), writes=[B_cp])
            P.dma("sp", lambda e: e.dma_start(out=wdw[:], in_=w_dwT.rearrange("(c p) k -> p c k", p=128)), writes=[B_cp])
            P.dma("sp", lambda e: e.dma_start(out=bdw[:], in_=b_dw), writes=[B_cp])
            P.dma("sp", lambda e: e.dma_start(out=lng[:], in_=ln_g), writes=[B_cp])
            P.dma("sp", lambda e: e.dma_start(out=lnb[:], in_=ln_b), writes=[B_cp])
            ones = sb("ones", [128, 128], F32, st)
            P.op("dve", lambda e: e.memset(ones[:], 1.0), writes=[B_cp])
            xg = sb("cxg", [128, 4, D], F32, st)
            B_xg = [Buf() for _ in range(4)]
            hTc = sb("chT", [128, 8, 512], BF16, st)
            B_hTc = Buf()
            nbc = norm_bufs(st, "cn")
            uTb = sb("cuT", [128, 8, 32 + 512], F32, st)
            B_u = [Buf() for _ in range(8)]
            sg = [sb("csg%d" % i, [128, 512], F32, st) for i in range(2)]
            B_sg = [Buf() for _ in range(2)]
            cv = sb("ccv", [128, 8, 512], F32, st)
            B_cv = [Buf() for _ in range(8)]
            sq2 = [sb("csq%d" % i, [128, 512], F32, st) for i in range(2)]
            B_sq2 = [Buf() for _ in range(2)]
            mean = sb("cmean", [128, 512], F32, st)
            var = sb("cvar", [128, 512], F32, st)
            B_mv = Buf()
            t1 = [sb("ct1%d" % i, [128, 512], F32, st) for i in range(2)]
            B_t1 = [Buf() for _ in range(2)]
            zT = sb("czT", [128, 8, 512], BF16, st)
            B_zT = [Buf() for _ in range(8)]
            groups = [(0, 1)] + [(1 + 4 * k, 4) for k in range(8)]
            for (gt0, gn) in groups:
                Tg = gn * 128
                for j in range(gn):
                    ti = gt0 + j
                    P.dma("sp", lambda e, ti=ti, j=j: e.dma_start(out=xg[:, j, :], in_=X2[ti * 128:(ti + 1) * 128, :]),
                          reads=[B_X2[ti]], writes=[B_xg[j]])
                    norm_T(st, "cc", xg[:, j, :], B_xg[j], g1[:], Bg1, hTc[:, :, j * 128:(j + 1) * 128], B_hTc, 0, nbc)
                if gt0 > 0:
                    prevT = 128 if gt0 == 1 else 512
                    for ch in range(8):
                        P.op("pool", lambda e, ch=ch, prevT=prevT: e.tensor_copy(out=uTb[:, ch, 0:32], in_=uTb[:, ch, prevT:prevT + 32]),
                             reads=[B_u[ch]], writes=[B_u[ch]])
                for ch in range(8):
                    bv_, bg_ = 1 + (ch % 2) * 2, 2 + (ch % 2) * 2
                    for (bank, col) in ((bv_, ch), (bg_, 8 + ch)):
                        for c in range(8):
                            P.op("pe", lambda e, c=c, bank=bank, col=col, Tg=Tg: e.matmul(
                                out=pb[bank][:, 0:Tg], lhsT=w1[:, c, col * 128:(col + 1) * 128], rhs=hTc[:, c, 0:Tg],
                                start=(c == 0), stop=(c == 7)), reads=[B_w1, B_hTc], writes=[PB[bank]])
                    si = ch % 2
                    P.op("act", lambda e, ch=ch, bg_=bg_, si=si, Tg=Tg: e.activation(
                        out=sg[si][:, 0:Tg], in_=pb[bg_][:, 0:Tg], func=AF.Sigmoid, bias=bp1[:, 8 + ch:9 + ch]),
                        reads=[PB[bg_], B_cp], writes=[B_sg[si]])
                    P.op("dve", lambda e, ch=ch, bv_=bv_, si=si, Tg=Tg: e.scalar_tensor_tensor(
                        out=uTb[:, ch, 32:32 + Tg], in0=pb[bv_][:, 0:Tg], scalar=bp1[:, ch:ch + 1], in1=sg[si][:, 0:Tg],
                        op0=ALU.add, op1=ALU.mult), reads=[PB[bv_], B_sg[si], B_cp], writes=[B_u[ch]])
                    if gt0 == 0:
                        P.op("dve", lambda e, ch=ch, Tg=Tg: e.tensor_scalar(
                            out=uTb[:, ch, 32:32 + Tg], in0=uTb[:, ch, 32:32 + Tg], scalar1=halo[:, 0:1], scalar2=None, op0=ALU.mult),
                            reads=[B_u[ch], B_halo], writes=[B_u[ch]])
                if gt0 == 0:
                    continue
                for ch in range(8):
                    eng = "dve"
                    P.op(eng, lambda e, ch=ch: e.tensor_scalar(
                        out=cv[:, ch, :], in0=uTb[:, ch, 2:514], scalar1=wdw[:, ch, 0:1], scalar2=bdw[:, ch:ch + 1],
                        op0=ALU.mult, op1=ALU.add), reads=[B_u[ch], B_cp], writes=[B_cv[ch]])
                    for k in range(1, CW):
                        P.op(eng, lambda e, ch=ch, k=k: e.scalar_tensor_tensor(
                            out=cv[:, ch, :], in0=uTb[:, ch, 2 + k:514 + k], scalar=wdw[:, ch, k:k + 1], in1=cv[:, ch, :],
                            op0=ALU.mult, op1=ALU.add), reads=[B_u[ch], B_cp, B_cv[ch]], writes=[B_cv[ch]])
                for ch in range(8):
                    si = ch % 2
                    P.op("act", lambda e, ch=ch, si=si: e.activation(out=sq2[si][:], in_=cv[:, ch, :], func=AF.Square),
                         reads=[B_cv[ch]], writes=[B_sq2[si]])
                    P.op("pe", lambda e, ch=ch: e.matmul(out=pb[5][:], lhsT=ones[:], rhs=cv[:, ch, :], start=(ch == 0), stop=(ch == 7)),
                         reads=[B_cv[ch], B_cp], writes=[PB[5]])
                    P.op("pe", lambda e, ch=ch, si=si: e.matmul(out=pb[6][:], lhsT=ones[:], rhs=sq2[si][:], start=(ch == 0), stop=(ch == 7)),
                         reads=[B_sq2[si], B_cp], writes=[PB[6]])
                P.op("dve", lambda e: e.tensor_scalar(out=mean[:], in0=pb[5][:], scalar1=1.0 / D, scalar2=None, op0=ALU.mult),
                     reads=[PB[5]], writes=[B_mv])
                P.op("dve", lambda e: e.tensor_tensor(out=var[:], in0=mean[:], in1=mean[:], op=ALU.mult), reads=[B_mv], writes=[B_mv])
                P.op("dve", lambda e: e.scalar_tensor_tensor(out=var[:], in0=pb[6][:], scalar=1.0 / D, in1=var[:],
                                                             op0=ALU.mult, op1=ALU.subtract), reads=[PB[6], B_mv], writes=[B_mv])
                P.op("dve", lambda e: e.tensor_scalar(out=var[:], in0=var[:], scalar1=EPS, scalar2=None, op0=ALU.add),
                     reads=[B_mv], writes=[B_mv])
                P.op("act", lambda e: e.activation(out=var[:], in_=var[:], func=AF.Sqrt), reads=[B_mv], writes=[B_mv])
                P.op("dve", lambda e: e.reciprocal(out=var[:], in_=var[:]), reads=[B_mv], writes=[B_mv])
                for ch in range(8):
                    si = ch % 2
                    P.op("dve", lambda e, ch=ch, si=si: e.tensor_tensor(out=t1[si][:], in0=cv[:, ch, :], in1=mean[:], op=ALU.subtract),
                         reads=[B_cv[ch], B_mv], writes=[B_t1[si]])
                    P.op("dve", lambda e, si=si: e.tensor_tensor(out=t1[si][:], in0=t1[si][:], in1=var[:], op=ALU.mult),
                         reads=[B_mv], writes=[B_t1[si]])
                    P.op("act", lambda e, ch=ch, si=si: e.activation(out=zT[:, ch, :], in_=t1[si][:], func=AF.Silu,
                                                                    bias=lnb[:, ch:ch + 1], scale=lng[:, ch:ch + 1]),
                         reads=[B_t1[si], B_cp], writes=[B_zT[ch]])
                for j in range(gn):
                    ti = gt0 + j
                    P.op("pool", lambda e, j=j: e.tensor_tensor(out=xg[:, j, :], in0=xg[:, j, :], in1=b2r[:], op=ALU.add),
                         reads=[Bb2], writes=[B_xg[j]])
                    for half in range(2):
                        bank = 3 + half if j % 2 == 0 else 1 + half
                        bank = 7 if half == 0 else 0
                        for c in range(8):
                            P.op("pe", lambda e, c=c, j=j, half=half, bank=bank: e.matmul(
                                out=pb[bank][:], lhsT=zT[:, c, j * 128:(j + 1) * 128], rhs=w2[:, c, half * 512:(half + 1) * 512],
                                start=(c == 0), stop=(c == 7)), reads=[B_zT[c], B_w2], writes=[PB[bank]])
                        P.op("dve", lambda e, j=j, half=half, bank=bank: e.tensor_tensor(
                            out=xg[:, j, half * 512:(half + 1) * 512], in0=pb[bank][:], in1=xg[:, j, half * 512:(half + 1) * 512],
                            op=ALU.add), reads=[PB[bank]], writes=[B_xg[j]])
                    P.dma("sp", lambda e, ti=ti, j=j: e.dma_start(out=X3[(ti - 1) * 128:ti * 128, :], in_=xg[:, j, :]),
                          reads=[B_xg[j]], writes=[B_X3[ti - 1]])
        P.barrier()

        B_Y = [Buf() for _ in range(32)]
        peer_phase(1, X3, B_X3, 32, y, B_Y, 0, True)
        P.emit(final_bufs=B_Y)
    P.close()
    return nc


_CACHE = {}


def host_tables(half):
    valid = np.ones(NB, np.float32) if half == 1 else (np.arange(NB) >= 16).astype(np.float32)
    pm = np.zeros((NG, NB), np.float32)
    a1 = np.zeros((NG, NB), np.float32)
    a2 = np.zeros((NG, NB), np.float32)
    for g in range(NG):
        own = 15 + g
        for n in range(NB):
            past = (n < own) and valid[n] > 0
            pm[g, n] = 0.0 if past else -1e30
            if n <= own - 1:
                a1[g, n] = NEGB
            if n <= own - 2:
                a2[g, n] = 8.0
    a3 = -a1
    return pm.reshape(1, -1), a1.reshape(1, -1), a2.reshape(1, -1), a3.reshape(1, -1)


def kernel(x, rel_bias, norm_mix, norm_ffn, attn_w_qkv, attn_w_o, conv_w_pw1, conv_b_pw1, conv_w_dw, conv_b_dw,
           conv_ln_g, conv_ln_b, conv_w_pw2, conv_b_pw2, peer_w_q, peer_sub_keys, peer_u, peer_v, norm_final, _dbg=False):
    f = lambda a: np.ascontiguousarray(np.asarray(a, dtype=np.float32))
    x = f(x)
    boh = np.zeros((NB, SEQ), np.float32)
    for n in range(NB):
        boh[n, n * BLK:(n + 1) * BLK] = 1.0
    shared = {
        "rel_bias": f(rel_bias), "norm_mix": f(norm_mix), "norm_ffn": f(norm_ffn), "norm_final": f(norm_final).reshape(1, D),
        "w_qkv": f(attn_w_qkv[0]), "w_o": f(attn_w_o[0]), "w_pw1": f(conv_w_pw1[0]),
        "b_pw1": f(np.asarray(conv_b_pw1[0]).reshape(16, 128).T), "w_dwT": f(np.asarray(conv_w_dw[0]).T),
        "b_dw": f(np.asarray(conv_b_dw[0]).reshape(8, 128).T), "ln_g": f(np.asarray(conv_ln_g[0]).reshape(8, 128).T), "ln_b": f(np.asarray(conv_ln_b[0]).reshape(8, 128).T),
        "w_pw2": f(conv_w_pw2[0]), "b_pw2": f(conv_b_pw2[0]).reshape(1, D), "w_pq": f(peer_w_q),
        "skT": f(np.asarray(peer_sub_keys).reshape(2, 16, 128, 128).transpose(0, 1, 3, 2)),
        "uT": f(np.asarray(peer_u).transpose(0, 2, 1)), "pv": f(peer_v),
        "ident": np.eye(128, dtype=np.float32), "boh": boh,
    }
    in_maps = []
    for c in range(8):
        b, half = c // 2, c % 2
        xvv = np.zeros((SEQ, D), np.float32)
        if half == 1:
            xvv[:] = x[b]
        else:
            xvv[4096:] = x[b, :4096]
        pm, a1, a2, a3 = host_tables(half)
        m = dict(shared)
        m.update({"xv": xvv, "pm": pm, "a1": a1, "a2": a2, "a3": a3, "halo": np.full((1, 1), float(half), np.float32)})
        in_maps.append(m)
    key = bool(_dbg)
    if key not in _CACHE:
        _CACHE[key] = build_program(dbg=key)
    nc = _CACHE[key]
    res = run_bass_kernel_spmd(nc, in_maps, core_ids=list(range(8)))
    out = np.zeros((4, SEQ, D), np.float32)
    for c in range(8):
        b, half = c // 2, c % 2
        out[b, half * 4096:(half + 1) * 4096] = res.results[c]["y"]
    if _dbg:
        return out, res.results
    return out
```
